# Optimizing a Trainium2 kernel written in Bass

```python
import math
import jax, jax.numpy as jnp
from jax import lax
import numpy as np

D_MODEL = 2048
BATCH = 4
SEQ = 2048
DEPTH = 1

CHUNK = 64
Q_BLOCK = 128
MIX_WIDTH = D_MODEL
POOL_WIDTH = MIX_WIDTH // 2
POOL_WINDOWS = (2, 4, 8, 16)
N_POOL_GROUPS = len(POOL_WINDOWS)
POOL_GROUP = POOL_WIDTH // N_POOL_GROUPS
ATTN_WIDTH = MIX_WIDTH - POOL_WIDTH
DIFF_HEAD_DIM = 64
N_DIFF_HEADS = ATTN_WIDTH // (2 * DIFF_HEAD_DIM)
QK_WIDTH = N_DIFF_HEADS * 2 * DIFF_HEAD_DIM
IN_WIDTH = POOL_WIDTH + 2 * QK_WIDTH + ATTN_WIDTH
ROPE_THETA = 10000.0
FFN_HIDDEN = int(math.ceil(8 * D_MODEL / 3 / 256) * 256)
NORM_EPS = 1e-6

kernel_name = "hybrid_pool_diffattn_block"


def rms_norm(x, g):
    xf = x.astype(jnp.float32)
    y = xf * lax.rsqrt(jnp.mean(xf * xf, axis=-1, keepdims=True) + NORM_EPS)
    return (y * g.astype(jnp.float32)).astype(x.dtype)


def rotary_tables(seq, dim):
    inv_freq = ROPE_THETA ** (-jnp.arange(0, dim, 2, dtype=jnp.float32) / dim)
    ang = jnp.arange(seq, dtype=jnp.float32)[:, None] * inv_freq[None, :]
    return jnp.cos(ang), jnp.sin(ang)


def apply_rotary(t, cos, sin):
    half = t.shape[-1] // 2
    t1, t2 = t[..., :half], t[..., half:]
    c = cos[None, :, None, None, :].astype(t.dtype)
    s = sin[None, :, None, None, :].astype(t.dtype)
    return jnp.concatenate([t1 * c - t2 * s, t1 * s + t2 * c], axis=-1)


def multiscale_pool(u, w_pool, pool_scale):
    B, S, _ = u.shape
    uf = u.astype(jnp.float32)
    cs = jnp.concatenate([jnp.zeros((B, 1, POOL_WIDTH), jnp.float32),
                          jnp.cumsum(uf, axis=1)], axis=1)
    pos = jnp.arange(S, dtype=jnp.int32)
    outs = []
    for gi, w in enumerate(POOL_WINDOWS):
        sl = slice(gi * POOL_GROUP, (gi + 1) * POOL_GROUP)
        csg = cs[:, :, sl]
        hi = csg[:, 1:]
        lo = jnp.concatenate([jnp.zeros((B, w - 1, POOL_GROUP), jnp.float32),
                              csg[:, :S - w + 1]], axis=1)
        cnt = jnp.minimum(pos + 1, w).astype(jnp.float32)[None, :, None]
        outs.append((hi - lo) / cnt - uf[:, :, sl])
    pooled = jnp.stack(outs, axis=2).astype(u.dtype)
    mapped = jnp.einsum('bsgc,gcd->bsgd', pooled, w_pool)
    return mapped.reshape(B, S, POOL_WIDTH) * pool_scale


def diff_attention(q, k, v, lam, lam_init, subln_gain):
    B, S = q.shape[0], q.shape[1]
    scale = 1.0 / math.sqrt(DIFF_HEAD_DIM)
    outs = []
    for i in range(S // Q_BLOCK):
        q0, kend = i * Q_BLOCK, (i + 1) * Q_BLOCK
        qb = q[:, q0:kend]
        kb = k[:, :kend]
        vb = v[:, :kend]
        s = jnp.einsum('bqhcd,bkhcd->bhcqk', qb, kb).astype(jnp.float32) * scale
        q_chunk = (q0 + jnp.arange(Q_BLOCK)) // CHUNK
        k_chunk = jnp.arange(kend) // CHUNK
        mask = q_chunk[:, None] >= k_chunk[None, :]
        s = jnp.where(mask[None, None, None], s, -jnp.inf)
        p = jax.nn.softmax(s, axis=-1)
        a = p[:, :, 0] - lam * p[:, :, 1]
        o = jnp.einsum('bhqk,bkhe->bqhe', a.astype(vb.dtype), vb)
        outs.append(o)
    o = jnp.concatenate(outs, axis=1)
    o = rms_norm(o, subln_gain) * (1.0 - lam_init)
    return o.reshape(B, S, ATTN_WIDTH)


def setup_inputs(seed: int = 0) -> dict:
    key = jax.random.key(seed)
    ks = jax.random.split(key, 16)
    f32 = jnp.float32
    nrm = lambda k, shape, s: (jax.random.normal(k, shape, f32) * s)
    L = DEPTH
    return {
        "x": jax.random.normal(ks[0], (BATCH, SEQ, D_MODEL), f32),
        "norm_mix": 1.0 + nrm(ks[1], (L, D_MODEL), 0.02),
        "w_in": nrm(ks[2], (L, D_MODEL, IN_WIDTH), D_MODEL ** -0.5),
        "w_pool": nrm(ks[3], (L, N_POOL_GROUPS, POOL_GROUP, POOL_GROUP), POOL_GROUP ** -0.5),
        "pool_scale": 1.0 + nrm(ks[4], (L, POOL_WIDTH), 0.1),
        "lambda_q1": nrm(ks[5], (L, DIFF_HEAD_DIM), 0.1),
        "lambda_k1": nrm(ks[6], (L, DIFF_HEAD_DIM), 0.1),
        "lambda_q2": nrm(ks[7], (L, DIFF_HEAD_DIM), 0.1),
        "lambda_k2": nrm(ks[8], (L, DIFF_HEAD_DIM), 0.1),
        "subln_gain": 1.0 + nrm(ks[9], (L, 2 * DIFF_HEAD_DIM), 0.02),
        "w_out": nrm(ks[10], (L, MIX_WIDTH, D_MODEL), MIX_WIDTH ** -0.5),
        "norm_ffn": 1.0 + nrm(ks[11], (L, D_MODEL), 0.02),
        "w_gate_up": nrm(ks[12], (L, D_MODEL, 2 * FFN_HIDDEN), D_MODEL ** -0.5),
        "w_down": nrm(ks[13], (L, FFN_HIDDEN, D_MODEL), FFN_HIDDEN ** -0.5),
        "norm_final": 1.0 + nrm(ks[14], (D_MODEL,), 0.02),
    }


def reference(x, norm_mix, w_in, w_pool, pool_scale, lambda_q1, lambda_k1, lambda_q2,
              lambda_k2, subln_gain, w_out, norm_ffn, w_gate_up, w_down, norm_final):
    B, S, _ = x.shape
    cos, sin = rotary_tables(S, DIFF_HEAD_DIM)
    for l in range(DEPTH):
        h = rms_norm(x, norm_mix[l])
        proj = h @ w_in[l]
        u = proj[..., :POOL_WIDTH]
        q = proj[..., POOL_WIDTH:POOL_WIDTH + QK_WIDTH]
        k = proj[..., POOL_WIDTH + QK_WIDTH:POOL_WIDTH + 2 * QK_WIDTH]
        v = proj[..., POOL_WIDTH + 2 * QK_WIDTH:]
        q = apply_rotary(q.reshape(B, S, N_DIFF_HEADS, 2, DIFF_HEAD_DIM), cos, sin)
        k = apply_rotary(k.reshape(B, S, N_DIFF_HEADS, 2, DIFF_HEAD_DIM), cos, sin)
        v = v.reshape(B, S, N_DIFF_HEADS, 2 * DIFF_HEAD_DIM)
        lam_init = 0.8 - 0.6 * math.exp(-0.3 * l)
        lam = (jnp.exp(jnp.sum(lambda_q1[l].astype(jnp.float32) * lambda_k1[l].astype(jnp.float32)))
               - jnp.exp(jnp.sum(lambda_q2[l].astype(jnp.float32) * lambda_k2[l].astype(jnp.float32)))
               + lam_init)
        pool_out = multiscale_pool(u, w_pool[l], pool_scale[l])
        attn_out = diff_attention(q, k, v, lam, lam_init, subln_gain[l])
        mixed = jnp.concatenate([pool_out, attn_out.astype(pool_out.dtype)], axis=-1)
        x = x + mixed @ w_out[l]
        h = rms_norm(x, norm_ffn[l])
        gu = h @ w_gate_up[l]
        gate, up = gu[..., :FFN_HIDDEN], gu[..., FFN_HIDDEN:]
        x = x + (jax.nn.silu(gate) * up) @ w_down[l]
    return rms_norm(x, norm_final)
```

```python
import contextlib
import math

import numpy as np
import concourse.bass as bass
import concourse.mybir as mybir
from concourse.bass_utils import run_bass_kernel_spmd

F32 = mybir.dt.float32
BF16 = mybir.dt.bfloat16
AF = mybir.ActivationFunctionType
ALU = mybir.AluOpType
AX = mybir.AxisListType

D = 2048
NDC = 16
S_OWN = 1024
FFN = 5632
NJ = 44
NG = 4
JG = 11
EPS = 1e-6
LAM_INIT = 0.8 - 0.6 * math.exp(0.0)
NEG_BIG = -30000.0

C_G1, C_G2, C_G3 = 0, 16, 32
C_PS = 48
C_PB = 56
C_IC = 57
C_LQ1, C_LK1, C_LQ2, C_LK2 = 121, 185, 249, 313
C_GREP = 377
C_ONES = 505
C_ID = 633
C_RM = 761
C_N = 896

NS = 4


class Op:
    __slots__ = ("id", "eng", "idx", "fn", "deps", "dma", "signal", "cum", "ordinal", "ndma")


class Prog:
    ENGS = ("pe", "act", "dve", "pool", "sp")

    def __init__(self):
        self.ops = []
        self.eng_ops = {e: [] for e in self.ENGS}
        self.last_w = {}
        self.readers = {}
        self.pending_fence = {}
        self.last_dma = {}
        self.aggregate = set()

    def add(self, eng, fn, reads=(), writes=(), dma=None, ndma=1):
        op = Op()
        op.ndma = ndma
        op.id = len(self.ops)
        op.eng = eng
        op.fn = fn
        op.dma = dma
        op.signal = False
        writes = list(writes) + [r for r in reads if isinstance(r, tuple) and r[0] == "ps" and r not in writes]
        deps = set()
        for r in reads:
            lw = self.last_w.get(r)
            if lw is not None:
                deps.add(lw)
        for w in writes:
            lw = self.last_w.get(w)
            if lw is not None:
                deps.add(lw)
            rs = self.readers.get(w)
            if rs:
                deps.update(rs)
        for r in reads:
            self.readers.setdefault(r, []).append(op.id)
        for w in writes:
            self.last_w[w] = op.id
            self.readers[w] = []
        pf = self.pending_fence.pop(eng, None)
        if pf:
            deps.update(pf)
        deps.discard(op.id)
        if eng == "pe" and dma is None:
            deps = {d for d in deps if not (self.ops[d].eng == "pe" and self.ops[d].dma is None)}
        op.deps = deps
        op.idx = len(self.eng_ops[eng])
        self.eng_ops[eng].append(op)
        self.ops.append(op)
        if dma is not None:
            self.last_dma[dma] = op.id
        return op.id

    def fence(self):
        deps = set()
        for e in self.ENGS:
            lst = self.eng_ops[e]
            for o in reversed(lst):
                if o.dma is None and o.fn is not None:
                    deps.add(o.id)
                    break
        deps.update(self.last_dma.values())
        for e in self.ENGS:
            self.pending_fence.setdefault(e, set()).update(deps)

    def finalize(self, nc, stack):
        for op in self.ops:
            for d in op.deps:
                t = self.ops[d]
                if t.dma is None:
                    t.signal = True
        self.eng_sem = {}
        for e in self.ENGS:
            self.eng_sem[e] = stack.enter_context(nc.semaphore("s_" + e))
            c = 0
            for o in self.eng_ops[e]:
                if o.dma is None and o.signal:
                    c += 1
                o.cum = c
        self.dma_sem = {}
        counts = {}
        for op in self.ops:
            if op.dma is not None:
                k = op.dma
                if k not in self.dma_sem:
                    nm = "d_" + "_".join(str(x) for x in (k if isinstance(k, tuple) else (k,)))
                    self.dma_sem[k] = stack.enter_context(nc.semaphore(nm))
                counts[k] = counts.get(k, 0) + op.ndma
                op.ordinal = counts[k]
        self.dma_total = counts

    def wait_of(self, d):
        t = self.ops[d]
        if t.dma is None:
            return (self.eng_sem[t.eng], t.cum)
        k = t.dma
        n = self.dma_total[k] if k in self.aggregate else t.ordinal
        return (self.dma_sem[k], 16 * n)

    def emit(self, eng, e):
        known = {}
        for op in self.eng_ops[eng]:
            ws = {}
            for d in op.deps:
                sem, val = self.wait_of(d)
                key = id(sem)
                if ws.get(key, (None, 0))[1] < val:
                    ws[key] = (sem, val)
            for key, (sem, val) in ws.items():
                if known.get(key, 0) < val:
                    e.wait_ge(sem, val)
                    known[key] = val
            if op.fn is None:
                continue
            ins = op.fn(e)
            if op.dma is not None:
                for one in (ins if isinstance(ins, list) else [ins]):
                    one.then_inc(self.dma_sem[op.dma], 16)
            elif op.signal:
                ins.then_inc(self.eng_sem[eng], 1)


class _Stop(Exception):
    pass


def build_program(stop=None):
    nc = bass.Bass("TRN2", target_bir_lowering=False)
    dt = nc.dram_tensor
    xT_own = dt("xT_own", [128, NDC, S_OWN], F32, kind="ExternalInput")
    xT_prev = dt("xT_prev", [128, NDC, S_OWN], F32, kind="ExternalInput")
    rot_d = dt("rot", [128, 2, 2, 1024], F32, kind="ExternalInput")
    cst_d = dt("cst", [128, C_N], F32, kind="ExternalInput")
    w_uqk = dt("w_uqk", [24, 128, 16, 128], F32, kind="ExternalInput")
    w_v = dt("w_v", [4, 128, 16, 256], F32, kind="ExternalInput")
    w_pool = dt("w_pool", [128, 4, 2, 256], F32, kind="ExternalInput")
    w_o = dt("w_o", [16, 128, 16, 128], F32, kind="ExternalInput")
    early = stop in ("setup", "pk_load", "pk_mm", "pk_rot", "n1prev", "projprev", "projown", "att", "wout", "n2")
    w_gu = None if early else dt("w_gu", [NJ, 128, 2, 16, 128], F32, kind="ExternalInput")
    w_dn = None if early else dt("w_dn", [NG, 8, 128, JG, 256], F32, kind="ExternalInput")
    outT = dt("outT", [128, NDC, S_OWN], F32, kind="ExternalOutput")

    BASE = 16640
    cnt = [0]

    def sb(name, shape, dtype, off):
        cnt[0] += 1
        return nc.alloc_sbuf_tensor_at(f"{name}_{cnt[0]}", list(shape), dtype, offset=BASE + off)

    cst = sb("cst", [128, C_N], F32, 0)
    ones_b = sb("ones", [128, 128], BF16, 3584)
    ident_b = sb("ident", [128, 128], BF16, 3840)
    Rm_b = sb("rm", [128, 128], BF16, 4096)
    small = sb("small", [128, 256], F32, 4352)
    wp = sb("wp", [128, 4, 2, 256], BF16, 5376)
    W0 = 9472
    wr_a = [sb("wra", [128, 16, 128], BF16, W0 + s * 8192) for s in range(NS)]
    wr_v = [sb("wrv", [128, 16, 256], BF16, W0 + s * 8192) for s in range(NS)]
    wr_gu = [sb("wrgu", [128, 2, 16, 128], BF16, W0 + s * 8192) for s in range(NS)]
    wr_dn = [sb("wrdn", [128, JG, 256], BF16, W0 + s * 8192) for s in range(NS)]
    RH = 42240
    hT = sb("hT", [128, 16, 1024], BF16, RH)
    mixA = sb("mixA", [128, 8, 1024], BF16, RH)
    ATS = RH + 16384
    PT = [sb("PT", [128, 512], BF16, ATS + r * 1024) for r in range(4)]
    PTd = [[sb("PTd", [128, 128], BF16, ATS + 4096 + (r * 2 + c) * 256) for c in range(2)] for r in range(2)]
    o_a = [sb("oa", [128, 128], F32, ATS + 5120 + r * 512) for r in range(2)]
    o_b = [sb("ob", [128, 128], F32, ATS + 6144 + r * 512) for r in range(2)]
    o_c = [sb("oc", [128, 128], BF16, ATS + 7168 + r * 256) for r in range(2)]
    esm = [sb("esm", [128, 16], F32, ATS + 7680 + r * 64) for r in range(2)]
    h2T = sb("h2T", [128, 16, 1024], BF16, RH)
    RK = 75008
    KT = sb("KT", [128, 8, 2048], BF16, RK)
    QT = sb("QT", [128, 8, 1024], BF16, RK + 32768)
    Vaug = sb("Vaug", [128, 16, 8, 130], BF16, RK + 49152)
    X1 = sb("X1", [128, 16, 1024], F32, RK)
    CS = RK + 65536
    sqbC = [sb("sqbC", [128, 1024], BF16, CS + r * 2048) for r in range(2)]
    rstdC = sb("rstdC", [128, 1024], F32, CS + 4096)
    sg = [sb("sg", [128, 512], F32, CS + 8192 + r * 2048) for r in range(2)]
    MP = 157440
    mixP = sb("mixP", [128, 8, 1024], BF16, MP)
    hTg = [sb("hTg", [128, JG, 1024], BF16, MP + r * 22528) for r in range(2)]
    RT = 173824
    rot = sb("rot", [128, 2, 1024], F32, RT)
    hThalo = sb("hThalo", [128, 16, 16], BF16, 182016)
    SCR = 182528
    xs = [sb("xs", [128, 1024], F32, SCR + r * 4096) for r in range(2)]
    sqb = [sb("sqb", [128, 1024], BF16, SCR + 8192 + r * 2048) for r in range(2)]
    rstd = sb("rstd", [128, 1024], F32, SCR + 12288)
    ubuf = sb("ubuf", [128, 1040], F32, SCR)
    pa = [sb("pa", [128, 1040], F32, SCR + 4160 * (r + 1)) for r in range(2)]
    pooled = [sb("pooled", [128, 1024], BF16, SCR + 12480 + r * 2048) for r in range(2)]
    tmp16 = sb("tmp16", [128, 16], F32, SCR + 16576)
    RS = 199168
    qb = [sb("qb", [128, 512], BF16, RS + r * 1024) for r in range(2)]
    t1 = [sb("t1", [128, 512], F32, RS + 2048 + r * 2048) for r in range(2)]
    t2 = [sb("t2", [128, 512], F32, RS + 6144 + r * 2048) for r in range(2)]
    assert BASE + RS + 10240 <= 229344

    ps = [nc.alloc_psum_tensor(f"ps{b}", [128, 512], F32) for b in range(8)]
    psb = [p.bitcast(BF16) for p in ps]

    P = Prog()
    P.aggregate.update(["const", "out"])
    bank = [0]

    def nextbank(lo=0, hi=8):
        b = lo + (bank[0] % (hi - lo))
        bank[0] += 1
        return b

    wslot = [0]

    def load_w(kind, src):
        s = wslot[0] % NS
        wslot[0] += 1
        if kind == "a":
            P.add("pool", lambda e, s=s, src=src: e.dma_start(out=wr_a[s][:, :, :], in_=src, max_dma_last_dim=4096),
                  writes=[("w", s)], dma=("w", s))
        elif kind == "v":
            P.add("pool", lambda e, s=s, src=src: e.dma_start(out=wr_v[s][:, :, :], in_=src, max_dma_last_dim=4096),
                  writes=[("w", s)], dma=("w", s))
        elif kind == "gu":
            P.add("pool", lambda e, s=s, src=src: [
                e.dma_start(out=wr_gu[s][:, 0, :, :], in_=src[:, 0, :, :], max_dma_last_dim=4096),
                e.dma_start(out=wr_gu[s][:, 1, :, :], in_=src[:, 1, :, :], max_dma_last_dim=4096)],
                writes=[("w", s)], dma=("w", s), ndma=2)
        elif kind == "dn":
            P.add("pool", lambda e, s=s, src=src: e.dma_start(out=wr_dn[s][:, :, :], in_=src, max_dma_last_dim=4096),
                  writes=[("w", s)], dma=("w", s))
        return s

    def cs(c0, n=1):
        return cst[:, c0:c0 + n]

    def ck(name):
        if stop == name:
            raise _Stop()

    def dump(name):
        srcs = {
            "setup": [(small[:, :], outT[:, 0, 0:256])],
            "pk_load": [(wr_a[0][:, dc, :], outT[:, dc, 0:128]) for dc in range(16)],
            "pk_mm": [(hT[:, 0, :], outT[:, 0, :])],
            "pk_rot": [(KT[:, 0, 0:1024], outT[:, 0, :])],
            "n1prev": [(rstd[:, :], outT[:, 0, :])] + [(hT[:, dc, :], outT[:, 1 + dc % 15, :]) for dc in (0, 5)],
            "projprev": [(KT[:, h, 0:1024], outT[:, h, :]) for h in range(8)] +
                        [(Vaug[:, tb, :, 0:128], outT[:, 8 + tb, :]) for tb in range(8)],
            "projown": [(KT[:, 0, 1024:2048], outT[:, 0, :]), (KT[:, 7, 0:1024], outT[:, 3, :])] +
                       [(QT[:, h, :], outT[:, 1 + h, :]) for h in (0, 1)] +
                       [(mixP[:, c, :], outT[:, 4 + c, :]) for c in range(8)] +
                       [(Vaug[:, tb, :, 0:128], outT[:, 12 + i_, :]) for i_, tb in enumerate((8, 9, 3, 15))],
            "att": [(mixA[:, h, :], outT[:, h, :]) for h in range(8)] + [(mixP[:, c, :], outT[:, 8 + c, :]) for c in range(8)],
            "wout": [(X1[:, c, :], outT[:, c, :]) for c in range(16)],
            "n2": [(X1[:, c, :], outT[:, c, :]) for c in range(8)] + [(h2T[:, c, :], outT[:, 8 + c, :]) for c in range(8)],
            "ffn": [(X1[:, c, :], outT[:, c, :]) for c in range(16)],
        }[name]
        P.fence()
        for i, (src, dst) in enumerate(srcs):
            P.add("pool", lambda e, src=src, dst=dst: e.dma_start(out=dst, in_=src, max_dma_last_dim=2048), writes=[("out", i)], dma="out")
        P.add("sp", None, reads=[("out", i) for i in range(len(srcs))])

    def body():
        P.add("sp", lambda e: e.dma_start(out=cst[:, :], in_=cst_d[:, :]), writes=["cst"], dma="const")
        P.add("pool", lambda e: e.dma_start(out=wp[:, :, :, :], in_=w_pool[:, :, :, :], max_dma_last_dim=4096),
              writes=["wp"], dma="wp")
        P.add("dve", lambda e: e.tensor_copy(out=ones_b[:, :], in_=cs(C_ONES, 128)), reads=["cst"], writes=["ones"])
        P.add("dve", lambda e: e.tensor_copy(out=ident_b[:, :], in_=cs(C_ID, 128)), reads=["cst"], writes=["ident"])
        P.add("dve", lambda e: e.tensor_copy(out=Rm_b[:, :], in_=cs(C_RM, 128)), reads=["cst"], writes=["rm"])
        P.add("dve", lambda e: e.tensor_tensor(out=small[:, 0:64], in0=cs(C_LQ1, 64), in1=cs(C_LK1, 64), op=ALU.mult),
              reads=["cst"], writes=["sm0"])
        P.add("dve", lambda e: e.reduce_sum(out=small[:, 128:129], in_=small[:, 0:64], axis=AX.X), reads=["sm0"], writes=["sm1"])
        P.add("dve", lambda e: e.tensor_tensor(out=small[:, 64:128], in0=cs(C_LQ2, 64), in1=cs(C_LK2, 64), op=ALU.mult),
              reads=["cst"], writes=["sm2"])
        P.add("dve", lambda e: e.reduce_sum(out=small[:, 129:130], in_=small[:, 64:128], axis=AX.X), reads=["sm2"], writes=["sm3"])
        P.add("act", lambda e: e.activation(out=small[:, 130:132], in_=small[:, 128:130], func=AF.Exp),
              reads=["sm1", "sm3"], writes=["sm4"])
        P.add("dve", lambda e: e.tensor_tensor(out=small[:, 132:133], in0=small[:, 131:132], in1=small[:, 130:131], op=ALU.subtract),
              reads=["sm4"], writes=["sm5"])
        P.add("dve", lambda e: e.tensor_scalar_add(out=small[:, 133:134], in0=small[:, 132:133], scalar1=-LAM_INIT),
              reads=["sm5"], writes=["neglam"])
        neglam = small[:, 133:134]
        P.add("dve", lambda e: e.tensor_scalar_mul(out=cst[:, C_GREP:C_GREP + 128], in0=cst[:, C_GREP:C_GREP + 128],
                                                   scalar1=(1.0 - LAM_INIT)), reads=["cst"], writes=["cst"])

        ck('setup')
        def norm_stats(get_src, sq_bufs, rstd_buf, tag):
            b0, b1 = nextbank(), nextbank()
            for dc in range(NDC):
                ap, res = get_src(dc)
                r = dc % 2
                P.add("act", lambda e, ap=ap, r=r: e.activation(out=sq_bufs[r][:, :], in_=ap, func=AF.Square),
                      reads=res, writes=[(tag + "sq", r)])
                for th, b in ((0, b0), (1, b1)):
                    P.add("pe", lambda e, r=r, th=th, b=b, dc=dc: e.matmul(
                        ps[b][:, :], lhsT=ones_b[:, :], rhs=sq_bufs[r][:, th * 512:(th + 1) * 512],
                        start=(dc == 0), stop=(dc == NDC - 1)),
                        reads=[(tag + "sq", r), "ones"], writes=[("ps", b)])
            for th, b in ((0, b0), (1, b1)):
                P.add("dve", lambda e, th=th, b=b: e.tensor_scalar(
                    out=rstd_buf[:, th * 512:(th + 1) * 512], in0=ps[b][:, :], scalar1=1.0 / D, scalar2=EPS,
                    op0=ALU.mult, op1=ALU.add), reads=[("ps", b)], writes=[(tag + "rstd", th)])
            P.add("act", lambda e: e.activation(out=rstd_buf[:, :], in_=rstd_buf[:, :], func=AF.Sqrt),
                  reads=[(tag + "rstd", 0), (tag + "rstd", 1)], writes=[tag + "rstd2"])
            P.add("dve", lambda e: e.reciprocal(out=rstd_buf[:, :], in_=rstd_buf[:, :]),
                  reads=[tag + "rstd2"], writes=[tag + "rstdF"])

        rot_pending = []

        def flush_rot():
            while rot_pending:
                rot_pending.pop(0)()

        rotc = [0]

        def rotary_tile(b, dest_ap, dest_res, th):
            r = rotc[0] % 2
            rotc[0] += 1
            P.add("act", lambda e, b=b, r=r: e.activation(out=qb[r][:, :], in_=ps[b][:, :], func=AF.Copy),
                  reads=[("ps", b)], writes=[("qb", r)])

            def rest(b=b, r=r, dest_ap=dest_ap, dest_res=dest_res, th=th):
                b2 = nextbank()
                P.add("pe", lambda e: e.matmul(ps[b2][:, :], lhsT=Rm_b[:, :], rhs=qb[r][:, :], start=True, stop=True),
                      reads=[("qb", r), "rm"], writes=[("ps", b2)])
                P.add("dve", lambda e: e.tensor_tensor(out=t1[r][:, :], in0=ps[b][:, :], in1=rot[:, 0, th * 512:(th + 1) * 512], op=ALU.mult),
                      reads=[("ps", b), "rot"], writes=[("t1", r)])
                P.add("dve", lambda e: e.tensor_tensor(out=t2[r][:, :], in0=ps[b2][:, :], in1=rot[:, 1, th * 512:(th + 1) * 512], op=ALU.mult),
                      reads=[("ps", b2), "rot"], writes=[("t2", r)])
                P.add("dve", lambda e: e.tensor_tensor(out=dest_ap, in0=t1[r][:, :], in1=t2[r][:, :], op=ALU.add),
                      reads=[("t1", r), ("t2", r)], writes=[dest_res])
            rot_pending.append(rest)

        def proj_fm_matmuls(s, th, b):
            for dc in range(NDC):
                P.add("pe", lambda e, dc=dc: e.matmul(ps[b][:, :], lhsT=wr_a[s][:, dc, :], rhs=hT[:, dc, th * 512:(th + 1) * 512],
                                                       start=(dc == 0), stop=(dc == NDC - 1)),
                      reads=[("w", s), "hT"], writes=[("ps", b)])

        for ipass, (xsrc, tokoff, tboff) in enumerate(((xT_prev, 0, 0), (xT_own, 1024, 8))):
            own = ipass == 1
            P.add("sp", lambda e, ipass=ipass: e.dma_start(out=rot[:, :, :], in_=rot_d[:, ipass, :, :]), writes=["rot"], dma="rot")

            def get_src(dc, xsrc=xsrc):
                r = dc % 2
                P.add("sp", lambda e, dc=dc, r=r: e.dma_start(out=xs[r][:, :], in_=xsrc[:, dc, :]), writes=[("xs", r)], dma=("xs", r))
                return xs[r][:, :], [("xs", r)]
            norm_stats(get_src, sqb, rstd, "n1")
            for dc in range(NDC):
                r = dc % 2
                P.add("sp", lambda e, dc=dc, r=r, xsrc=xsrc: e.dma_start(out=xs[r][:, :], in_=xsrc[:, dc, :]), writes=[("xs", r)], dma=("xs", r))
                P.add("dve", lambda e, dc=dc, r=r: e.scalar_tensor_tensor(
                    out=hT[:, dc, :], in0=xs[r][:, :], scalar=cs(C_G1 + dc), in1=rstd[:, :], op0=ALU.mult, op1=ALU.mult),
                    reads=[("xs", r), "n1rstdF", "cst"], writes=["hT"])
            if not own:
                P.add("dve", lambda e: e.tensor_copy(out=hThalo[:, :, :], in_=hT[:, :, 1008:1024]), reads=["hT"], writes=["hThalo"])
                ck('n1prev')
            else:
                P.fence()

            if own:
                for n in range(8):
                    g = n // 2
                    a_ = n % 2
                    s = load_w("a", w_uqk[n, :, :, :])
                    bh, b0, b1 = nextbank(), nextbank(), nextbank()
                    for dc in range(NDC):
                        P.add("pe", lambda e, dc=dc, s=s, bh=bh: e.matmul(ps[bh][:, 0:16], lhsT=wr_a[s][:, dc, :], rhs=hThalo[:, dc, :],
                                                                          start=(dc == 0), stop=(dc == NDC - 1)),
                              reads=[("w", s), "hThalo"], writes=[("ps", bh)])
                    proj_fm_matmuls(s, 0, b0)
                    proj_fm_matmuls(s, 1, b1)
                    flush_rot()
                    P.add("act", lambda e, bh=bh: e.activation(out=ubuf[:, 0:16], in_=ps[bh][:, 0:16], func=AF.Copy),
                          reads=[("ps", bh)], writes=["ubuf"])
                    P.add("act", lambda e, b0=b0: e.activation(out=ubuf[:, 16:528], in_=ps[b0][:, :], func=AF.Copy),
                          reads=[("ps", b0), "ubuf"], writes=["ubuf"])
                    P.add("act", lambda e, b1=b1: e.activation(out=ubuf[:, 528:1040], in_=ps[b1][:, :], func=AF.Copy),
                          reads=[("ps", b1), "ubuf"], writes=["ubuf"])
                    m = g + 1
                    src = ubuf
                    src_res = "ubuf"
                    for st in range(1, m + 1):
                        sh = 2 ** (st - 1)
                        S0 = 2 ** st - 1
                        dst = pa[(st - 1) % 2]
                        dres = ("pa", (st - 1) % 2)
                        P.add("dve", lambda e, src=src, dst=dst, sh=sh, S0=S0: e.tensor_tensor(
                            out=dst[:, S0:1040], in0=src[:, S0:1040], in1=src[:, S0 - sh:1040 - sh], op=ALU.add),
                            reads=[src_res], writes=[dres])
                        src, src_res = dst, dres
                    w_ = 2 ** m
                    P.add("dve", lambda e, src=src, a_=a_, w_=w_: e.scalar_tensor_tensor(
                        out=pooled[a_][:, :], in0=src[:, 16:1040], scalar=1.0 / w_, in1=ubuf[:, 16:1040],
                        op0=ALU.mult, op1=ALU.subtract), reads=[src_res, "ubuf"], writes=[("pooled", a_)])
                    P.add("dve", lambda e, src=src, g=g: e.tensor_tensor(
                        out=tmp16[:, :], in0=src[:, 16:32], in1=cs(C_IC + 16 * g, 16), op=ALU.mult),
                        reads=[src_res, "cst"], writes=["tmp16"])
                    P.add("dve", lambda e, a_=a_: e.tensor_tensor(
                        out=pooled[a_][:, 0:16], in0=tmp16[:, :], in1=ubuf[:, 16:32], op=ALU.subtract),
                        reads=["tmp16", "ubuf", ("pooled", a_)], writes=[("pooled", a_)])
                    if a_ == 1:
                        for o in range(2):
                            for th in range(2):
                                b = nextbank()
                                for a2 in range(2):
                                    P.add("pe", lambda e, g=g, a2=a2, o=o, th=th, b=b: e.matmul(
                                        ps[b][:, :], lhsT=wp[:, g, a2, o * 128:(o + 1) * 128],
                                        rhs=pooled[a2][:, th * 512:(th + 1) * 512], start=(a2 == 0), stop=(a2 == 1)),
                                        reads=["wp", ("pooled", a2)], writes=[("ps", b)])
                                P.add("act", lambda e, g=g, o=o, th=th, b=b: e.activation(
                                    out=mixP[:, 2 * g + o, th * 512:(th + 1) * 512], in_=ps[b][:, :], func=AF.Identity,
                                    scale=cs(C_PS + 2 * g + o)), reads=[("ps", b), "cst"], writes=[("mixP", 2 * g + o, th)])

            kinds = (["q"] if own else []) + ["k"]
            for kind in kinds:
                for hc in range(8):
                    n = (8 if kind == "q" else 16) + hc
                    s = load_w("a", w_uqk[n, :, :, :])
                    ck('pk_load')
                    for th in range(2):
                        b = nextbank()
                        proj_fm_matmuls(s, th, b)
                        ck('pk_mm')
                        flush_rot()
                        if th == 1:
                            ck('pk_rot')
                        if kind == "q":
                            dest = QT[:, hc, th * 512:(th + 1) * 512]
                            dres = ("QT", hc, th)
                        else:
                            dest = KT[:, hc, tokoff + th * 512: tokoff + (th + 1) * 512]
                            dres = ("KT", hc, ipass, th)
                        rotary_tile(b, dest, dres, th)
            flush_rot()

            if not own:
                P.add("dve", lambda e: e.memset(Vaug[:, :, :, 128:130], 1.0), writes=["Vones"])
            for vc in range(4):
                s = load_w("v", w_v[vc, :, :, :])
                for tbl in range(8):
                    b = nextbank()
                    for dc in range(NDC):
                        P.add("pe", lambda e, dc=dc, s=s, tbl=tbl, b=b: e.matmul(
                            ps[b][:, 0:256], lhsT=hT[:, dc, tbl * 128:(tbl + 1) * 128], rhs=wr_v[s][:, dc, :],
                            start=(dc == 0), stop=(dc == NDC - 1)), reads=[("w", s), "hT"], writes=[("ps", b)])
                    for hh in range(2):
                        P.add("act", lambda e, b=b, hh=hh, vc=vc, tbl=tbl, tboff=tboff: e.activation(
                            out=Vaug[:, tboff + tbl, 2 * vc + hh, 0:128], in_=ps[b][:, hh * 128:(hh + 1) * 128], func=AF.Copy),
                            reads=[("ps", b)], writes=[("V", tboff + tbl, 2 * vc + hh)])
            if not own:
                ck('projprev')

        ck('projown')
        P.fence()

        for r in range(2):
            for c in range(2):
                P.add("dve", lambda e, r=r, c=c: e.memset(PTd[r][c][:, :], 0.0), writes=[("PTd", r)])
        unit = 0
        ptc = [0]
        for h in range(8):
            for i in range(8):
                u2 = unit % 2
                bO = (0, 1) if u2 == 0 else (2, 3)
                first = [True, True]
                kq_reads = [("QT", h, i // 4)]
                groups = [([0, 1, 2, 3], True), ([4, 5, 6, 7], True)]
                ownk = list(range(8, 8 + i))
                while ownk:
                    groups.append((ownk[:4], False))
                    ownk = ownk[4:]
                for kbs, isprev in groups:
                    bS = (nextbank(4, 8), nextbank(4, 8))
                    W = len(kbs) * 128
                    for c in range(2):
                        for idx, kb in enumerate(kbs):
                            P.add("pe", lambda e, c=c, idx=idx, kb=kb, h=h, i=i, bS=bS: e.matmul(
                                ps[bS[c]][:, idx * 128:(idx + 1) * 128],
                                lhsT=KT[c * 64:(c + 1) * 64, h, kb * 128:(kb + 1) * 128],
                                rhs=QT[c * 64:(c + 1) * 64, h, i * 128:(i + 1) * 128], start=True, stop=True),
                                reads=kq_reads + [("KT", h, kb // 8, (kb % 8) // 4)], writes=[("ps", bS[c])])
                    prs = []
                    for c in range(2):
                        pr = ptc[0] % 4
                        ptc[0] += 1
                        prs.append(pr)
                        if isprev:
                            P.add("act", lambda e, c=c, pr=pr, bS=bS, W=W: e.activation(
                                out=PT[pr][:, 0:W], in_=ps[bS[c]][:, 0:W], func=AF.Exp, bias=cs(C_PB), scale=0.125),
                                reads=[("ps", bS[c]), "cst"], writes=[("PT", pr)])
                        else:
                            P.add("act", lambda e, c=c, pr=pr, bS=bS, W=W: e.activation(
                                out=PT[pr][:, 0:W], in_=ps[bS[c]][:, 0:W], func=AF.Exp, scale=0.125),
                                reads=[("ps", bS[c])], writes=[("PT", pr)])
                    for c in range(2):
                        for idx, kb in enumerate(kbs):
                            P.add("pe", lambda e, c=c, idx=idx, kb=kb, h=h, pr=prs[c], st=first[c], bO=bO: e.matmul(
                                ps[bO[c]][:, 0:129], lhsT=PT[pr][:, idx * 128:(idx + 1) * 128], rhs=Vaug[:, kb, h, 0:129],
                                start=st, stop=False),
                                reads=[("PT", prs[c]), ("V", kb, h), "Vones"], writes=[("ps", bO[c])])
                            first[c] = False
                kb = 8 + i
                bS = (nextbank(4, 8), nextbank(4, 8))
                for c in range(2):
                    P.add("pe", lambda e, c=c, kb=kb, h=h, i=i, bS=bS: e.matmul(
                        ps[bS[c]][:, 0:128], lhsT=KT[c * 64:(c + 1) * 64, h, kb * 128:(kb + 1) * 128],
                        rhs=QT[c * 64:(c + 1) * 64, h, i * 128:(i + 1) * 128], start=True, stop=True),
                        reads=kq_reads + [("KT", h, 1, i // 4)], writes=[("ps", bS[c])])
                for c in range(2):
                    P.add("act", lambda e, c=c, u2=u2, bS=bS: e.activation(
                        out=PTd[u2][c][0:64, 0:128], in_=ps[bS[c]][0:64, 0:128], func=AF.Exp, scale=0.125),
                        reads=[("ps", bS[c])], writes=[("PTd", u2)])
                    P.add("act", lambda e, c=c, u2=u2, bS=bS: e.activation(
                        out=PTd[u2][c][64:128, 64:128], in_=ps[bS[c]][64:128, 64:128], func=AF.Exp, scale=0.125),
                        reads=[("ps", bS[c]), ("PTd", u2)], writes=[("PTd", u2)])
                for c in range(2):
                    P.add("pe", lambda e, c=c, kb=kb, h=h, u2=u2, bO=bO: e.matmul(
                        ps[bO[c]][:, 0:129], lhsT=PTd[u2][c][:, :], rhs=Vaug[:, kb, h, 0:129], start=False, stop=True),
                        reads=[("PTd", u2), ("V", kb, h), "Vones"], writes=[("ps", bO[c])])
                E = esm[u2]
                er = ("esm", u2)
                P.add("dve", lambda e, E=E, bO=bO: e.reciprocal(out=E[:, 0:1], in_=ps[bO[0]][:, 128:129]),
                      reads=[("ps", bO[0])], writes=[er])
                P.add("dve", lambda e, E=E, bO=bO: e.reciprocal(out=E[:, 1:2], in_=ps[bO[1]][:, 128:129]),
                      reads=[("ps", bO[1]), er], writes=[er])
                P.add("dve", lambda e, E=E: e.tensor_tensor(out=E[:, 2:3], in0=E[:, 1:2], in1=neglam, op=ALU.mult),
                      reads=[er, "neglam"], writes=[er])
                P.add("dve", lambda e, E=E, bO=bO, u2=u2: e.tensor_scalar(
                    out=o_a[u2][:, :], in0=ps[bO[0]][:, 0:128], scalar1=E[:, 0:1], scalar2=None, op0=ALU.mult),
                    reads=[("ps", bO[0]), er], writes=[("oa", u2)])
                P.add("dve", lambda e, E=E, bO=bO, u2=u2: e.scalar_tensor_tensor(
                    out=o_b[u2][:, :], in0=ps[bO[1]][:, 0:128], scalar=E[:, 2:3], in1=o_a[u2][:, :], op0=ALU.mult, op1=ALU.add),
                    reads=[("ps", bO[1]), er, ("oa", u2)], writes=[("ob", u2)])
                P.add("dve", lambda e, E=E, u2=u2: e.scalar_tensor_tensor(
                    out=o_a[u2][:, :], in0=o_b[u2][:, :], scalar=1.0, in1=o_b[u2][:, :], op0=ALU.mult, op1=ALU.mult,
                    accum_out=E[:, 3:4]), reads=[("ob", u2), ("oa", u2)], writes=[("oa", u2), er])
                P.add("act", lambda e, E=E: e.activation(out=E[:, 4:5], in_=E[:, 3:4], func=AF.Ln, scale=1.0 / 128, bias=EPS),
                      reads=[er], writes=[er])
                P.add("act", lambda e, E=E: e.activation(out=E[:, 5:6], in_=E[:, 4:5], func=AF.Exp, scale=-0.5),
                      reads=[er], writes=[er])
                P.add("dve", lambda e, E=E, u2=u2: e.scalar_tensor_tensor(
                    out=o_c[u2][:, :], in0=o_b[u2][:, :], scalar=E[:, 5:6], in1=cs(C_GREP, 128), op0=ALU.mult, op1=ALU.mult),
                    reads=[("ob", u2), er, "cst"], writes=[("oc", u2)])
                bt = nextbank(4, 8)
                P.add("pe", lambda e, u2=u2, bt=bt: e.transpose(out=psb[bt][:, 0:128], in_=o_c[u2][:, :], identity=ident_b[:, :]),
                      reads=[("oc", u2), "ident"], writes=[("ps", bt)])
                P.add("act", lambda e, bt=bt, h=h, i=i: e.activation(
                    out=mixA[:, h, i * 128:(i + 1) * 128], in_=psb[bt][:, 0:128], func=AF.Copy),
                    reads=[("ps", bt)], writes=[("mixA", h, i // 4)])
                unit += 1

        ck('att')
        P.fence()

        for c in range(16):
            s = load_w("a", w_o[c, :, :, :])
            r = c % 2
            P.add("sp", lambda e, c=c, r=r: e.dma_start(out=xs[r][:, :], in_=xT_own[:, c, :]), writes=[("xs", r)], dma=("xs", r))
            for th in range(2):
                b = nextbank()
                for fc in range(16):
                    rhs = (mixP if fc < 8 else mixA)
                    P.add("pe", lambda e, fc=fc, s=s, th=th, b=b, rhs=rhs: e.matmul(
                        ps[b][:, :], lhsT=wr_a[s][:, fc, :], rhs=rhs[:, fc % 8, th * 512:(th + 1) * 512],
                        start=(fc == 0), stop=(fc == 15)),
                        reads=[("w", s), (("mixP", fc, th) if fc < 8 else ("mixA", fc - 8, th))], writes=[("ps", b)])
                P.add("dve", lambda e, c=c, th=th, b=b, r=r: e.tensor_tensor(
                    out=X1[:, c, th * 512:(th + 1) * 512], in0=ps[b][:, :], in1=xs[r][:, th * 512:(th + 1) * 512], op=ALU.add),
                    reads=[("ps", b), ("xs", r)], writes=[("X1", c)])

        ck('wout')
        P.fence()

        def get_x1(dc):
            return X1[:, dc, :], [("X1", dc)]
        norm_stats(get_x1, sqbC, rstdC, "n2")
        for dc in range(NDC):
            P.add("dve", lambda e, dc=dc: e.scalar_tensor_tensor(
                out=h2T[:, dc, :], in0=X1[:, dc, :], scalar=cs(C_G2 + dc), in1=rstdC[:, :], op0=ALU.mult, op1=ALU.mult),
                reads=[("X1", dc), "n2rstdF", "cst"], writes=["h2T"])

        ck('n2')
        sgc = [0]
        for G in range(NG):
            hb = hTg[G % 2]
            for jj in range(JG):
                j = G * JG + jj
                s = load_w("gu", w_gu[j, :, :, :, :])
                bks = {}
                for gu in range(2):
                    for th in range(2):
                        b = nextbank()
                        bks[(gu, th)] = b
                        for dc in range(NDC):
                            P.add("pe", lambda e, dc=dc, s=s, gu=gu, th=th, b=b: e.matmul(
                                ps[b][:, :], lhsT=wr_gu[s][:, gu, dc, :], rhs=h2T[:, dc, th * 512:(th + 1) * 512],
                                start=(dc == 0), stop=(dc == NDC - 1)),
                                reads=[("w", s), "h2T"], writes=[("ps", b)])
                for th in range(2):
                    r = sgc[0] % 2
                    sgc[0] += 1
                    bg, bu = bks[(0, th)], bks[(1, th)]
                    P.add("act", lambda e, r=r, bg=bg: e.activation(out=sg[r][:, :], in_=ps[bg][:, :], func=AF.Silu),
                          reads=[("ps", bg)], writes=[("sg", r)])
                    P.add("dve", lambda e, r=r, bu=bu, jj=jj, th=th, hb=hb: e.tensor_tensor(
                        out=hb[:, jj, th * 512:(th + 1) * 512], in0=sg[r][:, :], in1=ps[bu][:, :], op=ALU.mult),
                        reads=[("sg", r), ("ps", bu)], writes=[("hTg", G % 2, th)])
            for cp in range(8):
                s = load_w("dn", w_dn[G, cp, :, :, :])
                for cc in range(2):
                    c = cp * 2 + cc
                    for th in range(2):
                        b = nextbank()
                        for jj in range(JG):
                            P.add("pe", lambda e, jj=jj, s=s, cc=cc, th=th, b=b, hb=hb: e.matmul(
                                ps[b][:, :], lhsT=wr_dn[s][:, jj, cc * 128:(cc + 1) * 128], rhs=hb[:, jj, th * 512:(th + 1) * 512],
                                start=(jj == 0), stop=(jj == JG - 1)),
                                reads=[("w", s), ("hTg", G % 2, th)], writes=[("ps", b)])
                        P.add("dve", lambda e, c=c, th=th, b=b: e.tensor_tensor(
                            out=X1[:, c, th * 512:(th + 1) * 512], in0=X1[:, c, th * 512:(th + 1) * 512], in1=ps[b][:, :], op=ALU.add),
                            reads=[("ps", b), ("X1", c)], writes=[("X1", c)])

        ck('ffn')
        norm_stats(get_x1, sqbC, rstdC, "n3")
        for dc in range(NDC):
            P.add("dve", lambda e, dc=dc: e.scalar_tensor_tensor(
                out=X1[:, dc, :], in0=X1[:, dc, :], scalar=cs(C_G3 + dc), in1=rstdC[:, :], op0=ALU.mult, op1=ALU.mult),
                reads=[("X1", dc), "n3rstdF", "cst"], writes=[("X1", dc)])
            P.add("sp", lambda e, dc=dc: e.dma_start(out=outT[:, dc, :], in_=X1[:, dc, :]),
                  reads=[("X1", dc)], writes=[("out", dc)], dma="out")
        P.add("sp", None, reads=[("out", dc) for dc in range(NDC)])

    try:
        body()
    except _Stop:
        dump(stop)

    with contextlib.ExitStack() as stack:
        P.finalize(nc, stack)
        with nc.Block() as block:
            @block.tensor
            def _(e):
                P.emit("pe", e)

            @block.vector
            def _(e):
                P.emit("dve", e)

            @block.scalar
            def _(e):
                P.emit("act", e)

            @block.gpsimd
            def _(e):
                P.emit("pool", e)

            @block.sync
            def _(e):
                P.emit("sp", e)
    return nc


def _fm(a):
    t = a.shape[0]
    return np.ascontiguousarray(a.T.reshape(NDC, 128, t).transpose(1, 0, 2))


def prepare_inputs(x, norm_mix, w_in, w_pool, pool_scale, lambda_q1, lambda_k1, lambda_q2, lambda_k2,
                   subln_gain, w_out, norm_ffn, w_gate_up, w_down, norm_final):
    f = np.float32
    x = np.asarray(x, f)
    w_in = np.asarray(w_in, f)[0]
    w_pool = np.asarray(w_pool, f)[0]
    w_out = np.asarray(w_out, f)[0]
    w_gate_up = np.asarray(w_gate_up, f)[0]
    w_down = np.asarray(w_down, f)[0]

    def colchunks(w, ncol):
        C = w.shape[1]
        return np.ascontiguousarray(w.reshape(NDC, 128, C // ncol, ncol).transpose(2, 1, 0, 3))

    w_uqk = colchunks(w_in[:, :3072], 128)
    w_v = colchunks(w_in[:, 3072:], 256)
    w_pool_r = np.ascontiguousarray(w_pool.reshape(4, 2, 128, 256).transpose(2, 0, 1, 3))
    w_o = colchunks(w_out, 128)
    w_gu = np.ascontiguousarray(w_gate_up.reshape(NDC, 128, 2, NJ, 128).transpose(3, 1, 2, 0, 4))
    w_dn = np.ascontiguousarray(w_down.reshape(NG, JG, 128, 8, 256).transpose(0, 3, 2, 1, 4))

    inv_freq = (np.float32(10000.0) ** (-(np.arange(0, 64, 2, dtype=f) / f(64)))).astype(f)
    p_idx = np.arange(128)
    fr = inv_freq[p_idx % 32]
    sign = np.where((p_idx % 64) < 32, -1.0, 1.0).astype(f)
    partner = np.where((p_idx % 64) < 32, p_idx + 32, p_idx - 32)
    Rm = np.zeros((128, 128), f)
    Rm[partner, p_idx] = 1.0

    base = np.zeros((128, C_N), f)
    base[:, C_G1:C_G1 + 16] = np.asarray(norm_mix, f)[0].reshape(16, 128).T
    base[:, C_G2:C_G2 + 16] = np.asarray(norm_ffn, f)[0].reshape(16, 128).T
    base[:, C_G3:C_G3 + 16] = np.asarray(norm_final, f).reshape(16, 128).T
    base[:, C_PS:C_PS + 8] = np.asarray(pool_scale, f)[0].reshape(8, 128).T
    base[:, C_LQ1:C_LQ1 + 64] = np.asarray(lambda_q1, f)[0][None, :]
    base[:, C_LK1:C_LK1 + 64] = np.asarray(lambda_k1, f)[0][None, :]
    base[:, C_LQ2:C_LQ2 + 64] = np.asarray(lambda_q2, f)[0][None, :]
    base[:, C_LK2:C_LK2 + 64] = np.asarray(lambda_k2, f)[0][None, :]
    base[:, C_GREP:C_GREP + 128] = np.asarray(subln_gain, f)[0][None, :]
    base[:, C_ONES:C_ONES + 128] = 1.0
    base[:, C_ID:C_ID + 128] = np.eye(128, dtype=f)
    base[:, C_RM:C_RM + 128] = Rm

    shared = dict(w_uqk=w_uqk, w_v=w_v, w_pool=w_pool_r, w_o=w_o, w_gu=w_gu, w_dn=w_dn)
    in_maps = []
    for core in range(8):
        b, half = core // 2, core % 2
        own0 = half * S_OWN
        xo = _fm(x[b, own0:own0 + S_OWN])
        if half == 1:
            xp = _fm(x[b, 0:S_OWN])
        else:
            xp = np.zeros_like(xo)
        rot = np.zeros((128, 2, 2, 1024), f)
        for ip in range(2):
            pos = (own0 - S_OWN + ip * S_OWN + np.arange(S_OWN)).astype(f)
            ang = (pos[None, :] * fr[:, None]).astype(f)
            rot[:, ip, 0, :] = np.cos(ang)
            rot[:, ip, 1, :] = np.sin(ang) * sign[:, None]
        c = base.copy()
        c[:, C_PB] = 0.0 if half == 1 else NEG_BIG
        for g, w in enumerate((2, 4, 8, 16)):
            posn = own0 + np.arange(16)
            c[:, C_IC + 16 * g:C_IC + 16 * (g + 1)] = (1.0 / np.minimum(posn + 1, w).astype(f))[None, :]
        m = dict(shared)
        m.update(xT_own=xo, xT_prev=xp, rot=rot, cst=c)
        in_maps.append(m)
    return in_maps


_NC_CACHE = {}


def kernel(x, norm_mix, w_in, w_pool, pool_scale, lambda_q1, lambda_k1, lambda_q2, lambda_k2,
           subln_gain, w_out, norm_ffn, w_gate_up, w_down, norm_final):
    in_maps = prepare_inputs(x, norm_mix, w_in, w_pool, pool_scale, lambda_q1, lambda_k1, lambda_q2, lambda_k2,
                             subln_gain, w_out, norm_ffn, w_gate_up, w_down, norm_final)
    if "nc" not in _NC_CACHE:
        _NC_CACHE["nc"] = build_program()
    nc = _NC_CACHE["nc"]
    res = run_bass_kernel_spmd(nc, in_maps, core_ids=list(range(8)))
    out = np.empty((4, 2048, D), np.float32)
    for core in range(8):
        b, half = core // 2, core % 2
        o = np.asarray(res.results[core]["outT"], np.float32)
        out[b, half * S_OWN:(half + 1) * S_OWN, :] = o.transpose(2, 1, 0).reshape(S_OWN, D)
    return out
```

```python
import contextlib
import math

import numpy as np
import concourse.bass as bass
import concourse.mybir as mybir
from concourse.bass_utils import run_bass_kernel_spmd

F32 = mybir.dt.float32
BF16 = mybir.dt.bfloat16
AF = mybir.ActivationFunctionType
ALU = mybir.AluOpType
AX = mybir.AxisListType

D = 2048
NDC = 16
S_OWN = 1024
FFN = 5632
NJ = 44
NG = 4
JG = 11
EPS = 1e-6
LAM_INIT = 0.8 - 0.6 * math.exp(0.0)
NEG_BIG = -30000.0

C_G1, C_G2, C_G3 = 0, 16, 32
C_PS = 48
C_PB = 56
C_IC = 57
C_LQ1, C_LK1, C_LQ2, C_LK2 = 121, 185, 249, 313
C_GREP = 377
C_ONES = 505
C_ID = 633
C_RM = 761
C_N = 896

NS = 4


class Op:
    __slots__ = ("id", "eng", "idx", "fn", "deps", "dma", "signal", "cum", "ordinal", "ndma")


class Prog:
    ENGS = ("pe", "act", "dve", "pool", "sp")

    def __init__(self):
        self.ops = []
        self.eng_ops = {e: [] for e in self.ENGS}
        self.last_w = {}
        self.readers = {}
        self.pending_fence = {}
        self.last_dma = {}
        self.aggregate = set()

    def add(self, eng, fn, reads=(), writes=(), dma=None, ndma=1):
        op = Op()
        op.ndma = ndma
        op.id = len(self.ops)
        op.eng = eng
        op.fn = fn
        op.dma = dma
        op.signal = False
        writes = list(writes) + [r for r in reads if isinstance(r, tuple) and r[0] == "ps" and r not in writes]
        deps = set()
        for r in reads:
            lw = self.last_w.get(r)
            if lw is not None:
                deps.add(lw)
        for w in writes:
            lw = self.last_w.get(w)
            if lw is not None:
                deps.add(lw)
            rs = self.readers.get(w)
            if rs:
                deps.update(rs)
        for r in reads:
            self.readers.setdefault(r, []).append(op.id)
        for w in writes:
            self.last_w[w] = op.id
            self.readers[w] = []
        pf = self.pending_fence.pop(eng, None)
        if pf:
            deps.update(pf)
        deps.discard(op.id)
        if eng == "pe" and dma is None:
            deps = {d for d in deps if not (self.ops[d].eng == "pe" and self.ops[d].dma is None)}
        op.deps = deps
        op.idx = len(self.eng_ops[eng])
        self.eng_ops[eng].append(op)
        self.ops.append(op)
        if dma is not None:
            self.last_dma[dma] = op.id
        return op.id

    def fence(self):
        deps = set()
        for e in self.ENGS:
            lst = self.eng_ops[e]
            for o in reversed(lst):
                if o.dma is None and o.fn is not None:
                    deps.add(o.id)
                    break
        deps.update(self.last_dma.values())
        for e in self.ENGS:
            self.pending_fence.setdefault(e, set()).update(deps)

    def finalize(self, nc, stack):
        for op in self.ops:
            for d in op.deps:
                t = self.ops[d]
                if t.dma is None:
                    t.signal = True
        self.eng_sem = {}
        for e in self.ENGS:
            self.eng_sem[e] = stack.enter_context(nc.semaphore("s_" + e))
            c = 0
            for o in self.eng_ops[e]:
                if o.dma is None and o.signal:
                    c += 1
                o.cum = c
        self.dma_sem = {}
        counts = {}
        for op in self.ops:
            if op.dma is not None:
                k = op.dma
                if k not in self.dma_sem:
                    nm = "d_" + "_".join(str(x) for x in (k if isinstance(k, tuple) else (k,)))
                    self.dma_sem[k] = stack.enter_context(nc.semaphore(nm))
                counts[k] = counts.get(k, 0) + op.ndma
                op.ordinal = counts[k]
        self.dma_total = counts

    def wait_of(self, d):
        t = self.ops[d]
        if t.dma is None:
            return (self.eng_sem[t.eng], t.cum)
        k = t.dma
        n = self.dma_total[k] if k in self.aggregate else t.ordinal
        return (self.dma_sem[k], 16 * n)

    def emit(self, eng, e):
        known = {}
        for op in self.eng_ops[eng]:
            ws = {}
            for d in op.deps:
                sem, val = self.wait_of(d)
                key = id(sem)
                if ws.get(key, (None, 0))[1] < val:
                    ws[key] = (sem, val)
            for key, (sem, val) in ws.items():
                if known.get(key, 0) < val:
                    e.wait_ge(sem, val)
                    known[key] = val
            if op.fn is None:
                continue
            ins = op.fn(e)
            if op.dma is not None:
                for one in (ins if isinstance(ins, list) else [ins]):
                    one.then_inc(self.dma_sem[op.dma], 16)
            elif op.signal:
                ins.then_inc(self.eng_sem[eng], 1)


class _Stop(Exception):
    pass


def build_program(stop=None):
    nc = bass.Bass("TRN2", target_bir_lowering=False)
    dt = nc.dram_tensor
    xT_own = dt("xT_own", [128, NDC, S_OWN], F32, kind="ExternalInput")
    xT_prev = dt("xT_prev", [128, NDC, S_OWN], F32, kind="ExternalInput")
    rot_d = dt("rot", [128, 2, 2, 1024], F32, kind="ExternalInput")
    cst_d = dt("cst", [128, C_N], F32, kind="ExternalInput")
    w_uqk = dt("w_uqk", [24, 128, 16, 128], F32, kind="ExternalInput")
    w_v = dt("w_v", [4, 128, 16, 256], F32, kind="ExternalInput")
    w_pool = dt("w_pool", [128, 4, 2, 256], F32, kind="ExternalInput")
    w_o = dt("w_o", [16, 128, 16, 128], F32, kind="ExternalInput")
    early = stop in ("setup", "pk_load", "pk_mm", "pk_rot", "n1prev", "projprev", "projown", "att", "wout", "n2")
    w_gu = None if early else dt("w_gu", [NJ, 128, 2, 16, 128], F32, kind="ExternalInput")
    w_dn = None if early else dt("w_dn", [NG, 8, 128, JG, 256], F32, kind="ExternalInput")
    outT = dt("outT", [128, NDC, S_OWN], F32, kind="ExternalOutput")

    BASE = 16640
    cnt = [0]

    def sb(name, shape, dtype, off):
        cnt[0] += 1
        return nc.alloc_sbuf_tensor_at(f"{name}_{cnt[0]}", list(shape), dtype, offset=BASE + off)

    cst = sb("cst", [128, C_N], F32, 0)
    ones_b = sb("ones", [128, 128], BF16, 3584)
    ident_b = sb("ident", [128, 128], BF16, 3840)
    Rm_b = sb("rm", [128, 128], BF16, 4096)
    small = sb("small", [128, 256], F32, 4352)
    wp = sb("wp", [128, 4, 2, 256], BF16, 5376)
    W0 = 9472
    wr_a = [sb("wra", [128, 16, 128], BF16, W0 + s * 8192) for s in range(NS)]
    wr_v = [sb("wrv", [128, 16, 256], BF16, W0 + s * 8192) for s in range(NS)]
    wr_gu = [sb("wrgu", [128, 2, 16, 128], BF16, W0 + s * 8192) for s in range(NS)]
    wr_dn = [sb("wrdn", [128, JG, 256], BF16, W0 + s * 8192) for s in range(NS)]
    RH = 42240
    hT = sb("hT", [128, 16, 1024], BF16, RH)
    mixA = sb("mixA", [128, 8, 1024], BF16, RH)
    ATS = RH + 16384
    PT = [sb("PT", [128, 512], BF16, ATS + r * 1024) for r in range(8)]
    PTd = [[sb("PTd", [128, 128], BF16, ATS + 8192 + (r * 2 + c) * 256) for c in range(2)] for r in range(2)]
    o_a = [sb("oa", [128, 128], F32, ATS + 9216 + r * 512) for r in range(2)]
    o_b = [sb("ob", [128, 128], F32, ATS + 10240 + r * 512) for r in range(2)]
    o_c = [sb("oc", [128, 128], BF16, ATS + 11264 + r * 256) for r in range(2)]
    esm = [sb("esm", [128, 16], F32, ATS + 11776 + r * 64) for r in range(2)]
    h2T = sb("h2T", [128, 16, 1024], BF16, RH)
    RK = 75008
    KT = sb("KT", [128, 8, 2048], BF16, RK)
    QT = sb("QT", [128, 8, 1024], BF16, RK + 32768)
    Vaug = sb("Vaug", [128, 16, 8, 130], BF16, RK + 49152)
    X1 = sb("X1", [128, 16, 1024], F32, RK)
    CS = RK + 65536
    sqbC = [sb("sqbC", [128, 1024], BF16, CS + r * 2048) for r in range(2)]
    rstdC = sb("rstdC", [128, 1024], F32, CS + 4096)
    sg = [sb("sg", [128, 512], F32, CS + 8192 + r * 2048) for r in range(2)]
    MP = 157440
    mixP = sb("mixP", [128, 8, 1024], BF16, MP)
    hTg = [sb("hTg", [128, JG, 1024], BF16, MP + r * 22528) for r in range(2)]
    RT = 173824
    rot = sb("rot", [128, 2, 1024], F32, RT)
    hThalo = sb("hThalo", [128, 16, 16], BF16, 182016)
    SCR = 182528
    xs = [sb("xs", [128, 1024], F32, SCR + r * 4096) for r in range(2)]
    sqb = [sb("sqb", [128, 1024], BF16, SCR + 8192 + r * 2048) for r in range(2)]
    rstd = sb("rstd", [128, 1024], F32, SCR + 12288)
    ubuf = sb("ubuf", [128, 1040], F32, SCR)
    pa = [sb("pa", [128, 1040], F32, SCR + 4160 * (r + 1)) for r in range(2)]
    pooled = [sb("pooled", [128, 1024], BF16, SCR + 12480 + r * 2048) for r in range(2)]
    tmp16 = sb("tmp16", [128, 16], F32, SCR + 16576)
    RS = 199168
    qb = [sb("qb", [128, 512], BF16, RS + r * 1024) for r in range(2)]
    t1 = [sb("t1", [128, 512], F32, RS + 2048 + r * 2048) for r in range(2)]
    t2 = [sb("t2", [128, 512], F32, RS + 6144 + r * 2048) for r in range(2)]
    assert BASE + RS + 10240 <= 229344

    ps = [nc.alloc_psum_tensor(f"ps{b}", [128, 512], F32) for b in range(8)]
    psb = [p.bitcast(BF16) for p in ps]

    P = Prog()
    P.aggregate.update(["const", "out"])
    bank = [0]

    def nextbank(lo=0, hi=8):
        b = lo + (bank[0] % (hi - lo))
        bank[0] += 1
        return b

    wslot = [0]

    def load_w(kind, src):
        s = wslot[0] % NS
        wslot[0] += 1
        if kind == "a":
            P.add("pool", lambda e, s=s, src=src: e.dma_start(out=wr_a[s][:, :, :], in_=src, max_dma_last_dim=4096),
                  writes=[("w", s)], dma=("w", s))
        elif kind == "v":
            P.add("pool", lambda e, s=s, src=src: e.dma_start(out=wr_v[s][:, :, :], in_=src, max_dma_last_dim=4096),
                  writes=[("w", s)], dma=("w", s))
        elif kind == "gu":
            P.add("pool", lambda e, s=s, src=src: [
                e.dma_start(out=wr_gu[s][:, 0, :, :], in_=src[:, 0, :, :], max_dma_last_dim=4096),
                e.dma_start(out=wr_gu[s][:, 1, :, :], in_=src[:, 1, :, :], max_dma_last_dim=4096)],
                writes=[("w", s)], dma=("w", s), ndma=2)
        elif kind == "dn":
            P.add("pool", lambda e, s=s, src=src: e.dma_start(out=wr_dn[s][:, :, :], in_=src, max_dma_last_dim=4096),
                  writes=[("w", s)], dma=("w", s))
        return s

    def cs(c0, n=1):
        return cst[:, c0:c0 + n]

    def ck(name):
        if stop == name:
            raise _Stop()

    def dump(name):
        srcs = {
            "setup": [(small[:, :], outT[:, 0, 0:256])],
            "pk_load": [(wr_a[0][:, dc, :], outT[:, dc, 0:128]) for dc in range(16)],
            "pk_mm": [(hT[:, 0, :], outT[:, 0, :])],
            "pk_rot": [(KT[:, 0, 0:1024], outT[:, 0, :])],
            "n1prev": [(rstd[:, :], outT[:, 0, :])] + [(hT[:, dc, :], outT[:, 1 + dc % 15, :]) for dc in (0, 5)],
            "projprev": [(KT[:, h, 0:1024], outT[:, h, :]) for h in range(8)] +
                        [(Vaug[:, tb, :, 0:128], outT[:, 8 + tb, :]) for tb in range(8)],
            "projown": [(KT[:, 0, 1024:2048], outT[:, 0, :]), (KT[:, 7, 0:1024], outT[:, 3, :])] +
                       [(QT[:, h, :], outT[:, 1 + h, :]) for h in (0, 1)] +
                       [(mixP[:, c, :], outT[:, 4 + c, :]) for c in range(8)] +
                       [(Vaug[:, tb, :, 0:128], outT[:, 12 + i_, :]) for i_, tb in enumerate((8, 9, 3, 15))],
            "att": [(mixA[:, h, :], outT[:, h, :]) for h in range(8)] + [(mixP[:, c, :], outT[:, 8 + c, :]) for c in range(8)],
            "wout": [(X1[:, c, :], outT[:, c, :]) for c in range(16)],
            "n2": [(X1[:, c, :], outT[:, c, :]) for c in range(8)] + [(h2T[:, c, :], outT[:, 8 + c, :]) for c in range(8)],
            "ffn": [(X1[:, c, :], outT[:, c, :]) for c in range(16)],
        }[name]
        P.fence()
        for i, (src, dst) in enumerate(srcs):
            P.add("pool", lambda e, src=src, dst=dst: e.dma_start(out=dst, in_=src, max_dma_last_dim=2048), writes=[("out", i)], dma="out")
        P.add("sp", None, reads=[("out", i) for i in range(len(srcs))])

    def body():
        P.add("sp", lambda e: e.dma_start(out=cst[:, :], in_=cst_d[:, :]), writes=["cst"], dma="const")
        P.add("pool", lambda e: e.dma_start(out=wp[:, :, :, :], in_=w_pool[:, :, :, :], max_dma_last_dim=4096),
              writes=["wp"], dma="wp")
        P.add("dve", lambda e: e.tensor_copy(out=ones_b[:, :], in_=cs(C_ONES, 128)), reads=["cst"], writes=["ones"])
        P.add("dve", lambda e: e.tensor_copy(out=ident_b[:, :], in_=cs(C_ID, 128)), reads=["cst"], writes=["ident"])
        P.add("dve", lambda e: e.tensor_copy(out=Rm_b[:, :], in_=cs(C_RM, 128)), reads=["cst"], writes=["rm"])
        P.add("dve", lambda e: e.tensor_tensor(out=small[:, 0:64], in0=cs(C_LQ1, 64), in1=cs(C_LK1, 64), op=ALU.mult),
              reads=["cst"], writes=["sm0"])
        P.add("dve", lambda e: e.reduce_sum(out=small[:, 128:129], in_=small[:, 0:64], axis=AX.X), reads=["sm0"], writes=["sm1"])
        P.add("dve", lambda e: e.tensor_tensor(out=small[:, 64:128], in0=cs(C_LQ2, 64), in1=cs(C_LK2, 64), op=ALU.mult),
              reads=["cst"], writes=["sm2"])
        P.add("dve", lambda e: e.reduce_sum(out=small[:, 129:130], in_=small[:, 64:128], axis=AX.X), reads=["sm2"], writes=["sm3"])
        P.add("act", lambda e: e.activation(out=small[:, 130:132], in_=small[:, 128:130], func=AF.Exp),
              reads=["sm1", "sm3"], writes=["sm4"])
        P.add("dve", lambda e: e.tensor_tensor(out=small[:, 132:133], in0=small[:, 131:132], in1=small[:, 130:131], op=ALU.subtract),
              reads=["sm4"], writes=["sm5"])
        P.add("dve", lambda e: e.tensor_scalar_add(out=small[:, 133:134], in0=small[:, 132:133], scalar1=-LAM_INIT),
              reads=["sm5"], writes=["neglam"])
        neglam = small[:, 133:134]
        P.add("dve", lambda e: e.tensor_scalar_mul(out=cst[:, C_GREP:C_GREP + 128], in0=cst[:, C_GREP:C_GREP + 128],
                                                   scalar1=(1.0 - LAM_INIT)), reads=["cst"], writes=["cst"])

        ck('setup')
        def norm_stats(get_src, sq_bufs, rstd_buf, tag):
            b0, b1 = nextbank(), nextbank()
            for dc in range(NDC):
                ap, res = get_src(dc)
                r = dc % 2
                P.add("act", lambda e, ap=ap, r=r: e.activation(out=sq_bufs[r][:, :], in_=ap, func=AF.Square),
                      reads=res, writes=[(tag + "sq", r)])
                for th, b in ((0, b0), (1, b1)):
                    P.add("pe", lambda e, r=r, th=th, b=b, dc=dc: e.matmul(
                        ps[b][:, :], lhsT=ones_b[:, :], rhs=sq_bufs[r][:, th * 512:(th + 1) * 512],
                        start=(dc == 0), stop=(dc == NDC - 1)),
                        reads=[(tag + "sq", r), "ones"], writes=[("ps", b)])
            for th, b in ((0, b0), (1, b1)):
                P.add("dve", lambda e, th=th, b=b: e.tensor_scalar(
                    out=rstd_buf[:, th * 512:(th + 1) * 512], in0=ps[b][:, :], scalar1=1.0 / D, scalar2=EPS,
                    op0=ALU.mult, op1=ALU.add), reads=[("ps", b)], writes=[(tag + "rstd", th)])
            P.add("act", lambda e: e.activation(out=rstd_buf[:, :], in_=rstd_buf[:, :], func=AF.Sqrt),
                  reads=[(tag + "rstd", 0), (tag + "rstd", 1)], writes=[tag + "rstd2"])
            P.add("dve", lambda e: e.reciprocal(out=rstd_buf[:, :], in_=rstd_buf[:, :]),
                  reads=[tag + "rstd2"], writes=[tag + "rstdF"])

        rot_pending = []

        def flush_rot():
            while rot_pending:
                rot_pending.pop(0)()

        rotc = [0]

        def rotary_tile(b, dest_ap, dest_res, th):
            r = rotc[0] % 2
            rotc[0] += 1
            P.add("act", lambda e, b=b, r=r: e.activation(out=qb[r][:, :], in_=ps[b][:, :], func=AF.Copy),
                  reads=[("ps", b)], writes=[("qb", r)])

            def rest(b=b, r=r, dest_ap=dest_ap, dest_res=dest_res, th=th):
                b2 = nextbank()
                P.add("pe", lambda e: e.matmul(ps[b2][:, :], lhsT=Rm_b[:, :], rhs=qb[r][:, :], start=True, stop=True),
                      reads=[("qb", r), "rm"], writes=[("ps", b2)])
                P.add("dve", lambda e: e.tensor_tensor(out=t1[r][:, :], in0=ps[b][:, :], in1=rot[:, 0, th * 512:(th + 1) * 512], op=ALU.mult),
                      reads=[("ps", b), "rot"], writes=[("t1", r)])
                P.add("dve", lambda e: e.tensor_tensor(out=t2[r][:, :], in0=ps[b2][:, :], in1=rot[:, 1, th * 512:(th + 1) * 512], op=ALU.mult),
                      reads=[("ps", b2), "rot"], writes=[("t2", r)])
                P.add("dve", lambda e: e.tensor_tensor(out=dest_ap, in0=t1[r][:, :], in1=t2[r][:, :], op=ALU.add),
                      reads=[("t1", r), ("t2", r)], writes=[dest_res])
            rot_pending.append(rest)

        def proj_fm_matmuls(s, th, b):
            for dc in range(NDC):
                P.add("pe", lambda e, dc=dc: e.matmul(ps[b][:, :], lhsT=wr_a[s][:, dc, :], rhs=hT[:, dc, th * 512:(th + 1) * 512],
                                                       start=(dc == 0), stop=(dc == NDC - 1)),
                      reads=[("w", s), "hT"], writes=[("ps", b)])

        for ipass, (xsrc, tokoff, tboff) in enumerate(((xT_prev, 0, 0), (xT_own, 1024, 8))):
            own = ipass == 1
            P.add("sp", lambda e, ipass=ipass: e.dma_start(out=rot[:, :, :], in_=rot_d[:, ipass, :, :]), writes=["rot"], dma="rot")

            def get_src(dc, xsrc=xsrc):
                r = dc % 2
                P.add("sp", lambda e, dc=dc, r=r: e.dma_start(out=xs[r][:, :], in_=xsrc[:, dc, :]), writes=[("xs", r)], dma=("xs", r))
                return xs[r][:, :], [("xs", r)]
            norm_stats(get_src, sqb, rstd, "n1")
            for dc in range(NDC):
                r = dc % 2
                P.add("sp", lambda e, dc=dc, r=r, xsrc=xsrc: e.dma_start(out=xs[r][:, :], in_=xsrc[:, dc, :]), writes=[("xs", r)], dma=("xs", r))
                P.add("dve", lambda e, dc=dc, r=r: e.scalar_tensor_tensor(
                    out=hT[:, dc, :], in0=xs[r][:, :], scalar=cs(C_G1 + dc), in1=rstd[:, :], op0=ALU.mult, op1=ALU.mult),
                    reads=[("xs", r), "n1rstdF", "cst"], writes=["hT"])
            if not own:
                P.add("dve", lambda e: e.tensor_copy(out=hThalo[:, :, :], in_=hT[:, :, 1008:1024]), reads=["hT"], writes=["hThalo"])
                ck('n1prev')
            else:
                P.fence()

            if own:
                for n in range(8):
                    g = n // 2
                    a_ = n % 2
                    s = load_w("a", w_uqk[n, :, :, :])
                    bh, b0, b1 = nextbank(), nextbank(), nextbank()
                    for dc in range(NDC):
                        P.add("pe", lambda e, dc=dc, s=s, bh=bh: e.matmul(ps[bh][:, 0:16], lhsT=wr_a[s][:, dc, :], rhs=hThalo[:, dc, :],
                                                                          start=(dc == 0), stop=(dc == NDC - 1)),
                              reads=[("w", s), "hThalo"], writes=[("ps", bh)])
                    proj_fm_matmuls(s, 0, b0)
                    proj_fm_matmuls(s, 1, b1)
                    flush_rot()
                    P.add("act", lambda e, bh=bh: e.activation(out=ubuf[:, 0:16], in_=ps[bh][:, 0:16], func=AF.Copy),
                          reads=[("ps", bh)], writes=["ubuf"])
                    P.add("act", lambda e, b0=b0: e.activation(out=ubuf[:, 16:528], in_=ps[b0][:, :], func=AF.Copy),
                          reads=[("ps", b0), "ubuf"], writes=["ubuf"])
                    P.add("act", lambda e, b1=b1: e.activation(out=ubuf[:, 528:1040], in_=ps[b1][:, :], func=AF.Copy),
                          reads=[("ps", b1), "ubuf"], writes=["ubuf"])
                    m = g + 1
                    src = ubuf
                    src_res = "ubuf"
                    for st in range(1, m + 1):
                        sh = 2 ** (st - 1)
                        S0 = 2 ** st - 1
                        dst = pa[(st - 1) % 2]
                        dres = ("pa", (st - 1) % 2)
                        P.add("dve", lambda e, src=src, dst=dst, sh=sh, S0=S0: e.tensor_tensor(
                            out=dst[:, S0:1040], in0=src[:, S0:1040], in1=src[:, S0 - sh:1040 - sh], op=ALU.add),
                            reads=[src_res], writes=[dres])
                        src, src_res = dst, dres
                    w_ = 2 ** m
                    P.add("dve", lambda e, src=src, a_=a_, w_=w_: e.scalar_tensor_tensor(
                        out=pooled[a_][:, :], in0=src[:, 16:1040], scalar=1.0 / w_, in1=ubuf[:, 16:1040],
                        op0=ALU.mult, op1=ALU.subtract), reads=[src_res, "ubuf"], writes=[("pooled", a_)])
                    P.add("dve", lambda e, src=src, g=g: e.tensor_tensor(
                        out=tmp16[:, :], in0=src[:, 16:32], in1=cs(C_IC + 16 * g, 16), op=ALU.mult),
                        reads=[src_res, "cst"], writes=["tmp16"])
                    P.add("dve", lambda e, a_=a_: e.tensor_tensor(
                        out=pooled[a_][:, 0:16], in0=tmp16[:, :], in1=ubuf[:, 16:32], op=ALU.subtract),
                        reads=["tmp16", "ubuf", ("pooled", a_)], writes=[("pooled", a_)])
                    if a_ == 1:
                        for o in range(2):
                            for th in range(2):
                                b = nextbank()
                                for a2 in range(2):
                                    P.add("pe", lambda e, g=g, a2=a2, o=o, th=th, b=b: e.matmul(
                                        ps[b][:, :], lhsT=wp[:, g, a2, o * 128:(o + 1) * 128],
                                        rhs=pooled[a2][:, th * 512:(th + 1) * 512], start=(a2 == 0), stop=(a2 == 1)),
                                        reads=["wp", ("pooled", a2)], writes=[("ps", b)])
                                P.add("act", lambda e, g=g, o=o, th=th, b=b: e.activation(
                                    out=mixP[:, 2 * g + o, th * 512:(th + 1) * 512], in_=ps[b][:, :], func=AF.Identity,
                                    scale=cs(C_PS + 2 * g + o)), reads=[("ps", b), "cst"], writes=[("mixP", 2 * g + o, th)])

            kinds = (["q"] if own else []) + ["k"]
            for kind in kinds:
                for hc in range(8):
                    n = (8 if kind == "q" else 16) + hc
                    s = load_w("a", w_uqk[n, :, :, :])
                    ck('pk_load')
                    for th in range(2):
                        b = nextbank()
                        proj_fm_matmuls(s, th, b)
                        ck('pk_mm')
                        flush_rot()
                        if th == 1:
                            ck('pk_rot')
                        if kind == "q":
                            dest = QT[:, hc, th * 512:(th + 1) * 512]
                            dres = ("QT", hc, th)
                        else:
                            dest = KT[:, hc, tokoff + th * 512: tokoff + (th + 1) * 512]
                            dres = ("KT", hc, ipass, th)
                        rotary_tile(b, dest, dres, th)
            flush_rot()

            if not own:
                P.add("dve", lambda e: e.memset(Vaug[:, :, :, 128:130], 1.0), writes=["Vones"])
            for vc in range(4):
                s = load_w("v", w_v[vc, :, :, :])
                for tbl in range(8):
                    b = nextbank()
                    for dc in range(NDC):
                        P.add("pe", lambda e, dc=dc, s=s, tbl=tbl, b=b: e.matmul(
                            ps[b][:, 0:256], lhsT=hT[:, dc, tbl * 128:(tbl + 1) * 128], rhs=wr_v[s][:, dc, :],
                            start=(dc == 0), stop=(dc == NDC - 1)), reads=[("w", s), "hT"], writes=[("ps", b)])
                    for hh in range(2):
                        P.add("act", lambda e, b=b, hh=hh, vc=vc, tbl=tbl, tboff=tboff: e.activation(
                            out=Vaug[:, tboff + tbl, 2 * vc + hh, 0:128], in_=ps[b][:, hh * 128:(hh + 1) * 128], func=AF.Copy),
                            reads=[("ps", b)], writes=[("V", tboff + tbl, 2 * vc + hh)])
            if not own:
                ck('projprev')

        ck('projown')
        P.fence()

        for r in range(2):
            for c in range(2):
                P.add("dve", lambda e, r=r, c=c: e.memset(PTd[r][c][:, :], 0.0), writes=[("PTd", r)])
        ptc = [0]
        NPT = len(PT)

        def emit_qk(g):
            h, i, kbs, bS = g["h"], g["i"], g["kbs"], g["bS"]
            for idx, kb in enumerate(kbs):
                for c in range(2):
                    P.add("pe", lambda e, c=c, idx=idx, kb=kb, h=h, i=i, bS=bS: e.matmul(
                        ps[bS[c]][:, idx * 128:(idx + 1) * 128],
                        lhsT=KT[c * 64:(c + 1) * 64, h, kb * 128:(kb + 1) * 128],
                        rhs=QT[c * 64:(c + 1) * 64, h, i * 128:(i + 1) * 128], start=True, stop=True),
                        reads=[("QT", h, i // 4), ("KT", h, kb // 8, (kb % 8) // 4)], writes=[("ps", bS[c])])

        def emit_exp(g):
            bS, kbs, u2 = g["bS"], g["kbs"], g["u2"]
            W = len(kbs) * 128
            if g["diag"]:
                for c in range(2):
                    P.add("act", lambda e, c=c, u2=u2, bS=bS: e.activation(
                        out=PTd[u2][c][0:64, 0:128], in_=ps[bS[c]][0:64, 0:128], func=AF.Exp, scale=0.125),
                        reads=[("ps", bS[c])], writes=[("PTd", u2)])
                    P.add("act", lambda e, c=c, u2=u2, bS=bS: e.activation(
                        out=PTd[u2][c][64:128, 64:128], in_=ps[bS[c]][64:128, 64:128], func=AF.Exp, scale=0.125),
                        reads=[("ps", bS[c]), ("PTd", u2)], writes=[("PTd", u2)])
                return
            prs = []
            for c in range(2):
                pr = ptc[0] % NPT
                ptc[0] += 1
                prs.append(pr)
                if g["prev"]:
                    P.add("act", lambda e, c=c, pr=pr, bS=bS, W=W: e.activation(
                        out=PT[pr][:, 0:W], in_=ps[bS[c]][:, 0:W], func=AF.Exp, bias=cs(C_PB), scale=0.125),
                        reads=[("ps", bS[c]), "cst"], writes=[("PT", pr)])
                else:
                    P.add("act", lambda e, c=c, pr=pr, bS=bS, W=W: e.activation(
                        out=PT[pr][:, 0:W], in_=ps[bS[c]][:, 0:W], func=AF.Exp, scale=0.125),
                        reads=[("ps", bS[c])], writes=[("PT", pr)])
            g["prs"] = prs

        def emit_av(g):
            h, kbs, bO, u2 = g["h"], g["kbs"], g["bO"], g["u2"]
            if g["diag"]:
                kb = kbs[0]
                for c in range(2):
                    P.add("pe", lambda e, c=c, kb=kb, h=h, u2=u2, bO=bO: e.matmul(
                        ps[bO[c]][:, 0:129], lhsT=PTd[u2][c][:, :], rhs=Vaug[:, kb, h, 0:129], start=False, stop=True),
                        reads=[("PTd", u2), ("V", kb, h), "Vones"], writes=[("ps", bO[c])])
                return
            prs = g["prs"]
            for idx, kb in enumerate(kbs):
                for c in range(2):
                    st = g["first"] and idx == 0
                    P.add("pe", lambda e, c=c, idx=idx, kb=kb, h=h, pr=prs[c], st=st, bO=bO: e.matmul(
                        ps[bO[c]][:, 0:129], lhsT=PT[pr][:, idx * 128:(idx + 1) * 128], rhs=Vaug[:, kb, h, 0:129],
                        start=st, stop=False),
                        reads=[("PT", prs[c]), ("V", kb, h), "Vones"], writes=[("ps", bO[c])])

        def epilogue_stages(h, i, u2, bO):
            E = esm[u2]
            er = ("esm", u2)

            def stA():
                P.add("dve", lambda e: e.reciprocal(out=E[:, 0:1], in_=ps[bO[0]][:, 128:129]),
                      reads=[("ps", bO[0])], writes=[er])
                P.add("dve", lambda e: e.reciprocal(out=E[:, 1:2], in_=ps[bO[1]][:, 128:129]),
                      reads=[("ps", bO[1]), er], writes=[er])
                P.add("dve", lambda e: e.tensor_tensor(out=E[:, 2:3], in0=E[:, 1:2], in1=neglam, op=ALU.mult),
                      reads=[er, "neglam"], writes=[er])
                P.add("dve", lambda e: e.tensor_scalar(
                    out=o_a[u2][:, :], in0=ps[bO[0]][:, 0:128], scalar1=E[:, 0:1], scalar2=None, op0=ALU.mult),
                    reads=[("ps", bO[0]), er], writes=[("oa", u2)])
                P.add("dve", lambda e: e.scalar_tensor_tensor(
                    out=o_b[u2][:, :], in0=ps[bO[1]][:, 0:128], scalar=E[:, 2:3], in1=o_a[u2][:, :], op0=ALU.mult, op1=ALU.add),
                    reads=[("ps", bO[1]), er, ("oa", u2)], writes=[("ob", u2)])
                P.add("dve", lambda e: e.scalar_tensor_tensor(
                    out=o_a[u2][:, :], in0=o_b[u2][:, :], scalar=1.0, in1=o_b[u2][:, :], op0=ALU.mult, op1=ALU.mult,
                    accum_out=E[:, 3:4]), reads=[("ob", u2), ("oa", u2)], writes=[("oa", u2), er])

            def stB():
                P.add("act", lambda e: e.activation(out=E[:, 4:5], in_=E[:, 3:4], func=AF.Ln, scale=1.0 / 128, bias=EPS),
                      reads=[er], writes=[er])
                P.add("act", lambda e: e.activation(out=E[:, 5:6], in_=E[:, 4:5], func=AF.Exp, scale=-0.5),
                      reads=[er], writes=[er])
                P.add("dve", lambda e: e.scalar_tensor_tensor(
                    out=o_c[u2][:, :], in0=o_b[u2][:, :], scalar=E[:, 5:6], in1=cs(C_GREP, 128), op0=ALU.mult, op1=ALU.mult),
                    reads=[("ob", u2), er, "cst"], writes=[("oc", u2)])

            def stC():
                bt = bO[0]
                P.add("pe", lambda e: e.transpose(out=psb[bt][:, 0:128], in_=o_c[u2][:, :], identity=ident_b[:, :]),
                      reads=[("oc", u2), "ident"], writes=[("ps", bt)])
                P.add("act", lambda e: e.activation(
                    out=mixA[:, h, i * 128:(i + 1) * 128], in_=psb[bt][:, 0:128], func=AF.Copy),
                    reads=[("ps", bt)], writes=[("mixA", h, i // 4)])
            return [stA, stB, stC]

        glist = []
        unit = 0
        for h in range(8):
            for i in range(8):
                u2 = unit % 2
                bO = (0, 1) if u2 == 0 else (2, 3)
                gs = [([0, 1, 2, 3], True), ([4, 5, 6, 7], True)]
                ownk = list(range(8, 8 + i))
                while ownk:
                    gs.append((ownk[:4], False))
                    ownk = ownk[4:]
                ug = []
                for gi, (kbs, isprev) in enumerate(gs):
                    ug.append(dict(h=h, i=i, u2=u2, bO=bO, kbs=kbs, prev=isprev, diag=False, first=(gi == 0), last=False, gi=gi))
                ug.append(dict(h=h, i=i, u2=u2, bO=bO, kbs=[8 + i], prev=False, diag=True, first=False, last=True, gi=len(gs)))
                glist.extend(ug)
                unit += 1

        pend = {}

        def start_group(g):
            g["bS"] = (nextbank(4, 8), nextbank(4, 8))
            emit_qk(g)

        start_group(glist[0])
        for n, g in enumerate(glist):
            if n + 1 < len(glist):
                start_group(glist[n + 1])
            emit_exp(g)
            emit_av(g)
            for stg in pend.pop(g["gi"], []):
                stg()
            if g["last"]:
                for k_ in sorted(pend):
                    for stg in pend[k_]:
                        stg()
                pend.clear()
                sA, sB, sC = epilogue_stages(g["h"], g["i"], g["u2"], g["bO"])
                sA()
                pend = {0: [sB], 1: [sC]}
        for k_ in sorted(pend):
            for stg in pend[k_]:
                stg()

        ck('att')
        P.fence()

        for c in range(16):
            s = load_w("a", w_o[c, :, :, :])
            r = c % 2
            P.add("sp", lambda e, c=c, r=r: e.dma_start(out=xs[r][:, :], in_=xT_own[:, c, :]), writes=[("xs", r)], dma=("xs", r))
            for th in range(2):
                b = nextbank()
                for fc in range(16):
                    rhs = (mixP if fc < 8 else mixA)
                    P.add("pe", lambda e, fc=fc, s=s, th=th, b=b, rhs=rhs: e.matmul(
                        ps[b][:, :], lhsT=wr_a[s][:, fc, :], rhs=rhs[:, fc % 8, th * 512:(th + 1) * 512],
                        start=(fc == 0), stop=(fc == 15)),
                        reads=[("w", s), (("mixP", fc, th) if fc < 8 else ("mixA", fc - 8, th))], writes=[("ps", b)])
                P.add("dve", lambda e, c=c, th=th, b=b, r=r: e.tensor_tensor(
                    out=X1[:, c, th * 512:(th + 1) * 512], in0=ps[b][:, :], in1=xs[r][:, th * 512:(th + 1) * 512], op=ALU.add),
                    reads=[("ps", b), ("xs", r)], writes=[("X1", c)])

        ck('wout')
        P.fence()

        def get_x1(dc):
            return X1[:, dc, :], [("X1", dc)]
        norm_stats(get_x1, sqbC, rstdC, "n2")
        for dc in range(NDC):
            P.add("dve", lambda e, dc=dc: e.scalar_tensor_tensor(
                out=h2T[:, dc, :], in0=X1[:, dc, :], scalar=cs(C_G2 + dc), in1=rstdC[:, :], op0=ALU.mult, op1=ALU.mult),
                reads=[("X1", dc), "n2rstdF", "cst"], writes=["h2T"])

        ck('n2')
        sgc = [0]
        for G in range(NG):
            hb = hTg[G % 2]
            for jj in range(JG):
                j = G * JG + jj
                s = load_w("gu", w_gu[j, :, :, :, :])
                bks = {}
                for gu in range(2):
                    for th in range(2):
                        b = nextbank()
                        bks[(gu, th)] = b
                        for dc in range(NDC):
                            P.add("pe", lambda e, dc=dc, s=s, gu=gu, th=th, b=b: e.matmul(
                                ps[b][:, :], lhsT=wr_gu[s][:, gu, dc, :], rhs=h2T[:, dc, th * 512:(th + 1) * 512],
                                start=(dc == 0), stop=(dc == NDC - 1)),
                                reads=[("w", s), "h2T"], writes=[("ps", b)])
                for th in range(2):
                    r = sgc[0] % 2
                    sgc[0] += 1
                    bg, bu = bks[(0, th)], bks[(1, th)]
                    P.add("act", lambda e, r=r, bg=bg: e.activation(out=sg[r][:, :], in_=ps[bg][:, :], func=AF.Silu),
                          reads=[("ps", bg)], writes=[("sg", r)])
                    P.add("dve", lambda e, r=r, bu=bu, jj=jj, th=th, hb=hb: e.tensor_tensor(
                        out=hb[:, jj, th * 512:(th + 1) * 512], in0=sg[r][:, :], in1=ps[bu][:, :], op=ALU.mult),
                        reads=[("sg", r), ("ps", bu)], writes=[("hTg", G % 2, th)])
            for cp in range(8):
                s = load_w("dn", w_dn[G, cp, :, :, :])
                for cc in range(2):
                    c = cp * 2 + cc
                    for th in range(2):
                        b = nextbank()
                        for jj in range(JG):
                            P.add("pe", lambda e, jj=jj, s=s, cc=cc, th=th, b=b, hb=hb: e.matmul(
                                ps[b][:, :], lhsT=wr_dn[s][:, jj, cc * 128:(cc + 1) * 128], rhs=hb[:, jj, th * 512:(th + 1) * 512],
                                start=(jj == 0), stop=(jj == JG - 1)),
                                reads=[("w", s), ("hTg", G % 2, th)], writes=[("ps", b)])
                        P.add("dve", lambda e, c=c, th=th, b=b: e.tensor_tensor(
                            out=X1[:, c, th * 512:(th + 1) * 512], in0=X1[:, c, th * 512:(th + 1) * 512], in1=ps[b][:, :], op=ALU.add),
                            reads=[("ps", b), ("X1", c)], writes=[("X1", c)])

        ck('ffn')
        norm_stats(get_x1, sqbC, rstdC, "n3")
        for dc in range(NDC):
            P.add("dve", lambda e, dc=dc: e.scalar_tensor_tensor(
                out=X1[:, dc, :], in0=X1[:, dc, :], scalar=cs(C_G3 + dc), in1=rstdC[:, :], op0=ALU.mult, op1=ALU.mult),
                reads=[("X1", dc), "n3rstdF", "cst"], writes=[("X1", dc)])
            P.add("sp", lambda e, dc=dc: e.dma_start(out=outT[:, dc, :], in_=X1[:, dc, :]),
                  reads=[("X1", dc)], writes=[("out", dc)], dma="out")
        P.add("sp", None, reads=[("out", dc) for dc in range(NDC)])

    try:
        body()
    except _Stop:
        dump(stop)

    with contextlib.ExitStack() as stack:
        P.finalize(nc, stack)
        with nc.Block() as block:
            @block.tensor
            def _(e):
                P.emit("pe", e)

            @block.vector
            def _(e):
                P.emit("dve", e)

            @block.scalar
            def _(e):
                P.emit("act", e)

            @block.gpsimd
            def _(e):
                P.emit("pool", e)

            @block.sync
            def _(e):
                P.emit("sp", e)
    return nc


def _fm(a):
    t = a.shape[0]
    return np.ascontiguousarray(a.T.reshape(NDC, 128, t).transpose(1, 0, 2))


def prepare_inputs(x, norm_mix, w_in, w_pool, pool_scale, lambda_q1, lambda_k1, lambda_q2, lambda_k2,
                   subln_gain, w_out, norm_ffn, w_gate_up, w_down, norm_final):
    f = np.float32
    x = np.asarray(x, f)
    w_in = np.asarray(w_in, f)[0]
    w_pool = np.asarray(w_pool, f)[0]
    w_out = np.asarray(w_out, f)[0]
    w_gate_up = np.asarray(w_gate_up, f)[0]
    w_down = np.asarray(w_down, f)[0]

    def colchunks(w, ncol):
        C = w.shape[1]
        return np.ascontiguousarray(w.reshape(NDC, 128, C // ncol, ncol).transpose(2, 1, 0, 3))

    w_uqk = colchunks(w_in[:, :3072], 128)
    w_v = colchunks(w_in[:, 3072:], 256)
    w_pool_r = np.ascontiguousarray(w_pool.reshape(4, 2, 128, 256).transpose(2, 0, 1, 3))
    w_o = colchunks(w_out, 128)
    w_gu = np.ascontiguousarray(w_gate_up.reshape(NDC, 128, 2, NJ, 128).transpose(3, 1, 2, 0, 4))
    w_dn = np.ascontiguousarray(w_down.reshape(NG, JG, 128, 8, 256).transpose(0, 3, 2, 1, 4))

    inv_freq = (np.float32(10000.0) ** (-(np.arange(0, 64, 2, dtype=f) / f(64)))).astype(f)
    p_idx = np.arange(128)
    fr = inv_freq[p_idx % 32]
    sign = np.where((p_idx % 64) < 32, -1.0, 1.0).astype(f)
    partner = np.where((p_idx % 64) < 32, p_idx + 32, p_idx - 32)
    Rm = np.zeros((128, 128), f)
    Rm[partner, p_idx] = 1.0

    base = np.zeros((128, C_N), f)
    base[:, C_G1:C_G1 + 16] = np.asarray(norm_mix, f)[0].reshape(16, 128).T
    base[:, C_G2:C_G2 + 16] = np.asarray(norm_ffn, f)[0].reshape(16, 128).T
    base[:, C_G3:C_G3 + 16] = np.asarray(norm_final, f).reshape(16, 128).T
    base[:, C_PS:C_PS + 8] = np.asarray(pool_scale, f)[0].reshape(8, 128).T
    base[:, C_LQ1:C_LQ1 + 64] = np.asarray(lambda_q1, f)[0][None, :]
    base[:, C_LK1:C_LK1 + 64] = np.asarray(lambda_k1, f)[0][None, :]
    base[:, C_LQ2:C_LQ2 + 64] = np.asarray(lambda_q2, f)[0][None, :]
    base[:, C_LK2:C_LK2 + 64] = np.asarray(lambda_k2, f)[0][None, :]
    base[:, C_GREP:C_GREP + 128] = np.asarray(subln_gain, f)[0][None, :]
    base[:, C_ONES:C_ONES + 128] = 1.0
    base[:, C_ID:C_ID + 128] = np.eye(128, dtype=f)
    base[:, C_RM:C_RM + 128] = Rm

    shared = dict(w_uqk=w_uqk, w_v=w_v, w_pool=w_pool_r, w_o=w_o, w_gu=w_gu, w_dn=w_dn)
    in_maps = []
    for core in range(8):
        b, half = core // 2, core % 2
        own0 = half * S_OWN
        xo = _fm(x[b, own0:own0 + S_OWN])
        if half == 1:
            xp = _fm(x[b, 0:S_OWN])
        else:
            xp = np.zeros_like(xo)
        rot = np.zeros((128, 2, 2, 1024), f)
        for ip in range(2):
            pos = (own0 - S_OWN + ip * S_OWN + np.arange(S_OWN)).astype(f)
            ang = (pos[None, :] * fr[:, None]).astype(f)
            rot[:, ip, 0, :] = np.cos(ang)
            rot[:, ip, 1, :] = np.sin(ang) * sign[:, None]
        c = base.copy()
        c[:, C_PB] = 0.0 if half == 1 else NEG_BIG
        for g, w in enumerate((2, 4, 8, 16)):
            posn = own0 + np.arange(16)
            c[:, C_IC + 16 * g:C_IC + 16 * (g + 1)] = (1.0 / np.minimum(posn + 1, w).astype(f))[None, :]
        m = dict(shared)
        m.update(xT_own=xo, xT_prev=xp, rot=rot, cst=c)
        in_maps.append(m)
    return in_maps


_NC_CACHE = {}


def kernel(x, norm_mix, w_in, w_pool, pool_scale, lambda_q1, lambda_k1, lambda_q2, lambda_k2,
           subln_gain, w_out, norm_ffn, w_gate_up, w_down, norm_final):
    in_maps = prepare_inputs(x, norm_mix, w_in, w_pool, pool_scale, lambda_q1, lambda_k1, lambda_q2, lambda_k2,
                             subln_gain, w_out, norm_ffn, w_gate_up, w_down, norm_final)
    if "nc" not in _NC_CACHE:
        _NC_CACHE["nc"] = build_program()
    nc = _NC_CACHE["nc"]
    res = run_bass_kernel_spmd(nc, in_maps, core_ids=list(range(8)))
    out = np.empty((4, 2048, D), np.float32)
    for core in range(8):
        b, half = core // 2, core % 2
        o = np.asarray(res.results[core]["outT"], np.float32)
        out[b, half * S_OWN:(half + 1) * S_OWN, :] = o.transpose(2, 1, 0).reshape(S_OWN, D)
    return out
```

```python
import contextlib
import math

import numpy as np
import concourse.bass as bass
import concourse.mybir as mybir
from concourse.bass_utils import run_bass_kernel_spmd

F32 = mybir.dt.float32
BF16 = mybir.dt.bfloat16
AF = mybir.ActivationFunctionType
ALU = mybir.AluOpType
AX = mybir.AxisListType

D = 2048
NDC = 16
S_OWN = 1024
FFN = 5632
NJ = 44
NG = 4
JG = 11
EPS = 1e-6
LAM_INIT = 0.8 - 0.6 * math.exp(0.0)
NEG_BIG = -30000.0

C_G1, C_G2, C_G3 = 0, 16, 32
C_PS = 48
C_PB = 56
C_IC = 57
C_LQ1, C_LK1, C_LQ2, C_LK2 = 121, 185, 249, 313
C_GREP = 377
C_ONES = 505
C_ID = 633
C_RM = 761
C_N = 896

NS = 4


class Op:
    __slots__ = ("id", "eng", "idx", "fn", "deps", "dma", "signal", "cum", "ordinal", "ndma")


class Prog:
    ENGS = ("pe", "act", "dve", "pool", "sp")

    def __init__(self):
        self.ops = []
        self.eng_ops = {e: [] for e in self.ENGS}
        self.last_w = {}
        self.readers = {}
        self.pending_fence = {}
        self.last_dma = {}
        self.aggregate = set()

    def add(self, eng, fn, reads=(), writes=(), dma=None, ndma=1):
        op = Op()
        op.ndma = ndma
        op.id = len(self.ops)
        op.eng = eng
        op.fn = fn
        op.dma = dma
        op.signal = False
        writes = list(writes) + [r for r in reads if isinstance(r, tuple) and r[0] == "ps" and r not in writes]
        deps = set()
        for r in reads:
            lw = self.last_w.get(r)
            if lw is not None:
                deps.add(lw)
        for w in writes:
            lw = self.last_w.get(w)
            if lw is not None:
                deps.add(lw)
            rs = self.readers.get(w)
            if rs:
                deps.update(rs)
        for r in reads:
            self.readers.setdefault(r, []).append(op.id)
        for w in writes:
            self.last_w[w] = op.id
            self.readers[w] = []
        pf = self.pending_fence.pop(eng, None)
        if pf:
            deps.update(pf)
        deps.discard(op.id)
        if eng == "pe" and dma is None:
            deps = {d for d in deps if not (self.ops[d].eng == "pe" and self.ops[d].dma is None)}
        op.deps = deps
        op.idx = len(self.eng_ops[eng])
        self.eng_ops[eng].append(op)
        self.ops.append(op)
        if dma is not None:
            self.last_dma[dma] = op.id
        return op.id

    def fence(self):
        deps = set()
        for e in self.ENGS:
            lst = self.eng_ops[e]
            for o in reversed(lst):
                if o.dma is None and o.fn is not None:
                    deps.add(o.id)
                    break
        deps.update(self.last_dma.values())
        for e in self.ENGS:
            self.pending_fence.setdefault(e, set()).update(deps)

    def finalize(self, nc, stack):
        for op in self.ops:
            for d in op.deps:
                t = self.ops[d]
                if t.dma is None:
                    t.signal = True
        self.eng_sem = {}
        for e in self.ENGS:
            self.eng_sem[e] = stack.enter_context(nc.semaphore("s_" + e))
            c = 0
            for o in self.eng_ops[e]:
                if o.dma is None and o.signal:
                    c += 1
                o.cum = c
        self.dma_sem = {}
        counts = {}
        for op in self.ops:
            if op.dma is not None:
                k = op.dma
                if k not in self.dma_sem:
                    nm = "d_" + "_".join(str(x) for x in (k if isinstance(k, tuple) else (k,)))
                    self.dma_sem[k] = stack.enter_context(nc.semaphore(nm))
                counts[k] = counts.get(k, 0) + op.ndma
                op.ordinal = counts[k]
        self.dma_total = counts

    def wait_of(self, d):
        t = self.ops[d]
        if t.dma is None:
            return (self.eng_sem[t.eng], t.cum)
        k = t.dma
        n = self.dma_total[k] if k in self.aggregate else t.ordinal
        return (self.dma_sem[k], 16 * n)

    def emit(self, eng, e):
        known = {}
        for op in self.eng_ops[eng]:
            ws = {}
            for d in op.deps:
                sem, val = self.wait_of(d)
                key = id(sem)
                if ws.get(key, (None, 0))[1] < val:
                    ws[key] = (sem, val)
            for key, (sem, val) in ws.items():
                if known.get(key, 0) < val:
                    e.wait_ge(sem, val)
                    known[key] = val
            if op.fn is None:
                continue
            ins = op.fn(e)
            if op.dma is not None:
                for one in (ins if isinstance(ins, list) else [ins]):
                    one.then_inc(self.dma_sem[op.dma], 16)
            elif op.signal:
                ins.then_inc(self.eng_sem[eng], 1)


class _Stop(Exception):
    pass


def build_program(stop=None):
    nc = bass.Bass("TRN2", target_bir_lowering=False)
    dt = nc.dram_tensor
    xT_own = dt("xT_own", [128, NDC, S_OWN], F32, kind="ExternalInput")
    xT_prev = dt("xT_prev", [128, NDC, S_OWN], F32, kind="ExternalInput")
    rot_d = dt("rot", [128, 2, 2, 1024], F32, kind="ExternalInput")
    cst_d = dt("cst", [128, C_N], F32, kind="ExternalInput")
    w_uqk = dt("w_uqk", [24, 128, 16, 128], F32, kind="ExternalInput")
    w_v = dt("w_v", [4, 128, 16, 256], F32, kind="ExternalInput")
    w_pool = dt("w_pool", [128, 4, 2, 256], F32, kind="ExternalInput")
    w_o = dt("w_o", [16, 128, 16, 128], F32, kind="ExternalInput")
    early = stop in ("setup", "pk_load", "pk_mm", "pk_rot", "n1prev", "projprev", "projown", "att", "wout", "n2")
    w_gu = None if early else dt("w_gu", [NJ, 128, 2, 16, 128], F32, kind="ExternalInput")
    w_dn = None if early else dt("w_dn", [NG, 8, 128, JG, 256], F32, kind="ExternalInput")
    outT = dt("outT", [128, NDC, S_OWN], F32, kind="ExternalOutput")

    BASE = 16640
    cnt = [0]

    def sb(name, shape, dtype, off):
        cnt[0] += 1
        return nc.alloc_sbuf_tensor_at(f"{name}_{cnt[0]}", list(shape), dtype, offset=BASE + off)

    cst = sb("cst", [128, C_N], F32, 0)
    ones_b = sb("ones", [128, 128], BF16, 3584)
    ident_b = sb("ident", [128, 128], BF16, 3840)
    Rm_b = sb("rm", [128, 128], BF16, 4096)
    small = sb("small", [128, 256], F32, 4352)
    wp = sb("wp", [128, 4, 2, 256], BF16, 5376)
    W0 = 9472
    wr_a = [sb("wra", [128, 16, 128], BF16, W0 + s * 8192) for s in range(NS)]
    wr_v = [sb("wrv", [128, 16, 256], BF16, W0 + s * 8192) for s in range(NS)]
    wr_gu = [sb("wrgu", [128, 2, 16, 128], BF16, W0 + s * 8192) for s in range(NS)]
    wr_dn = [sb("wrdn", [128, JG, 256], BF16, W0 + s * 8192) for s in range(NS)]
    RH = 42240
    hT = sb("hT", [128, 16, 1024], BF16, RH)
    mixA = sb("mixA", [128, 8, 1024], BF16, RH)
    ATS = RH + 16384
    PT = [sb("PT", [128, 512], BF16, ATS + r * 1024) for r in range(8)]
    PTd = [[sb("PTd", [128, 128], BF16, ATS + 8192 + (r * 2 + c) * 256) for c in range(2)] for r in range(2)]
    o_a = [sb("oa", [128, 128], F32, ATS + 9216 + r * 512) for r in range(2)]
    o_b = [sb("ob", [128, 128], F32, ATS + 10240 + r * 512) for r in range(2)]
    o_c = [sb("oc", [128, 128], BF16, ATS + 11264 + r * 256) for r in range(2)]
    esm = [sb("esm", [128, 16], F32, ATS + 11776 + r * 64) for r in range(2)]
    h2T = sb("h2T", [128, 16, 1024], BF16, RH)
    RK = 75008
    KT = sb("KT", [128, 8, 2048], BF16, RK)
    QT = sb("QT", [128, 8, 1024], BF16, RK + 32768)
    Vaug = sb("Vaug", [128, 16, 8, 130], BF16, RK + 49152)
    X1 = sb("X1", [128, 16, 1024], F32, RK)
    CS = RK + 65536
    sqbC = [sb("sqbC", [128, 1024], BF16, CS + r * 2048) for r in range(2)]
    rstdC = sb("rstdC", [128, 1024], F32, CS + 4096)
    sg = [sb("sg", [128, 512], F32, CS + 8192 + r * 2048) for r in range(2)]
    MP = 157440
    mixP = sb("mixP", [128, 8, 1024], BF16, MP)
    hTg = [sb("hTg", [128, JG, 1024], BF16, MP + r * 22528) for r in range(2)]
    RT = 173824
    rot = sb("rot", [128, 2, 1024], F32, RT)
    hThalo = sb("hThalo", [128, 16, 16], BF16, 182016)
    SCR = 182528
    xs = [sb("xs", [128, 1024], F32, SCR + r * 4096) for r in range(2)]
    sqb = [sb("sqb", [128, 1024], BF16, SCR + 8192 + r * 2048) for r in range(2)]
    rstd = sb("rstd", [128, 1024], F32, SCR + 12288)
    ubuf = sb("ubuf", [128, 1040], F32, SCR)
    pa = [sb("pa", [128, 1040], F32, SCR + 4160 * (r + 1)) for r in range(2)]
    pooled = [sb("pooled", [128, 1024], BF16, SCR + 12480 + r * 2048) for r in range(2)]
    tmp16 = sb("tmp16", [128, 16], F32, SCR + 16576)
    RS = 199168
    qb = [sb("qb", [128, 512], BF16, RS + r * 1024) for r in range(2)]
    t1 = [sb("t1", [128, 512], F32, RS + 2048 + r * 2048) for r in range(2)]
    t2 = [sb("t2", [128, 512], F32, RS + 6144 + r * 2048) for r in range(2)]
    assert BASE + RS + 10240 <= 229344

    ps = [nc.alloc_psum_tensor(f"ps{b}", [128, 512], F32) for b in range(8)]
    psb = [p.bitcast(BF16) for p in ps]

    P = Prog()
    P.aggregate.update(["const", "out"])
    bank = [0]

    def nextbank(lo=0, hi=8):
        b = lo + (bank[0] % (hi - lo))
        bank[0] += 1
        return b

    wslot = [0]

    def load_w(kind, src):
        s = wslot[0] % NS
        wslot[0] += 1
        if kind == "a":
            P.add("pool", lambda e, s=s, src=src: e.dma_start(out=wr_a[s][:, :, :], in_=src, max_dma_last_dim=4096),
                  writes=[("w", s)], dma=("w", s))
        elif kind == "v":
            P.add("pool", lambda e, s=s, src=src: e.dma_start(out=wr_v[s][:, :, :], in_=src, max_dma_last_dim=4096),
                  writes=[("w", s)], dma=("w", s))
        elif kind == "gu":
            P.add("pool", lambda e, s=s, src=src: [
                e.dma_start(out=wr_gu[s][:, 0, :, :], in_=src[:, 0, :, :], max_dma_last_dim=4096),
                e.dma_start(out=wr_gu[s][:, 1, :, :], in_=src[:, 1, :, :], max_dma_last_dim=4096)],
                writes=[("w", s)], dma=("w", s), ndma=2)
        elif kind == "dn":
            P.add("pool", lambda e, s=s, src=src: e.dma_start(out=wr_dn[s][:, :, :], in_=src, max_dma_last_dim=4096),
                  writes=[("w", s)], dma=("w", s))
        return s

    def cs(c0, n=1):
        return cst[:, c0:c0 + n]

    def ck(name):
        if stop == name:
            raise _Stop()

    def dump(name):
        srcs = {
            "setup": [(small[:, :], outT[:, 0, 0:256])],
            "pk_load": [(wr_a[0][:, dc, :], outT[:, dc, 0:128]) for dc in range(16)],
            "pk_mm": [(hT[:, 0, :], outT[:, 0, :])],
            "pk_rot": [(KT[:, 0, 0:1024], outT[:, 0, :])],
            "n1prev": [(rstd[:, :], outT[:, 0, :])] + [(hT[:, dc, :], outT[:, 1 + dc % 15, :]) for dc in (0, 5)],
            "projprev": [(KT[:, h, 0:1024], outT[:, h, :]) for h in range(8)] +
                        [(Vaug[:, tb, :, 0:128], outT[:, 8 + tb, :]) for tb in range(8)],
            "projown": [(KT[:, 0, 1024:2048], outT[:, 0, :]), (KT[:, 7, 0:1024], outT[:, 3, :])] +
                       [(QT[:, h, :], outT[:, 1 + h, :]) for h in (0, 1)] +
                       [(mixP[:, c, :], outT[:, 4 + c, :]) for c in range(8)] +
                       [(Vaug[:, tb, :, 0:128], outT[:, 12 + i_, :]) for i_, tb in enumerate((8, 9, 3, 15))],
            "att": [(mixA[:, h, :], outT[:, h, :]) for h in range(8)] + [(mixP[:, c, :], outT[:, 8 + c, :]) for c in range(8)],
            "wout": [(X1[:, c, :], outT[:, c, :]) for c in range(16)],
            "n2": [(X1[:, c, :], outT[:, c, :]) for c in range(8)] + [(h2T[:, c, :], outT[:, 8 + c, :]) for c in range(8)],
            "ffn": [(X1[:, c, :], outT[:, c, :]) for c in range(16)],
        }[name]
        P.fence()
        for i, (src, dst) in enumerate(srcs):
            P.add("pool", lambda e, src=src, dst=dst: e.dma_start(out=dst, in_=src, max_dma_last_dim=2048), writes=[("out", i)], dma="out")
        P.add("sp", None, reads=[("out", i) for i in range(len(srcs))])

    def body():
        P.add("sp", lambda e: e.dma_start(out=cst[:, :], in_=cst_d[:, :]), writes=["cst"], dma="const")
        P.add("pool", lambda e: e.dma_start(out=wp[:, :, :, :], in_=w_pool[:, :, :, :], max_dma_last_dim=4096),
              writes=["wp"], dma="wp")
        P.add("dve", lambda e: e.tensor_copy(out=ones_b[:, :], in_=cs(C_ONES, 128)), reads=["cst"], writes=["ones"])
        P.add("dve", lambda e: e.tensor_copy(out=ident_b[:, :], in_=cs(C_ID, 128)), reads=["cst"], writes=["ident"])
        P.add("dve", lambda e: e.tensor_copy(out=Rm_b[:, :], in_=cs(C_RM, 128)), reads=["cst"], writes=["rm"])
        P.add("dve", lambda e: e.tensor_tensor(out=small[:, 0:64], in0=cs(C_LQ1, 64), in1=cs(C_LK1, 64), op=ALU.mult),
              reads=["cst"], writes=["sm0"])
        P.add("dve", lambda e: e.reduce_sum(out=small[:, 128:129], in_=small[:, 0:64], axis=AX.X), reads=["sm0"], writes=["sm1"])
        P.add("dve", lambda e: e.tensor_tensor(out=small[:, 64:128], in0=cs(C_LQ2, 64), in1=cs(C_LK2, 64), op=ALU.mult),
              reads=["cst"], writes=["sm2"])
        P.add("dve", lambda e: e.reduce_sum(out=small[:, 129:130], in_=small[:, 64:128], axis=AX.X), reads=["sm2"], writes=["sm3"])
        P.add("act", lambda e: e.activation(out=small[:, 130:132], in_=small[:, 128:130], func=AF.Exp),
              reads=["sm1", "sm3"], writes=["sm4"])
        P.add("dve", lambda e: e.tensor_tensor(out=small[:, 132:133], in0=small[:, 131:132], in1=small[:, 130:131], op=ALU.subtract),
              reads=["sm4"], writes=["sm5"])
        P.add("dve", lambda e: e.tensor_scalar_add(out=small[:, 133:134], in0=small[:, 132:133], scalar1=-LAM_INIT),
              reads=["sm5"], writes=["neglam"])
        neglam = small[:, 133:134]
        P.add("dve", lambda e: e.tensor_scalar_mul(out=cst[:, C_GREP:C_GREP + 128], in0=cst[:, C_GREP:C_GREP + 128],
                                                   scalar1=(1.0 - LAM_INIT)), reads=["cst"], writes=["cst"])

        ck('setup')
        def norm_stats(get_src, sq_bufs, rstd_buf, tag):
            b0, b1 = nextbank(), nextbank()
            for dc in range(NDC):
                r = dc % 2
                for th, b in ((0, b0), (1, b1)):
                    ap, res = get_src(dc, th)
                    P.add("act", lambda e, ap=ap, r=r, th=th: e.activation(
                        out=sq_bufs[r][:, th * 512:(th + 1) * 512], in_=ap, func=AF.Square),
                        reads=res, writes=[(tag + "sq", r, th)])
                    P.add("pe", lambda e, r=r, th=th, b=b, dc=dc: e.matmul(
                        ps[b][:, :], lhsT=ones_b[:, :], rhs=sq_bufs[r][:, th * 512:(th + 1) * 512],
                        start=(dc == 0), stop=(dc == NDC - 1)),
                        reads=[(tag + "sq", r, th), "ones"], writes=[("ps", b)])
            for th, b in ((0, b0), (1, b1)):
                P.add("dve", lambda e, th=th, b=b: e.tensor_scalar(
                    out=rstd_buf[:, th * 512:(th + 1) * 512], in0=ps[b][:, :], scalar1=1.0 / D, scalar2=EPS,
                    op0=ALU.mult, op1=ALU.add), reads=[("ps", b)], writes=[(tag + "rstd", th)])
            P.add("act", lambda e: e.activation(out=rstd_buf[:, :], in_=rstd_buf[:, :], func=AF.Sqrt),
                  reads=[(tag + "rstd", 0), (tag + "rstd", 1)], writes=[tag + "rstd2"])
            P.add("dve", lambda e: e.reciprocal(out=rstd_buf[:, :], in_=rstd_buf[:, :]),
                  reads=[tag + "rstd2"], writes=[tag + "rstdF"])

        rot_pending = []
        wp_pending = []

        def flush_wp():
            while wp_pending:
                wp_pending.pop(0)()

        def flush_rot():
            while rot_pending:
                rot_pending.pop(0)()

        rotc = [0]

        def rotary_tile(b, dest_ap, dest_res, th):
            r = rotc[0] % 2
            rotc[0] += 1
            P.add("act", lambda e, b=b, r=r: e.activation(out=qb[r][:, :], in_=ps[b][:, :], func=AF.Copy),
                  reads=[("ps", b)], writes=[("qb", r)])

            def rest(b=b, r=r, dest_ap=dest_ap, dest_res=dest_res, th=th):
                b2 = nextbank()
                P.add("pe", lambda e: e.matmul(ps[b2][:, :], lhsT=Rm_b[:, :], rhs=qb[r][:, :], start=True, stop=True),
                      reads=[("qb", r), "rm"], writes=[("ps", b2)])
                P.add("dve", lambda e: e.tensor_tensor(out=t1[r][:, :], in0=ps[b][:, :], in1=rot[:, 0, th * 512:(th + 1) * 512], op=ALU.mult),
                      reads=[("ps", b), "rot"], writes=[("t1", r)])
                P.add("dve", lambda e: e.tensor_tensor(out=t2[r][:, :], in0=ps[b2][:, :], in1=rot[:, 1, th * 512:(th + 1) * 512], op=ALU.mult),
                      reads=[("ps", b2), "rot"], writes=[("t2", r)])
                P.add("dve", lambda e: e.tensor_tensor(out=dest_ap, in0=t1[r][:, :], in1=t2[r][:, :], op=ALU.add),
                      reads=[("t1", r), ("t2", r)], writes=[dest_res])
            rot_pending.append(rest)

        def proj_fm_matmuls(s, th, b):
            for dc in range(NDC):
                P.add("pe", lambda e, dc=dc: e.matmul(ps[b][:, :], lhsT=wr_a[s][:, dc, :], rhs=hT[:, dc, th * 512:(th + 1) * 512],
                                                       start=(dc == 0), stop=(dc == NDC - 1)),
                      reads=[("w", s), "hT"], writes=[("ps", b)])

        for ipass, (xsrc, tokoff, tboff) in enumerate(((xT_prev, 0, 0), (xT_own, 1024, 8))):
            own = ipass == 1
            P.add("sp", lambda e, ipass=ipass: e.dma_start(out=rot[:, :, :], in_=rot_d[:, ipass, :, :]), writes=["rot"], dma="rot")

            def get_src(dc, th, xsrc=xsrc):
                r = dc % 2
                P.add("sp", lambda e, dc=dc, r=r, th=th: e.dma_start(
                    out=xs[r][:, th * 512:(th + 1) * 512], in_=xsrc[:, dc, th * 512:(th + 1) * 512]),
                    writes=[("xs", r, th)], dma=("xs", r, th))
                return xs[r][:, th * 512:(th + 1) * 512], [("xs", r, th)]
            norm_stats(get_src, sqb, rstd, "n1")
            for dc in range(NDC):
                r = dc % 2
                for th in range(2):
                    P.add("sp", lambda e, dc=dc, r=r, th=th, xsrc=xsrc: e.dma_start(
                        out=xs[r][:, th * 512:(th + 1) * 512], in_=xsrc[:, dc, th * 512:(th + 1) * 512]),
                        writes=[("xs", r, th)], dma=("xs", r, th))
                    P.add("dve", lambda e, dc=dc, r=r, th=th: e.scalar_tensor_tensor(
                        out=hT[:, dc, th * 512:(th + 1) * 512], in0=xs[r][:, th * 512:(th + 1) * 512], scalar=cs(C_G1 + dc),
                        in1=rstd[:, th * 512:(th + 1) * 512], op0=ALU.mult, op1=ALU.mult),
                        reads=[("xs", r, th), "n1rstdF", "cst"], writes=["hT"])
            if not own:
                P.add("dve", lambda e: e.tensor_copy(out=hThalo[:, :, :], in_=hT[:, :, 1008:1024]), reads=["hT"], writes=["hThalo"])
                ck('n1prev')
            else:
                P.fence()

            if own:
                for n in range(8):
                    g = n // 2
                    a_ = n % 2
                    s = load_w("a", w_uqk[n, :, :, :])
                    bh, b0, b1 = nextbank(), nextbank(), nextbank()
                    for dc in range(NDC):
                        P.add("pe", lambda e, dc=dc, s=s, bh=bh: e.matmul(ps[bh][:, 0:16], lhsT=wr_a[s][:, dc, :], rhs=hThalo[:, dc, :],
                                                                          start=(dc == 0), stop=(dc == NDC - 1)),
                              reads=[("w", s), "hThalo"], writes=[("ps", bh)])
                    proj_fm_matmuls(s, 0, b0)
                    proj_fm_matmuls(s, 1, b1)
                    flush_wp()
                    P.add("act", lambda e, bh=bh: e.activation(out=ubuf[:, 0:16], in_=ps[bh][:, 0:16], func=AF.Copy),
                          reads=[("ps", bh)], writes=["ubuf"])
                    P.add("act", lambda e, b0=b0: e.activation(out=ubuf[:, 16:528], in_=ps[b0][:, :], func=AF.Copy),
                          reads=[("ps", b0), "ubuf"], writes=["ubuf"])
                    P.add("act", lambda e, b1=b1: e.activation(out=ubuf[:, 528:1040], in_=ps[b1][:, :], func=AF.Copy),
                          reads=[("ps", b1), "ubuf"], writes=["ubuf"])
                    m = g + 1
                    src = ubuf
                    src_res = "ubuf"
                    for st in range(1, m + 1):
                        sh = 2 ** (st - 1)
                        S0 = 2 ** st - 1
                        dst = pa[(st - 1) % 2]
                        dres = ("pa", (st - 1) % 2)
                        P.add("dve", lambda e, src=src, dst=dst, sh=sh, S0=S0: e.tensor_tensor(
                            out=dst[:, S0:1040], in0=src[:, S0:1040], in1=src[:, S0 - sh:1040 - sh], op=ALU.add),
                            reads=[src_res], writes=[dres])
                        src, src_res = dst, dres
                    w_ = 2 ** m
                    P.add("dve", lambda e, src=src, a_=a_, w_=w_: e.scalar_tensor_tensor(
                        out=pooled[a_][:, :], in0=src[:, 16:1040], scalar=1.0 / w_, in1=ubuf[:, 16:1040],
                        op0=ALU.mult, op1=ALU.subtract), reads=[src_res, "ubuf"], writes=[("pooled", a_)])
                    P.add("dve", lambda e, src=src, g=g: e.tensor_tensor(
                        out=tmp16[:, :], in0=src[:, 16:32], in1=cs(C_IC + 16 * g, 16), op=ALU.mult),
                        reads=[src_res, "cst"], writes=["tmp16"])
                    P.add("dve", lambda e, a_=a_: e.tensor_tensor(
                        out=pooled[a_][:, 0:16], in0=tmp16[:, :], in1=ubuf[:, 16:32], op=ALU.subtract),
                        reads=["tmp16", "ubuf", ("pooled", a_)], writes=[("pooled", a_)])
                    if a_ == 1:
                        def wpool_job(g=g):
                            for o in range(2):
                                for th in range(2):
                                    b = nextbank()
                                    for a2 in range(2):
                                        P.add("pe", lambda e, g=g, a2=a2, o=o, th=th, b=b: e.matmul(
                                            ps[b][:, :], lhsT=wp[:, g, a2, o * 128:(o + 1) * 128],
                                            rhs=pooled[a2][:, th * 512:(th + 1) * 512], start=(a2 == 0), stop=(a2 == 1)),
                                            reads=["wp", ("pooled", a2)], writes=[("ps", b)])
                                    P.add("act", lambda e, g=g, o=o, th=th, b=b: e.activation(
                                        out=mixP[:, 2 * g + o, th * 512:(th + 1) * 512], in_=ps[b][:, :], func=AF.Identity,
                                        scale=cs(C_PS + 2 * g + o)), reads=[("ps", b), "cst"], writes=[("mixP", 2 * g + o, th)])
                        wp_pending.append(wpool_job)
                flush_wp()

            kinds = (["q"] if own else []) + ["k"]
            for kind in kinds:
                for hc in range(8):
                    n = (8 if kind == "q" else 16) + hc
                    s = load_w("a", w_uqk[n, :, :, :])
                    ck('pk_load')
                    for th in range(2):
                        b = nextbank()
                        proj_fm_matmuls(s, th, b)
                        ck('pk_mm')
                        flush_rot()
                        if th == 1:
                            ck('pk_rot')
                        if kind == "q":
                            dest = QT[:, hc, th * 512:(th + 1) * 512]
                            dres = ("QT", hc, th)
                        else:
                            dest = KT[:, hc, tokoff + th * 512: tokoff + (th + 1) * 512]
                            dres = ("KT", hc, ipass, th)
                        rotary_tile(b, dest, dres, th)
            flush_rot()

            if not own:
                P.add("dve", lambda e: e.memset(Vaug[:, :, :, 128:130], 1.0), writes=["Vones"])
            for vc in range(4):
                s = load_w("v", w_v[vc, :, :, :])
                for tbl in range(8):
                    b = nextbank()
                    for dc in range(NDC):
                        P.add("pe", lambda e, dc=dc, s=s, tbl=tbl, b=b: e.matmul(
                            ps[b][:, 0:256], lhsT=hT[:, dc, tbl * 128:(tbl + 1) * 128], rhs=wr_v[s][:, dc, :],
                            start=(dc == 0), stop=(dc == NDC - 1)), reads=[("w", s), "hT"], writes=[("ps", b)])
                    for hh in range(2):
                        P.add("act", lambda e, b=b, hh=hh, vc=vc, tbl=tbl, tboff=tboff: e.activation(
                            out=Vaug[:, tboff + tbl, 2 * vc + hh, 0:128], in_=ps[b][:, hh * 128:(hh + 1) * 128], func=AF.Copy),
                            reads=[("ps", b)], writes=[("V", tboff + tbl, 2 * vc + hh)])
            if not own:
                ck('projprev')

        ck('projown')
        P.fence()

        for r in range(2):
            for c in range(2):
                P.add("dve", lambda e, r=r, c=c: e.memset(PTd[r][c][:, :], 0.0), writes=[("PTd", r)])
        ptc = [0]
        NPT = len(PT)

        def emit_qk(g):
            h, i, kbs, bS = g["h"], g["i"], g["kbs"], g["bS"]
            for idx, kb in enumerate(kbs):
                for c in range(2):
                    P.add("pe", lambda e, c=c, idx=idx, kb=kb, h=h, i=i, bS=bS: e.matmul(
                        ps[bS[c]][:, idx * 128:(idx + 1) * 128],
                        lhsT=KT[c * 64:(c + 1) * 64, h, kb * 128:(kb + 1) * 128],
                        rhs=QT[c * 64:(c + 1) * 64, h, i * 128:(i + 1) * 128], start=True, stop=True),
                        reads=[("QT", h, i // 4), ("KT", h, kb // 8, (kb % 8) // 4)], writes=[("ps", bS[c])])

        def emit_exp(g):
            bS, kbs, u2 = g["bS"], g["kbs"], g["u2"]
            W = len(kbs) * 128
            if g["diag"]:
                for c in range(2):
                    P.add("act", lambda e, c=c, u2=u2, bS=bS: e.activation(
                        out=PTd[u2][c][0:64, 0:128], in_=ps[bS[c]][0:64, 0:128], func=AF.Exp, scale=0.125),
                        reads=[("ps", bS[c])], writes=[("PTd", u2)])
                    P.add("act", lambda e, c=c, u2=u2, bS=bS: e.activation(
                        out=PTd[u2][c][64:128, 64:128], in_=ps[bS[c]][64:128, 64:128], func=AF.Exp, scale=0.125),
                        reads=[("ps", bS[c]), ("PTd", u2)], writes=[("PTd", u2)])
                return
            prs = []
            for c in range(2):
                pr = ptc[0] % NPT
                ptc[0] += 1
                prs.append(pr)
                if g["prev"]:
                    P.add("act", lambda e, c=c, pr=pr, bS=bS, W=W: e.activation(
                        out=PT[pr][:, 0:W], in_=ps[bS[c]][:, 0:W], func=AF.Exp, bias=cs(C_PB), scale=0.125),
                        reads=[("ps", bS[c]), "cst"], writes=[("PT", pr)])
                else:
                    P.add("act", lambda e, c=c, pr=pr, bS=bS, W=W: e.activation(
                        out=PT[pr][:, 0:W], in_=ps[bS[c]][:, 0:W], func=AF.Exp, scale=0.125),
                        reads=[("ps", bS[c])], writes=[("PT", pr)])
            g["prs"] = prs

        def emit_av(g):
            h, kbs, bO, u2 = g["h"], g["kbs"], g["bO"], g["u2"]
            if g["diag"]:
                kb = kbs[0]
                for c in range(2):
                    P.add("pe", lambda e, c=c, kb=kb, h=h, u2=u2, bO=bO: e.matmul(
                        ps[bO[c]][:, 0:129], lhsT=PTd[u2][c][:, :], rhs=Vaug[:, kb, h, 0:129], start=False, stop=True),
                        reads=[("PTd", u2), ("V", kb, h), "Vones"], writes=[("ps", bO[c])])
                return
            prs = g["prs"]
            for idx, kb in enumerate(kbs):
                for c in range(2):
                    st = g["first"] and idx == 0
                    P.add("pe", lambda e, c=c, idx=idx, kb=kb, h=h, pr=prs[c], st=st, bO=bO: e.matmul(
                        ps[bO[c]][:, 0:129], lhsT=PT[pr][:, idx * 128:(idx + 1) * 128], rhs=Vaug[:, kb, h, 0:129],
                        start=st, stop=False),
                        reads=[("PT", prs[c]), ("V", kb, h), "Vones"], writes=[("ps", bO[c])])

        def epilogue_stages(h, i, u2, bO):
            E = esm[u2]
            er = ("esm", u2)

            def stA():
                P.add("dve", lambda e: e.reciprocal(out=E[:, 0:1], in_=ps[bO[0]][:, 128:129]),
                      reads=[("ps", bO[0])], writes=[er])
                P.add("dve", lambda e: e.reciprocal(out=E[:, 1:2], in_=ps[bO[1]][:, 128:129]),
                      reads=[("ps", bO[1]), er], writes=[er])
                P.add("dve", lambda e: e.tensor_tensor(out=E[:, 2:3], in0=E[:, 1:2], in1=neglam, op=ALU.mult),
                      reads=[er, "neglam"], writes=[er])
                P.add("dve", lambda e: e.tensor_scalar(
                    out=o_a[u2][:, :], in0=ps[bO[0]][:, 0:128], scalar1=E[:, 0:1], scalar2=None, op0=ALU.mult),
                    reads=[("ps", bO[0]), er], writes=[("oa", u2)])
                P.add("dve", lambda e: e.scalar_tensor_tensor(
                    out=o_b[u2][:, :], in0=ps[bO[1]][:, 0:128], scalar=E[:, 2:3], in1=o_a[u2][:, :], op0=ALU.mult, op1=ALU.add),
                    reads=[("ps", bO[1]), er, ("oa", u2)], writes=[("ob", u2)])
                P.add("dve", lambda e: e.scalar_tensor_tensor(
                    out=o_a[u2][:, :], in0=o_b[u2][:, :], scalar=1.0, in1=o_b[u2][:, :], op0=ALU.mult, op1=ALU.mult,
                    accum_out=E[:, 3:4]), reads=[("ob", u2), ("oa", u2)], writes=[("oa", u2), er])

            def stB():
                P.add("act", lambda e: e.activation(out=E[:, 4:5], in_=E[:, 3:4], func=AF.Ln, scale=1.0 / 128, bias=EPS),
                      reads=[er], writes=[er])
                P.add("act", lambda e: e.activation(out=E[:, 5:6], in_=E[:, 4:5], func=AF.Exp, scale=-0.5),
                      reads=[er], writes=[er])
                P.add("dve", lambda e: e.scalar_tensor_tensor(
                    out=o_c[u2][:, :], in0=o_b[u2][:, :], scalar=E[:, 5:6], in1=cs(C_GREP, 128), op0=ALU.mult, op1=ALU.mult),
                    reads=[("ob", u2), er, "cst"], writes=[("oc", u2)])

            def stC():
                bt = bO[0]
                P.add("pe", lambda e: e.transpose(out=psb[bt][:, 0:128], in_=o_c[u2][:, :], identity=ident_b[:, :]),
                      reads=[("oc", u2), "ident"], writes=[("ps", bt)])
                P.add("act", lambda e: e.activation(
                    out=mixA[:, h, i * 128:(i + 1) * 128], in_=psb[bt][:, 0:128], func=AF.Copy),
                    reads=[("ps", bt)], writes=[("mixA", h, i // 4)])
            return [stA, stB, stC]

        glist = []
        unit = 0
        for h in range(8):
            for i in range(8):
                u2 = unit % 2
                bO = (0, 1) if u2 == 0 else (2, 3)
                gs = [([0, 1, 2, 3], True), ([4, 5, 6, 7], True)]
                ownk = list(range(8, 8 + i))
                while ownk:
                    gs.append((ownk[:4], False))
                    ownk = ownk[4:]
                ug = []
                for gi, (kbs, isprev) in enumerate(gs):
                    ug.append(dict(h=h, i=i, u2=u2, bO=bO, kbs=kbs, prev=isprev, diag=False, first=(gi == 0), last=False, gi=gi))
                ug.append(dict(h=h, i=i, u2=u2, bO=bO, kbs=[8 + i], prev=False, diag=True, first=False, last=True, gi=len(gs)))
                glist.extend(ug)
                unit += 1

        pend = {}

        def start_group(g):
            g["bS"] = (nextbank(4, 8), nextbank(4, 8))
            emit_qk(g)

        start_group(glist[0])
        for n, g in enumerate(glist):
            if n + 1 < len(glist):
                start_group(glist[n + 1])
            emit_exp(g)
            emit_av(g)
            for stg in pend.pop(g["gi"], []):
                stg()
            if g["last"]:
                for k_ in sorted(pend):
                    for stg in pend[k_]:
                        stg()
                pend.clear()
                sA, sB, sC = epilogue_stages(g["h"], g["i"], g["u2"], g["bO"])
                sA()
                pend = {0: [sB], 1: [sC]}
        for k_ in sorted(pend):
            for stg in pend[k_]:
                stg()

        ck('att')
        P.fence()

        for c in range(16):
            s = load_w("a", w_o[c, :, :, :])
            r = c % 2
            for th in range(2):
                P.add("sp", lambda e, c=c, r=r, th=th: e.dma_start(
                    out=xs[r][:, th * 512:(th + 1) * 512], in_=xT_own[:, c, th * 512:(th + 1) * 512]),
                    writes=[("xs", r, th)], dma=("xs", r, th))
            for th in range(2):
                b = nextbank()
                for fc in range(16):
                    rhs = (mixP if fc < 8 else mixA)
                    P.add("pe", lambda e, fc=fc, s=s, th=th, b=b, rhs=rhs: e.matmul(
                        ps[b][:, :], lhsT=wr_a[s][:, fc, :], rhs=rhs[:, fc % 8, th * 512:(th + 1) * 512],
                        start=(fc == 0), stop=(fc == 15)),
                        reads=[("w", s), (("mixP", fc, th) if fc < 8 else ("mixA", fc - 8, th))], writes=[("ps", b)])
                P.add("dve", lambda e, c=c, th=th, b=b, r=r: e.tensor_tensor(
                    out=X1[:, c, th * 512:(th + 1) * 512], in0=ps[b][:, :], in1=xs[r][:, th * 512:(th + 1) * 512], op=ALU.add),
                    reads=[("ps", b), ("xs", r, th)], writes=[("X1", c)])

        ck('wout')
        P.fence()

        def get_x1(dc, th):
            return X1[:, dc, th * 512:(th + 1) * 512], [("X1", dc)]
        norm_stats(get_x1, sqbC, rstdC, "n2")
        for dc in range(NDC):
            P.add("dve", lambda e, dc=dc: e.scalar_tensor_tensor(
                out=h2T[:, dc, :], in0=X1[:, dc, :], scalar=cs(C_G2 + dc), in1=rstdC[:, :], op0=ALU.mult, op1=ALU.mult),
                reads=[("X1", dc), "n2rstdF", "cst"], writes=["h2T"])

        ck('n2')
        sgc = [0]
        for G in range(NG):
            hb = hTg[G % 2]
            for jj in range(JG):
                j = G * JG + jj
                s = load_w("gu", w_gu[j, :, :, :, :])
                bks = {}
                for gu in range(2):
                    for th in range(2):
                        b = nextbank()
                        bks[(gu, th)] = b
                        for dc in range(NDC):
                            P.add("pe", lambda e, dc=dc, s=s, gu=gu, th=th, b=b: e.matmul(
                                ps[b][:, :], lhsT=wr_gu[s][:, gu, dc, :], rhs=h2T[:, dc, th * 512:(th + 1) * 512],
                                start=(dc == 0), stop=(dc == NDC - 1)),
                                reads=[("w", s), "h2T"], writes=[("ps", b)])
                for th in range(2):
                    r = sgc[0] % 2
                    sgc[0] += 1
                    bg, bu = bks[(0, th)], bks[(1, th)]
                    P.add("act", lambda e, r=r, bg=bg: e.activation(out=sg[r][:, :], in_=ps[bg][:, :], func=AF.Silu),
                          reads=[("ps", bg)], writes=[("sg", r)])
                    P.add("dve", lambda e, r=r, bu=bu, jj=jj, th=th, hb=hb: e.tensor_tensor(
                        out=hb[:, jj, th * 512:(th + 1) * 512], in0=sg[r][:, :], in1=ps[bu][:, :], op=ALU.mult),
                        reads=[("sg", r), ("ps", bu)], writes=[("hTg", G % 2, th)])
            for cp in range(8):
                s = load_w("dn", w_dn[G, cp, :, :, :])
                for cc in range(2):
                    c = cp * 2 + cc
                    for th in range(2):
                        b = nextbank()
                        for jj in range(JG):
                            P.add("pe", lambda e, jj=jj, s=s, cc=cc, th=th, b=b, hb=hb: e.matmul(
                                ps[b][:, :], lhsT=wr_dn[s][:, jj, cc * 128:(cc + 1) * 128], rhs=hb[:, jj, th * 512:(th + 1) * 512],
                                start=(jj == 0), stop=(jj == JG - 1)),
                                reads=[("w", s), ("hTg", G % 2, th)], writes=[("ps", b)])
                        P.add("dve", lambda e, c=c, th=th, b=b: e.tensor_tensor(
                            out=X1[:, c, th * 512:(th + 1) * 512], in0=X1[:, c, th * 512:(th + 1) * 512], in1=ps[b][:, :], op=ALU.add),
                            reads=[("ps", b), ("X1", c)], writes=[("X1", c)])

        ck('ffn')
        norm_stats(get_x1, sqbC, rstdC, "n3")
        for dc in range(NDC):
            P.add("dve", lambda e, dc=dc: e.scalar_tensor_tensor(
                out=X1[:, dc, :], in0=X1[:, dc, :], scalar=cs(C_G3 + dc), in1=rstdC[:, :], op0=ALU.mult, op1=ALU.mult),
                reads=[("X1", dc), "n3rstdF", "cst"], writes=[("X1", dc)])
            P.add("sp", lambda e, dc=dc: e.dma_start(out=outT[:, dc, :], in_=X1[:, dc, :]),
                  reads=[("X1", dc)], writes=[("out", dc)], dma="out")
        P.add("sp", None, reads=[("out", dc) for dc in range(NDC)])

    try:
        body()
    except _Stop:
        dump(stop)

    with contextlib.ExitStack() as stack:
        P.finalize(nc, stack)
        with nc.Block() as block:
            @block.tensor
            def _(e):
                P.emit("pe", e)

            @block.vector
            def _(e):
                P.emit("dve", e)

            @block.scalar
            def _(e):
                P.emit("act", e)

            @block.gpsimd
            def _(e):
                P.emit("pool", e)

            @block.sync
            def _(e):
                P.emit("sp", e)
    return nc


def _fm(a):
    t = a.shape[0]
    return np.ascontiguousarray(a.T.reshape(NDC, 128, t).transpose(1, 0, 2))


def prepare_inputs(x, norm_mix, w_in, w_pool, pool_scale, lambda_q1, lambda_k1, lambda_q2, lambda_k2,
                   subln_gain, w_out, norm_ffn, w_gate_up, w_down, norm_final):
    f = np.float32
    x = np.asarray(x, f)
    w_in = np.asarray(w_in, f)[0]
    w_pool = np.asarray(w_pool, f)[0]
    w_out = np.asarray(w_out, f)[0]
    w_gate_up = np.asarray(w_gate_up, f)[0]
    w_down = np.asarray(w_down, f)[0]

    def colchunks(w, ncol):
        C = w.shape[1]
        return np.ascontiguousarray(w.reshape(NDC, 128, C // ncol, ncol).transpose(2, 1, 0, 3))

    w_uqk = colchunks(w_in[:, :3072], 128)
    w_v = colchunks(w_in[:, 3072:], 256)
    w_pool_r = np.ascontiguousarray(w_pool.reshape(4, 2, 128, 256).transpose(2, 0, 1, 3))
    w_o = colchunks(w_out, 128)
    w_gu = np.ascontiguousarray(w_gate_up.reshape(NDC, 128, 2, NJ, 128).transpose(3, 1, 2, 0, 4))
    w_dn = np.ascontiguousarray(w_down.reshape(NG, JG, 128, 8, 256).transpose(0, 3, 2, 1, 4))

    inv_freq = (np.float32(10000.0) ** (-(np.arange(0, 64, 2, dtype=f) / f(64)))).astype(f)
    p_idx = np.arange(128)
    fr = inv_freq[p_idx % 32]
    sign = np.where((p_idx % 64) < 32, -1.0, 1.0).astype(f)
    partner = np.where((p_idx % 64) < 32, p_idx + 32, p_idx - 32)
    Rm = np.zeros((128, 128), f)
    Rm[partner, p_idx] = 1.0

    base = np.zeros((128, C_N), f)
    base[:, C_G1:C_G1 + 16] = np.asarray(norm_mix, f)[0].reshape(16, 128).T
    base[:, C_G2:C_G2 + 16] = np.asarray(norm_ffn, f)[0].reshape(16, 128).T
    base[:, C_G3:C_G3 + 16] = np.asarray(norm_final, f).reshape(16, 128).T
    base[:, C_PS:C_PS + 8] = np.asarray(pool_scale, f)[0].reshape(8, 128).T
    base[:, C_LQ1:C_LQ1 + 64] = np.asarray(lambda_q1, f)[0][None, :]
    base[:, C_LK1:C_LK1 + 64] = np.asarray(lambda_k1, f)[0][None, :]
    base[:, C_LQ2:C_LQ2 + 64] = np.asarray(lambda_q2, f)[0][None, :]
    base[:, C_LK2:C_LK2 + 64] = np.asarray(lambda_k2, f)[0][None, :]
    base[:, C_GREP:C_GREP + 128] = np.asarray(subln_gain, f)[0][None, :]
    base[:, C_ONES:C_ONES + 128] = 1.0
    base[:, C_ID:C_ID + 128] = np.eye(128, dtype=f)
    base[:, C_RM:C_RM + 128] = Rm

    shared = dict(w_uqk=w_uqk, w_v=w_v, w_pool=w_pool_r, w_o=w_o, w_gu=w_gu, w_dn=w_dn)
    in_maps = []
    for core in range(8):
        b, half = core // 2, core % 2
        own0 = half * S_OWN
        xo = _fm(x[b, own0:own0 + S_OWN])
        if half == 1:
            xp = _fm(x[b, 0:S_OWN])
        else:
            xp = np.zeros_like(xo)
        rot = np.zeros((128, 2, 2, 1024), f)
        for ip in range(2):
            pos = (own0 - S_OWN + ip * S_OWN + np.arange(S_OWN)).astype(f)
            ang = (pos[None, :] * fr[:, None]).astype(f)
            rot[:, ip, 0, :] = np.cos(ang)
            rot[:, ip, 1, :] = np.sin(ang) * sign[:, None]
        c = base.copy()
        c[:, C_PB] = 0.0 if half == 1 else NEG_BIG
        for g, w in enumerate((2, 4, 8, 16)):
            posn = own0 + np.arange(16)
            c[:, C_IC + 16 * g:C_IC + 16 * (g + 1)] = (1.0 / np.minimum(posn + 1, w).astype(f))[None, :]
        m = dict(shared)
        m.update(xT_own=xo, xT_prev=xp, rot=rot, cst=c)
        in_maps.append(m)
    return in_maps


_NC_CACHE = {}


def kernel(x, norm_mix, w_in, w_pool, pool_scale, lambda_q1, lambda_k1, lambda_q2, lambda_k2,
           subln_gain, w_out, norm_ffn, w_gate_up, w_down, norm_final):
    in_maps = prepare_inputs(x, norm_mix, w_in, w_pool, pool_scale, lambda_q1, lambda_k1, lambda_q2, lambda_k2,
                             subln_gain, w_out, norm_ffn, w_gate_up, w_down, norm_final)
    if "nc" not in _NC_CACHE:
        _NC_CACHE["nc"] = build_program()
    nc = _NC_CACHE["nc"]
    res = run_bass_kernel_spmd(nc, in_maps, core_ids=list(range(8)))
    out = np.empty((4, 2048, D), np.float32)
    for core in range(8):
        b, half = core // 2, core % 2
        o = np.asarray(res.results[core]["outT"], np.float32)
        out[b, half * S_OWN:(half + 1) * S_OWN, :] = o.transpose(2, 1, 0).reshape(S_OWN, D)
    return out
```

```python
import contextlib
import math

import numpy as np
import concourse.bass as bass
import concourse.mybir as mybir
from concourse.bass_utils import run_bass_kernel_spmd

F32 = mybir.dt.float32
BF16 = mybir.dt.bfloat16
AF = mybir.ActivationFunctionType
ALU = mybir.AluOpType
AX = mybir.AxisListType

D = 2048
NDC = 16
S_OWN = 1024
FFN = 5632
NJ = 44
NG = 4
JG = 11
EPS = 1e-6
LAM_INIT = 0.8 - 0.6 * math.exp(0.0)
NEG_BIG = -30000.0

C_G1, C_G2, C_G3 = 0, 16, 32
C_PS = 48
C_PB = 56
C_IC = 57
C_LQ1, C_LK1, C_LQ2, C_LK2 = 121, 185, 249, 313
C_GREP = 377
C_ONES = 505
C_ID = 633
C_RM = 761
C_N = 896

NS = 4


class Op:
    __slots__ = ("id", "eng", "idx", "fn", "deps", "dma", "signal", "cum", "ordinal", "ndma")


class Prog:
    ENGS = ("pe", "act", "dve", "pool", "sp")

    def __init__(self):
        self.ops = []
        self.eng_ops = {e: [] for e in self.ENGS}
        self.last_w = {}
        self.readers = {}
        self.pending_fence = {}
        self.last_dma = {}
        self.aggregate = set()

    def add(self, eng, fn, reads=(), writes=(), dma=None, ndma=1):
        op = Op()
        op.ndma = ndma
        op.id = len(self.ops)
        op.eng = eng
        op.fn = fn
        op.dma = dma
        op.signal = False
        writes = list(writes) + [r for r in reads if isinstance(r, tuple) and r[0] == "ps" and r not in writes]
        deps = set()
        for r in reads:
            lw = self.last_w.get(r)
            if lw is not None:
                deps.add(lw)
        for w in writes:
            lw = self.last_w.get(w)
            if lw is not None:
                deps.add(lw)
            rs = self.readers.get(w)
            if rs:
                deps.update(rs)
        for r in reads:
            self.readers.setdefault(r, []).append(op.id)
        for w in writes:
            self.last_w[w] = op.id
            self.readers[w] = []
        pf = self.pending_fence.pop(eng, None)
        if pf:
            deps.update(pf)
        deps.discard(op.id)
        if eng == "pe" and dma is None:
            deps = {d for d in deps if not (self.ops[d].eng == "pe" and self.ops[d].dma is None)}
        op.deps = deps
        op.idx = len(self.eng_ops[eng])
        self.eng_ops[eng].append(op)
        self.ops.append(op)
        if dma is not None:
            self.last_dma[dma] = op.id
        return op.id

    def fence(self):
        deps = set()
        for e in self.ENGS:
            lst = self.eng_ops[e]
            for o in reversed(lst):
                if o.dma is None and o.fn is not None:
                    deps.add(o.id)
                    break
        deps.update(self.last_dma.values())
        for e in self.ENGS:
            self.pending_fence.setdefault(e, set()).update(deps)

    def finalize(self, nc, stack):
        for op in self.ops:
            for d in op.deps:
                t = self.ops[d]
                if t.dma is None:
                    t.signal = True
        self.eng_sem = {}
        for e in self.ENGS:
            self.eng_sem[e] = stack.enter_context(nc.semaphore("s_" + e))
            c = 0
            for o in self.eng_ops[e]:
                if o.dma is None and o.signal:
                    c += 1
                o.cum = c
        self.dma_sem = {}
        counts = {}
        for op in self.ops:
            if op.dma is not None:
                k = op.dma
                if k not in self.dma_sem:
                    nm = "d_" + "_".join(str(x) for x in (k if isinstance(k, tuple) else (k,)))
                    self.dma_sem[k] = stack.enter_context(nc.semaphore(nm))
                counts[k] = counts.get(k, 0) + op.ndma
                op.ordinal = counts[k]
        self.dma_total = counts

    def wait_of(self, d):
        t = self.ops[d]
        if t.dma is None:
            return (self.eng_sem[t.eng], t.cum)
        k = t.dma
        n = self.dma_total[k] if k in self.aggregate else t.ordinal
        return (self.dma_sem[k], 16 * n)

    def emit(self, eng, e):
        known = {}
        for op in self.eng_ops[eng]:
            ws = {}
            for d in op.deps:
                sem, val = self.wait_of(d)
                key = id(sem)
                if ws.get(key, (None, 0))[1] < val:
                    ws[key] = (sem, val)
            for key, (sem, val) in ws.items():
                if known.get(key, 0) < val:
                    e.wait_ge(sem, val)
                    known[key] = val
            if op.fn is None:
                continue
            ins = op.fn(e)
            if op.dma is not None:
                for one in (ins if isinstance(ins, list) else [ins]):
                    one.then_inc(self.dma_sem[op.dma], 16)
            elif op.signal:
                ins.then_inc(self.eng_sem[eng], 1)


class _Stop(Exception):
    pass


def build_program(stop=None):
    nc = bass.Bass("TRN2", target_bir_lowering=False)
    dt = nc.dram_tensor
    xT_own = dt("xT_own", [128, NDC, S_OWN], F32, kind="ExternalInput")
    xT_prev = dt("xT_prev", [128, NDC, S_OWN], F32, kind="ExternalInput")
    rot_d = dt("rot", [128, 2, 2, 1024], F32, kind="ExternalInput")
    cst_d = dt("cst", [128, C_N], F32, kind="ExternalInput")
    w_uqk = dt("w_uqk", [24, 128, 16, 128], F32, kind="ExternalInput")
    w_v = dt("w_v", [4, 128, 16, 256], F32, kind="ExternalInput")
    w_pool = dt("w_pool", [128, 4, 2, 256], F32, kind="ExternalInput")
    w_o = dt("w_o", [16, 128, 16, 128], F32, kind="ExternalInput")
    early = stop in ("setup", "pk_load", "pk_mm", "pk_rot", "n1prev", "projprev", "projown", "att", "wout", "n2")
    w_gu = None if early else dt("w_gu", [NJ, 128, 2, 16, 128], F32, kind="ExternalInput")
    w_dn = None if early else dt("w_dn", [NG, 8, 128, JG, 256], F32, kind="ExternalInput")
    outT = dt("outT", [128, NDC, S_OWN], F32, kind="ExternalOutput")

    BASE = 16640
    cnt = [0]

    def sb(name, shape, dtype, off):
        cnt[0] += 1
        return nc.alloc_sbuf_tensor_at(f"{name}_{cnt[0]}", list(shape), dtype, offset=BASE + off)

    cst = sb("cst", [128, C_N], F32, 0)
    ones_b = sb("ones", [128, 128], BF16, 3584)
    ident_b = sb("ident", [128, 128], BF16, 3840)
    Rm_b = sb("rm", [128, 128], BF16, 4096)
    small = sb("small", [128, 256], F32, 4352)
    wp = sb("wp", [128, 4, 2, 256], BF16, 5376)
    W0 = 9472
    wr_a = [sb("wra", [128, 16, 128], BF16, W0 + s * 8192) for s in range(NS)]
    wr_v = [sb("wrv", [128, 16, 256], BF16, W0 + s * 8192) for s in range(NS)]
    wr_gu = [sb("wrgu", [128, 2, 16, 128], BF16, W0 + s * 8192) for s in range(NS)]
    wr_dn = [sb("wrdn", [128, JG, 256], BF16, W0 + s * 8192) for s in range(NS)]
    RH = 42240
    hT = sb("hT", [128, 16, 1024], BF16, RH)
    mixA = sb("mixA", [128, 8, 1024], BF16, RH)
    ATS = RH + 16384
    PT = [sb("PT", [128, 512], BF16, ATS + r * 1024) for r in range(8)]
    PTd = [[sb("PTd", [128, 128], BF16, ATS + 8192 + (r * 2 + c) * 256) for c in range(2)] for r in range(2)]
    o_a = [sb("oa", [128, 128], F32, ATS + 9216 + r * 512) for r in range(2)]
    o_b = [sb("ob", [128, 128], F32, ATS + 10240 + r * 512) for r in range(2)]
    o_c = [sb("oc", [128, 128], BF16, ATS + 11264 + r * 256) for r in range(2)]
    esm = [sb("esm", [128, 16], F32, ATS + 11776 + r * 64) for r in range(2)]
    h2T = sb("h2T", [128, 16, 1024], BF16, RH)
    RK = 75008
    KT = sb("KT", [128, 8, 2048], BF16, RK)
    QT = sb("QT", [128, 8, 1024], BF16, RK + 32768)
    Vaug = sb("Vaug", [128, 16, 8, 130], BF16, RK + 49152)
    X1 = sb("X1", [128, 16, 1024], F32, RK)
    CS = RK + 65536
    sqbC = [sb("sqbC", [128, 1024], BF16, CS + r * 2048) for r in range(2)]
    rstdC = sb("rstdC", [128, 1024], F32, CS + 4096)
    sg = [sb("sg", [128, 512], F32, CS + 8192 + r * 2048) for r in range(2)]
    MP = 157440
    mixP = sb("mixP", [128, 8, 1024], BF16, MP)
    hTg = [sb("hTg", [128, JG, 1024], BF16, MP + r * 22528) for r in range(2)]
    RT = 173824
    rot = sb("rot", [128, 2, 1024], F32, RT)
    hThalo = sb("hThalo", [128, 16, 16], BF16, 182016)
    SCR = 182528
    xs = [sb("xs", [128, 1024], F32, SCR + r * 4096) for r in range(2)]
    sqb = [sb("sqb", [128, 1024], BF16, SCR + 8192 + r * 2048) for r in range(2)]
    rstd = sb("rstd", [128, 1024], F32, SCR + 12288)
    ubuf = sb("ubuf", [128, 1040], F32, SCR)
    pa = [sb("pa", [128, 1040], F32, SCR + 4160 * (r + 1)) for r in range(2)]
    pooled = [sb("pooled", [128, 1024], BF16, SCR + 12480 + r * 2048) for r in range(2)]
    tmp16 = sb("tmp16", [128, 16], F32, SCR + 16576)
    RS = 199168
    qb = [sb("qb", [128, 512], BF16, RS + r * 1024) for r in range(2)]
    t1 = [sb("t1", [128, 512], F32, RS + 2048 + r * 2048) for r in range(2)]
    t2 = [sb("t2", [128, 512], F32, RS + 6144 + r * 2048) for r in range(2)]
    assert BASE + RS + 10240 <= 229344

    ps = [nc.alloc_psum_tensor(f"ps{b}", [128, 512], F32) for b in range(8)]
    psb = [p.bitcast(BF16) for p in ps]

    P = Prog()
    P.aggregate.update(["const", "out"])
    bank = [0]

    def nextbank(lo=0, hi=8):
        b = lo + (bank[0] % (hi - lo))
        bank[0] += 1
        return b

    wslot = [0]

    def load_w(kind, src):
        s = wslot[0] % NS
        wslot[0] += 1
        if kind == "a":
            P.add("pool", lambda e, s=s, src=src: e.dma_start(out=wr_a[s][:, :, :], in_=src, max_dma_last_dim=4096),
                  writes=[("w", s)], dma=("w", s))
        elif kind == "v":
            P.add("pool", lambda e, s=s, src=src: e.dma_start(out=wr_v[s][:, :, :], in_=src, max_dma_last_dim=4096),
                  writes=[("w", s)], dma=("w", s))
        elif kind == "gu":
            P.add("pool", lambda e, s=s, src=src: [
                e.dma_start(out=wr_gu[s][:, 0, :, :], in_=src[:, 0, :, :], max_dma_last_dim=4096),
                e.dma_start(out=wr_gu[s][:, 1, :, :], in_=src[:, 1, :, :], max_dma_last_dim=4096)],
                writes=[("w", s)], dma=("w", s), ndma=2)
        elif kind == "dn":
            P.add("pool", lambda e, s=s, src=src: e.dma_start(out=wr_dn[s][:, :, :], in_=src, max_dma_last_dim=4096),
                  writes=[("w", s)], dma=("w", s))
        return s

    def cs(c0, n=1):
        return cst[:, c0:c0 + n]

    def ck(name):
        if stop == name:
            raise _Stop()

    def dump(name):
        srcs = {
            "setup": [(small[:, :], outT[:, 0, 0:256])],
            "pk_load": [(wr_a[0][:, dc, :], outT[:, dc, 0:128]) for dc in range(16)],
            "pk_mm": [(hT[:, 0, :], outT[:, 0, :])],
            "pk_rot": [(KT[:, 0, 0:1024], outT[:, 0, :])],
            "n1prev": [(rstd[:, :], outT[:, 0, :])] + [(hT[:, dc, :], outT[:, 1 + dc % 15, :]) for dc in (0, 5)],
            "projprev": [(KT[:, h, 0:1024], outT[:, h, :]) for h in range(8)] +
                        [(Vaug[:, tb, :, 0:128], outT[:, 8 + tb, :]) for tb in range(8)],
            "projown": [(KT[:, 0, 1024:2048], outT[:, 0, :]), (KT[:, 7, 0:1024], outT[:, 3, :])] +
                       [(QT[:, h, :], outT[:, 1 + h, :]) for h in (0, 1)] +
                       [(mixP[:, c, :], outT[:, 4 + c, :]) for c in range(8)] +
                       [(Vaug[:, tb, :, 0:128], outT[:, 12 + i_, :]) for i_, tb in enumerate((8, 9, 3, 15))],
            "att": [(mixA[:, h, :], outT[:, h, :]) for h in range(8)] + [(mixP[:, c, :], outT[:, 8 + c, :]) for c in range(8)],
            "wout": [(X1[:, c, :], outT[:, c, :]) for c in range(16)],
            "n2": [(X1[:, c, :], outT[:, c, :]) for c in range(8)] + [(h2T[:, c, :], outT[:, 8 + c, :]) for c in range(8)],
            "ffn": [(X1[:, c, :], outT[:, c, :]) for c in range(16)],
        }[name]
        P.fence()
        for i, (src, dst) in enumerate(srcs):
            P.add("pool", lambda e, src=src, dst=dst: e.dma_start(out=dst, in_=src, max_dma_last_dim=2048), writes=[("out", i)], dma="out")
        P.add("sp", None, reads=[("out", i) for i in range(len(srcs))])

    def body():
        P.add("sp", lambda e: e.dma_start(out=cst[:, :], in_=cst_d[:, :]), writes=["cst"], dma="const")
        P.add("pool", lambda e: e.dma_start(out=wp[:, :, :, :], in_=w_pool[:, :, :, :], max_dma_last_dim=4096),
              writes=["wp"], dma="wp")
        P.add("dve", lambda e: e.tensor_copy(out=ones_b[:, :], in_=cs(C_ONES, 128)), reads=["cst"], writes=["ones"])
        P.add("dve", lambda e: e.tensor_copy(out=ident_b[:, :], in_=cs(C_ID, 128)), reads=["cst"], writes=["ident"])
        P.add("dve", lambda e: e.tensor_copy(out=Rm_b[:, :], in_=cs(C_RM, 128)), reads=["cst"], writes=["rm"])
        P.add("dve", lambda e: e.tensor_tensor(out=small[:, 0:64], in0=cs(C_LQ1, 64), in1=cs(C_LK1, 64), op=ALU.mult),
              reads=["cst"], writes=["sm0"])
        P.add("dve", lambda e: e.reduce_sum(out=small[:, 128:129], in_=small[:, 0:64], axis=AX.X), reads=["sm0"], writes=["sm1"])
        P.add("dve", lambda e: e.tensor_tensor(out=small[:, 64:128], in0=cs(C_LQ2, 64), in1=cs(C_LK2, 64), op=ALU.mult),
              reads=["cst"], writes=["sm2"])
        P.add("dve", lambda e: e.reduce_sum(out=small[:, 129:130], in_=small[:, 64:128], axis=AX.X), reads=["sm2"], writes=["sm3"])
        P.add("act", lambda e: e.activation(out=small[:, 130:132], in_=small[:, 128:130], func=AF.Exp),
              reads=["sm1", "sm3"], writes=["sm4"])
        P.add("dve", lambda e: e.tensor_tensor(out=small[:, 132:133], in0=small[:, 131:132], in1=small[:, 130:131], op=ALU.subtract),
              reads=["sm4"], writes=["sm5"])
        P.add("dve", lambda e: e.tensor_scalar_add(out=small[:, 133:134], in0=small[:, 132:133], scalar1=-LAM_INIT),
              reads=["sm5"], writes=["neglam"])
        neglam = small[:, 133:134]
        P.add("dve", lambda e: e.tensor_scalar_mul(out=cst[:, C_GREP:C_GREP + 128], in0=cst[:, C_GREP:C_GREP + 128],
                                                   scalar1=(1.0 - LAM_INIT)), reads=["cst"], writes=["cst"])

        ck('setup')
        def norm_stats(get_src, sq_bufs, rstd_buf, tag):
            b0, b1 = nextbank(), nextbank()
            for dc in range(NDC):
                r = dc % 2
                for th, b in ((0, b0), (1, b1)):
                    ap, res = get_src(dc, th)
                    P.add("act", lambda e, ap=ap, r=r, th=th: e.activation(
                        out=sq_bufs[r][:, th * 512:(th + 1) * 512], in_=ap, func=AF.Square),
                        reads=res, writes=[(tag + "sq", r, th)])
                    P.add("pe", lambda e, r=r, th=th, b=b, dc=dc: e.matmul(
                        ps[b][:, :], lhsT=ones_b[:, :], rhs=sq_bufs[r][:, th * 512:(th + 1) * 512],
                        start=(dc == 0), stop=(dc == NDC - 1)),
                        reads=[(tag + "sq", r, th), "ones"], writes=[("ps", b)])
            for th, b in ((0, b0), (1, b1)):
                P.add("dve", lambda e, th=th, b=b: e.tensor_scalar(
                    out=rstd_buf[:, th * 512:(th + 1) * 512], in0=ps[b][:, :], scalar1=1.0 / D, scalar2=EPS,
                    op0=ALU.mult, op1=ALU.add), reads=[("ps", b)], writes=[(tag + "rstd", th)])
            P.add("act", lambda e: e.activation(out=rstd_buf[:, :], in_=rstd_buf[:, :], func=AF.Sqrt),
                  reads=[(tag + "rstd", 0), (tag + "rstd", 1)], writes=[tag + "rstd2"])
            P.add("dve", lambda e: e.reciprocal(out=rstd_buf[:, :], in_=rstd_buf[:, :]),
                  reads=[tag + "rstd2"], writes=[tag + "rstdF"])

        rot_pending = []
        wp_pending = []

        def flush_wp():
            while wp_pending:
                wp_pending.pop(0)()

        def flush_rot():
            while rot_pending:
                rot_pending.pop(0)()

        rotc = [0]

        def rotary_tile(b, dest_ap, dest_res, th):
            r = rotc[0] % 2
            rotc[0] += 1
            P.add("act", lambda e, b=b, r=r: e.activation(out=qb[r][:, :], in_=ps[b][:, :], func=AF.Copy),
                  reads=[("ps", b)], writes=[("qb", r)])

            def rest(b=b, r=r, dest_ap=dest_ap, dest_res=dest_res, th=th):
                b2 = nextbank()
                P.add("pe", lambda e: e.matmul(ps[b2][:, :], lhsT=Rm_b[:, :], rhs=qb[r][:, :], start=True, stop=True),
                      reads=[("qb", r), "rm"], writes=[("ps", b2)])
                P.add("dve", lambda e: e.tensor_tensor(out=t1[r][:, :], in0=ps[b][:, :], in1=rot[:, 0, th * 512:(th + 1) * 512], op=ALU.mult),
                      reads=[("ps", b), "rot"], writes=[("t1", r)])
                P.add("dve", lambda e: e.tensor_tensor(out=t2[r][:, :], in0=ps[b2][:, :], in1=rot[:, 1, th * 512:(th + 1) * 512], op=ALU.mult),
                      reads=[("ps", b2), "rot"], writes=[("t2", r)])
                P.add("dve", lambda e: e.tensor_tensor(out=dest_ap, in0=t1[r][:, :], in1=t2[r][:, :], op=ALU.add),
                      reads=[("t1", r), ("t2", r)], writes=[dest_res])
            rot_pending.append(rest)

        def proj_fm_matmuls(s, th, b):
            for dc in range(NDC):
                P.add("pe", lambda e, dc=dc: e.matmul(ps[b][:, :], lhsT=wr_a[s][:, dc, :], rhs=hT[:, dc, th * 512:(th + 1) * 512],
                                                       start=(dc == 0), stop=(dc == NDC - 1)),
                      reads=[("w", s), "hT"], writes=[("ps", b)])

        for ipass, (xsrc, tokoff, tboff) in enumerate(((xT_prev, 0, 0), (xT_own, 1024, 8))):
            own = ipass == 1
            P.add("sp", lambda e, ipass=ipass: e.dma_start(out=rot[:, :, :], in_=rot_d[:, ipass, :, :]), writes=["rot"], dma="rot")

            def get_src(dc, th, xsrc=xsrc):
                r = dc % 2
                P.add("sp", lambda e, dc=dc, r=r, th=th: e.dma_start(
                    out=xs[r][:, th * 512:(th + 1) * 512], in_=xsrc[:, dc, th * 512:(th + 1) * 512]),
                    writes=[("xs", r, th)], dma=("xs", r, th))
                return xs[r][:, th * 512:(th + 1) * 512], [("xs", r, th)]
            norm_stats(get_src, sqb, rstd, "n1")
            for dc in range(NDC):
                r = dc % 2
                for th in range(2):
                    P.add("sp", lambda e, dc=dc, r=r, th=th, xsrc=xsrc: e.dma_start(
                        out=xs[r][:, th * 512:(th + 1) * 512], in_=xsrc[:, dc, th * 512:(th + 1) * 512]),
                        writes=[("xs", r, th)], dma=("xs", r, th))
                    P.add("dve", lambda e, dc=dc, r=r, th=th: e.scalar_tensor_tensor(
                        out=hT[:, dc, th * 512:(th + 1) * 512], in0=xs[r][:, th * 512:(th + 1) * 512], scalar=cs(C_G1 + dc),
                        in1=rstd[:, th * 512:(th + 1) * 512], op0=ALU.mult, op1=ALU.mult),
                        reads=[("xs", r, th), "n1rstdF", "cst"], writes=["hT"])
            if not own:
                P.add("dve", lambda e: e.tensor_copy(out=hThalo[:, :, :], in_=hT[:, :, 1008:1024]), reads=["hT"], writes=["hThalo"])
                ck('n1prev')
            else:
                P.fence()

            if own:
                for n in range(8):
                    g = n // 2
                    a_ = n % 2
                    s = load_w("a", w_uqk[n, :, :, :])
                    bh, b0, b1 = nextbank(), nextbank(), nextbank()
                    for dc in range(NDC):
                        P.add("pe", lambda e, dc=dc, s=s, bh=bh: e.matmul(ps[bh][:, 0:16], lhsT=wr_a[s][:, dc, :], rhs=hThalo[:, dc, :],
                                                                          start=(dc == 0), stop=(dc == NDC - 1)),
                              reads=[("w", s), "hThalo"], writes=[("ps", bh)])
                    proj_fm_matmuls(s, 0, b0)
                    proj_fm_matmuls(s, 1, b1)
                    flush_wp()
                    P.add("act", lambda e, bh=bh: e.activation(out=ubuf[:, 0:16], in_=ps[bh][:, 0:16], func=AF.Copy),
                          reads=[("ps", bh)], writes=["ubuf"])
                    P.add("act", lambda e, b0=b0: e.activation(out=ubuf[:, 16:528], in_=ps[b0][:, :], func=AF.Copy),
                          reads=[("ps", b0), "ubuf"], writes=["ubuf"])
                    P.add("act", lambda e, b1=b1: e.activation(out=ubuf[:, 528:1040], in_=ps[b1][:, :], func=AF.Copy),
                          reads=[("ps", b1), "ubuf"], writes=["ubuf"])
                    m = g + 1
                    src = ubuf
                    src_res = "ubuf"
                    for st in range(1, m + 1):
                        sh = 2 ** (st - 1)
                        S0 = 2 ** st - 1
                        dst = pa[(st - 1) % 2]
                        dres = ("pa", (st - 1) % 2)
                        P.add("dve", lambda e, src=src, dst=dst, sh=sh, S0=S0: e.tensor_tensor(
                            out=dst[:, S0:1040], in0=src[:, S0:1040], in1=src[:, S0 - sh:1040 - sh], op=ALU.add),
                            reads=[src_res], writes=[dres])
                        src, src_res = dst, dres
                    w_ = 2 ** m
                    P.add("dve", lambda e, src=src, a_=a_, w_=w_: e.scalar_tensor_tensor(
                        out=pooled[a_][:, :], in0=src[:, 16:1040], scalar=1.0 / w_, in1=ubuf[:, 16:1040],
                        op0=ALU.mult, op1=ALU.subtract), reads=[src_res, "ubuf"], writes=[("pooled", a_)])
                    P.add("dve", lambda e, src=src, g=g: e.tensor_tensor(
                        out=tmp16[:, :], in0=src[:, 16:32], in1=cs(C_IC + 16 * g, 16), op=ALU.mult),
                        reads=[src_res, "cst"], writes=["tmp16"])
                    P.add("dve", lambda e, a_=a_: e.tensor_tensor(
                        out=pooled[a_][:, 0:16], in0=tmp16[:, :], in1=ubuf[:, 16:32], op=ALU.subtract),
                        reads=["tmp16", "ubuf", ("pooled", a_)], writes=[("pooled", a_)])
                    if a_ == 1:
                        def wpool_job(g=g):
                            for o in range(2):
                                for th in range(2):
                                    b = nextbank()
                                    for a2 in range(2):
                                        P.add("pe", lambda e, g=g, a2=a2, o=o, th=th, b=b: e.matmul(
                                            ps[b][:, :], lhsT=wp[:, g, a2, o * 128:(o + 1) * 128],
                                            rhs=pooled[a2][:, th * 512:(th + 1) * 512], start=(a2 == 0), stop=(a2 == 1)),
                                            reads=["wp", ("pooled", a2)], writes=[("ps", b)])
                                    P.add("act", lambda e, g=g, o=o, th=th, b=b: e.activation(
                                        out=mixP[:, 2 * g + o, th * 512:(th + 1) * 512], in_=ps[b][:, :], func=AF.Identity,
                                        scale=cs(C_PS + 2 * g + o)), reads=[("ps", b), "cst"], writes=[("mixP", 2 * g + o, th)])
                        wp_pending.append(wpool_job)
                flush_wp()

            kinds = (["q"] if own else []) + ["k"]
            for kind in kinds:
                for hc in range(8):
                    n = (8 if kind == "q" else 16) + hc
                    s = load_w("a", w_uqk[n, :, :, :])
                    ck('pk_load')
                    for th in range(2):
                        b = nextbank()
                        proj_fm_matmuls(s, th, b)
                        ck('pk_mm')
                        flush_rot()
                        if th == 1:
                            ck('pk_rot')
                        if kind == "q":
                            dest = QT[:, hc, th * 512:(th + 1) * 512]
                            dres = ("QT", hc, th)
                        else:
                            dest = KT[:, hc, tokoff + th * 512: tokoff + (th + 1) * 512]
                            dres = ("KT", hc, ipass, th)
                        rotary_tile(b, dest, dres, th)
            flush_rot()

            if not own:
                P.add("dve", lambda e: e.memset(Vaug[:, :, :, 128:130], 1.0), writes=["Vones"])
            for vc in range(4):
                s = load_w("v", w_v[vc, :, :, :])
                for tbl in range(8):
                    b = nextbank()
                    for dc in range(NDC):
                        P.add("pe", lambda e, dc=dc, s=s, tbl=tbl, b=b: e.matmul(
                            ps[b][:, 0:256], lhsT=hT[:, dc, tbl * 128:(tbl + 1) * 128], rhs=wr_v[s][:, dc, :],
                            start=(dc == 0), stop=(dc == NDC - 1)), reads=[("w", s), "hT"], writes=[("ps", b)])
                    for hh in range(2):
                        P.add("act", lambda e, b=b, hh=hh, vc=vc, tbl=tbl, tboff=tboff: e.activation(
                            out=Vaug[:, tboff + tbl, 2 * vc + hh, 0:128], in_=ps[b][:, hh * 128:(hh + 1) * 128], func=AF.Copy),
                            reads=[("ps", b)], writes=[("V", tboff + tbl, 2 * vc + hh)])
            if not own:
                ck('projprev')

        ck('projown')
        P.fence()

        for r in range(2):
            for c in range(2):
                P.add("dve", lambda e, r=r, c=c: e.memset(PTd[r][c][:, :], 0.0), writes=[("PTd", r)])
        ptc = [0]
        NPT = len(PT)

        def emit_qk(g):
            h, i, kbs, bS = g["h"], g["i"], g["kbs"], g["bS"]
            for idx, kb in enumerate(kbs):
                for c in range(2):
                    P.add("pe", lambda e, c=c, idx=idx, kb=kb, h=h, i=i, bS=bS: e.matmul(
                        ps[bS[c]][:, idx * 128:(idx + 1) * 128],
                        lhsT=KT[c * 64:(c + 1) * 64, h, kb * 128:(kb + 1) * 128],
                        rhs=QT[c * 64:(c + 1) * 64, h, i * 128:(i + 1) * 128], start=True, stop=True),
                        reads=[("QT", h, i // 4), ("KT", h, kb // 8, (kb % 8) // 4)], writes=[("ps", bS[c])])

        def emit_exp(g):
            bS, kbs, u2 = g["bS"], g["kbs"], g["u2"]
            W = len(kbs) * 128
            if g["diag"]:
                for c in range(2):
                    P.add("act", lambda e, c=c, u2=u2, bS=bS: e.activation(
                        out=PTd[u2][c][0:64, 0:128], in_=ps[bS[c]][0:64, 0:128], func=AF.Exp, scale=0.125),
                        reads=[("ps", bS[c])], writes=[("PTd", u2)])
                    P.add("act", lambda e, c=c, u2=u2, bS=bS: e.activation(
                        out=PTd[u2][c][64:128, 64:128], in_=ps[bS[c]][64:128, 64:128], func=AF.Exp, scale=0.125),
                        reads=[("ps", bS[c]), ("PTd", u2)], writes=[("PTd", u2)])
                return
            prs = []
            for c in range(2):
                pr = ptc[0] % NPT
                ptc[0] += 1
                prs.append(pr)
                if g["prev"]:
                    P.add("act", lambda e, c=c, pr=pr, bS=bS, W=W: e.activation(
                        out=PT[pr][:, 0:W], in_=ps[bS[c]][:, 0:W], func=AF.Exp, bias=cs(C_PB), scale=0.125),
                        reads=[("ps", bS[c]), "cst"], writes=[("PT", pr)])
                else:
                    P.add("act", lambda e, c=c, pr=pr, bS=bS, W=W: e.activation(
                        out=PT[pr][:, 0:W], in_=ps[bS[c]][:, 0:W], func=AF.Exp, scale=0.125),
                        reads=[("ps", bS[c])], writes=[("PT", pr)])
            g["prs"] = prs

        def emit_av(g):
            h, kbs, bO, u2 = g["h"], g["kbs"], g["bO"], g["u2"]
            if g["diag"]:
                kb = kbs[0]
                for c in range(2):
                    P.add("pe", lambda e, c=c, kb=kb, h=h, u2=u2, bO=bO: e.matmul(
                        ps[bO[c]][:, c * 256:c * 256 + 129], lhsT=PTd[u2][c][:, :], rhs=Vaug[:, kb, h, 0:129],
                        start=False, stop=True, skip_group_check=True),
                        reads=[("PTd", u2), ("V", kb, h), "Vones"], writes=[("ps", bO[c])])
                return
            prs = g["prs"]
            for idx, kb in enumerate(kbs):
                for c in range(2):
                    st = g["first"] and idx == 0 and c == 0
                    P.add("pe", lambda e, c=c, idx=idx, kb=kb, h=h, pr=prs[c], st=st, bO=bO: e.matmul(
                        ps[bO[c]][:, c * 256:c * 256 + 129], lhsT=PT[pr][:, idx * 128:(idx + 1) * 128], rhs=Vaug[:, kb, h, 0:129],
                        start=st, stop=False, skip_group_check=True),
                        reads=[("PT", prs[c]), ("V", kb, h), "Vones"], writes=[("ps", bO[c])])

        def epilogue_stages(h, i, u2, bO):
            E = esm[u2]
            er = ("esm", u2)

            def stA():
                P.add("dve", lambda e: e.reciprocal(out=E[:, 0:1], in_=ps[bO[0]][:, 128:129]),
                      reads=[("ps", bO[0])], writes=[er])
                P.add("dve", lambda e: e.reciprocal(out=E[:, 1:2], in_=ps[bO[1]][:, 384:385]),
                      reads=[("ps", bO[1]), er], writes=[er])
                P.add("dve", lambda e: e.tensor_tensor(out=E[:, 2:3], in0=E[:, 1:2], in1=neglam, op=ALU.mult),
                      reads=[er, "neglam"], writes=[er])
                P.add("dve", lambda e: e.tensor_scalar(
                    out=o_a[u2][:, :], in0=ps[bO[0]][:, 0:128], scalar1=E[:, 0:1], scalar2=None, op0=ALU.mult),
                    reads=[("ps", bO[0]), er], writes=[("oa", u2)])
                P.add("dve", lambda e: e.scalar_tensor_tensor(
                    out=o_b[u2][:, :], in0=ps[bO[1]][:, 256:384], scalar=E[:, 2:3], in1=o_a[u2][:, :], op0=ALU.mult, op1=ALU.add),
                    reads=[("ps", bO[1]), er, ("oa", u2)], writes=[("ob", u2)])
                P.add("dve", lambda e: e.scalar_tensor_tensor(
                    out=o_a[u2][:, :], in0=o_b[u2][:, :], scalar=1.0, in1=o_b[u2][:, :], op0=ALU.mult, op1=ALU.mult,
                    accum_out=E[:, 3:4]), reads=[("ob", u2), ("oa", u2)], writes=[("oa", u2), er])

            def stB():
                P.add("act", lambda e: e.activation(out=E[:, 4:5], in_=E[:, 3:4], func=AF.Ln, scale=1.0 / 128, bias=EPS),
                      reads=[er], writes=[er])
                P.add("act", lambda e: e.activation(out=E[:, 5:6], in_=E[:, 4:5], func=AF.Exp, scale=-0.5),
                      reads=[er], writes=[er])
                P.add("dve", lambda e: e.scalar_tensor_tensor(
                    out=o_c[u2][:, :], in0=o_b[u2][:, :], scalar=E[:, 5:6], in1=cs(C_GREP, 128), op0=ALU.mult, op1=ALU.mult),
                    reads=[("ob", u2), er, "cst"], writes=[("oc", u2)])

            def stC():
                bt = bO[0]
                P.add("pe", lambda e: e.transpose(out=psb[bt][:, 0:128], in_=o_c[u2][:, :], identity=ident_b[:, :]),
                      reads=[("oc", u2), "ident"], writes=[("ps", bt)])
                P.add("act", lambda e: e.activation(
                    out=mixA[:, h, i * 128:(i + 1) * 128], in_=psb[bt][:, 0:128], func=AF.Copy),
                    reads=[("ps", bt)], writes=[("mixA", h, i // 4)])
            return [stA, stB, stC]

        glist = []
        unit = 0
        for h in range(8):
            for i in range(8):
                u2 = unit % 2
                bO = (u2, u2)
                gs = [([0, 1, 2, 3], True), ([4, 5, 6, 7], True)]
                ownk = list(range(8, 8 + i))
                while ownk:
                    gs.append((ownk[:4], False))
                    ownk = ownk[4:]
                ug = []
                for gi, (kbs, isprev) in enumerate(gs):
                    ug.append(dict(h=h, i=i, u2=u2, bO=bO, kbs=kbs, prev=isprev, diag=False, first=(gi == 0), last=False, gi=gi))
                ug.append(dict(h=h, i=i, u2=u2, bO=bO, kbs=[8 + i], prev=False, diag=True, first=False, last=True, gi=len(gs)))
                glist.extend(ug)
                unit += 1

        pend = {}

        def start_group(g):
            g["bS"] = (nextbank(2, 8), nextbank(2, 8))
            emit_qk(g)

        start_group(glist[0])
        start_group(glist[1])
        for n, g in enumerate(glist):
            if n + 2 < len(glist):
                start_group(glist[n + 2])
            emit_exp(g)
            emit_av(g)
            for stg in pend.pop(g["gi"], []):
                stg()
            if g["last"]:
                for k_ in sorted(pend):
                    for stg in pend[k_]:
                        stg()
                pend.clear()
                sA, sB, sC = epilogue_stages(g["h"], g["i"], g["u2"], g["bO"])
                sA()
                pend = {0: [sB], 1: [sC]}
        for k_ in sorted(pend):
            for stg in pend[k_]:
                stg()

        ck('att')
        P.fence()

        for c in range(16):
            s = load_w("a", w_o[c, :, :, :])
            r = c % 2
            for th in range(2):
                P.add("sp", lambda e, c=c, r=r, th=th: e.dma_start(
                    out=xs[r][:, th * 512:(th + 1) * 512], in_=xT_own[:, c, th * 512:(th + 1) * 512]),
                    writes=[("xs", r, th)], dma=("xs", r, th))
            for th in range(2):
                b = nextbank()
                for fc in range(16):
                    rhs = (mixP if fc < 8 else mixA)
                    P.add("pe", lambda e, fc=fc, s=s, th=th, b=b, rhs=rhs: e.matmul(
                        ps[b][:, :], lhsT=wr_a[s][:, fc, :], rhs=rhs[:, fc % 8, th * 512:(th + 1) * 512],
                        start=(fc == 0), stop=(fc == 15)),
                        reads=[("w", s), (("mixP", fc, th) if fc < 8 else ("mixA", fc - 8, th))], writes=[("ps", b)])
                P.add("dve", lambda e, c=c, th=th, b=b, r=r: e.tensor_tensor(
                    out=X1[:, c, th * 512:(th + 1) * 512], in0=ps[b][:, :], in1=xs[r][:, th * 512:(th + 1) * 512], op=ALU.add),
                    reads=[("ps", b), ("xs", r, th)], writes=[("X1", c)])

        ck('wout')
        P.fence()

        def get_x1(dc, th):
            return X1[:, dc, th * 512:(th + 1) * 512], [("X1", dc)]
        norm_stats(get_x1, sqbC, rstdC, "n2")
        for dc in range(NDC):
            P.add("dve", lambda e, dc=dc: e.scalar_tensor_tensor(
                out=h2T[:, dc, :], in0=X1[:, dc, :], scalar=cs(C_G2 + dc), in1=rstdC[:, :], op0=ALU.mult, op1=ALU.mult),
                reads=[("X1", dc), "n2rstdF", "cst"], writes=["h2T"])

        ck('n2')
        sgc = [0]
        for G in range(NG):
            hb = hTg[G % 2]
            for jj in range(JG):
                j = G * JG + jj
                s = load_w("gu", w_gu[j, :, :, :, :])
                bks = {}
                for gu in range(2):
                    for th in range(2):
                        b = nextbank()
                        bks[(gu, th)] = b
                        for dc in range(NDC):
                            P.add("pe", lambda e, dc=dc, s=s, gu=gu, th=th, b=b: e.matmul(
                                ps[b][:, :], lhsT=wr_gu[s][:, gu, dc, :], rhs=h2T[:, dc, th * 512:(th + 1) * 512],
                                start=(dc == 0), stop=(dc == NDC - 1)),
                                reads=[("w", s), "h2T"], writes=[("ps", b)])
                for th in range(2):
                    r = sgc[0] % 2
                    sgc[0] += 1
                    bg, bu = bks[(0, th)], bks[(1, th)]
                    P.add("act", lambda e, r=r, bg=bg: e.activation(out=sg[r][:, :], in_=ps[bg][:, :], func=AF.Silu),
                          reads=[("ps", bg)], writes=[("sg", r)])
                    P.add("dve", lambda e, r=r, bu=bu, jj=jj, th=th, hb=hb: e.tensor_tensor(
                        out=hb[:, jj, th * 512:(th + 1) * 512], in0=sg[r][:, :], in1=ps[bu][:, :], op=ALU.mult),
                        reads=[("sg", r), ("ps", bu)], writes=[("hTg", G % 2, th)])
            for cp in range(8):
                s = load_w("dn", w_dn[G, cp, :, :, :])
                for cc in range(2):
                    c = cp * 2 + cc
                    for th in range(2):
                        b = nextbank()
                        for jj in range(JG):
                            P.add("pe", lambda e, jj=jj, s=s, cc=cc, th=th, b=b, hb=hb: e.matmul(
                                ps[b][:, :], lhsT=wr_dn[s][:, jj, cc * 128:(cc + 1) * 128], rhs=hb[:, jj, th * 512:(th + 1) * 512],
                                start=(jj == 0), stop=(jj == JG - 1)),
                                reads=[("w", s), ("hTg", G % 2, th)], writes=[("ps", b)])
                        P.add("dve", lambda e, c=c, th=th, b=b: e.tensor_tensor(
                            out=X1[:, c, th * 512:(th + 1) * 512], in0=X1[:, c, th * 512:(th + 1) * 512], in1=ps[b][:, :], op=ALU.add),
                            reads=[("ps", b), ("X1", c)], writes=[("X1", c)])

        ck('ffn')
        norm_stats(get_x1, sqbC, rstdC, "n3")
        for dc in range(NDC):
            P.add("dve", lambda e, dc=dc: e.scalar_tensor_tensor(
                out=X1[:, dc, :], in0=X1[:, dc, :], scalar=cs(C_G3 + dc), in1=rstdC[:, :], op0=ALU.mult, op1=ALU.mult),
                reads=[("X1", dc), "n3rstdF", "cst"], writes=[("X1", dc)])
            P.add("sp", lambda e, dc=dc: e.dma_start(out=outT[:, dc, :], in_=X1[:, dc, :]),
                  reads=[("X1", dc)], writes=[("out", dc)], dma="out")
        P.add("sp", None, reads=[("out", dc) for dc in range(NDC)])

    try:
        body()
    except _Stop:
        dump(stop)

    with contextlib.ExitStack() as stack:
        P.finalize(nc, stack)
        with nc.Block() as block:
            @block.tensor
            def _(e):
                P.emit("pe", e)

            @block.vector
            def _(e):
                P.emit("dve", e)

            @block.scalar
            def _(e):
                P.emit("act", e)

            @block.gpsimd
            def _(e):
                P.emit("pool", e)

            @block.sync
            def _(e):
                P.emit("sp", e)
    return nc


def _fm(a):
    t = a.shape[0]
    return np.ascontiguousarray(a.T.reshape(NDC, 128, t).transpose(1, 0, 2))


def prepare_inputs(x, norm_mix, w_in, w_pool, pool_scale, lambda_q1, lambda_k1, lambda_q2, lambda_k2,
                   subln_gain, w_out, norm_ffn, w_gate_up, w_down, norm_final):
    f = np.float32
    x = np.asarray(x, f)
    w_in = np.asarray(w_in, f)[0]
    w_pool = np.asarray(w_pool, f)[0]
    w_out = np.asarray(w_out, f)[0]
    w_gate_up = np.asarray(w_gate_up, f)[0]
    w_down = np.asarray(w_down, f)[0]

    def colchunks(w, ncol):
        C = w.shape[1]
        return np.ascontiguousarray(w.reshape(NDC, 128, C // ncol, ncol).transpose(2, 1, 0, 3))

    w_uqk = colchunks(w_in[:, :3072], 128)
    w_v = colchunks(w_in[:, 3072:], 256)
    w_pool_r = np.ascontiguousarray(w_pool.reshape(4, 2, 128, 256).transpose(2, 0, 1, 3))
    w_o = colchunks(w_out, 128)
    w_gu = np.ascontiguousarray(w_gate_up.reshape(NDC, 128, 2, NJ, 128).transpose(3, 1, 2, 0, 4))
    w_dn = np.ascontiguousarray(w_down.reshape(NG, JG, 128, 8, 256).transpose(0, 3, 2, 1, 4))

    inv_freq = (np.float32(10000.0) ** (-(np.arange(0, 64, 2, dtype=f) / f(64)))).astype(f)
    p_idx = np.arange(128)
    fr = inv_freq[p_idx % 32]
    sign = np.where((p_idx % 64) < 32, -1.0, 1.0).astype(f)
    partner = np.where((p_idx % 64) < 32, p_idx + 32, p_idx - 32)
    Rm = np.zeros((128, 128), f)
    Rm[partner, p_idx] = 1.0

    base = np.zeros((128, C_N), f)
    base[:, C_G1:C_G1 + 16] = np.asarray(norm_mix, f)[0].reshape(16, 128).T
    base[:, C_G2:C_G2 + 16] = np.asarray(norm_ffn, f)[0].reshape(16, 128).T
    base[:, C_G3:C_G3 + 16] = np.asarray(norm_final, f).reshape(16, 128).T
    base[:, C_PS:C_PS + 8] = np.asarray(pool_scale, f)[0].reshape(8, 128).T
    base[:, C_LQ1:C_LQ1 + 64] = np.asarray(lambda_q1, f)[0][None, :]
    base[:, C_LK1:C_LK1 + 64] = np.asarray(lambda_k1, f)[0][None, :]
    base[:, C_LQ2:C_LQ2 + 64] = np.asarray(lambda_q2, f)[0][None, :]
    base[:, C_LK2:C_LK2 + 64] = np.asarray(lambda_k2, f)[0][None, :]
    base[:, C_GREP:C_GREP + 128] = np.asarray(subln_gain, f)[0][None, :]
    base[:, C_ONES:C_ONES + 128] = 1.0
    base[:, C_ID:C_ID + 128] = np.eye(128, dtype=f)
    base[:, C_RM:C_RM + 128] = Rm

    shared = dict(w_uqk=w_uqk, w_v=w_v, w_pool=w_pool_r, w_o=w_o, w_gu=w_gu, w_dn=w_dn)
    in_maps = []
    for core in range(8):
        b, half = core // 2, core % 2
        own0 = half * S_OWN
        xo = _fm(x[b, own0:own0 + S_OWN])
        if half == 1:
            xp = _fm(x[b, 0:S_OWN])
        else:
            xp = np.zeros_like(xo)
        rot = np.zeros((128, 2, 2, 1024), f)
        for ip in range(2):
            pos = (own0 - S_OWN + ip * S_OWN + np.arange(S_OWN)).astype(f)
            ang = (pos[None, :] * fr[:, None]).astype(f)
            rot[:, ip, 0, :] = np.cos(ang)
            rot[:, ip, 1, :] = np.sin(ang) * sign[:, None]
        c = base.copy()
        c[:, C_PB] = 0.0 if half == 1 else NEG_BIG
        for g, w in enumerate((2, 4, 8, 16)):
            posn = own0 + np.arange(16)
            c[:, C_IC + 16 * g:C_IC + 16 * (g + 1)] = (1.0 / np.minimum(posn + 1, w).astype(f))[None, :]
        m = dict(shared)
        m.update(xT_own=xo, xT_prev=xp, rot=rot, cst=c)
        in_maps.append(m)
    return in_maps


_NC_CACHE = {}


def kernel(x, norm_mix, w_in, w_pool, pool_scale, lambda_q1, lambda_k1, lambda_q2, lambda_k2,
           subln_gain, w_out, norm_ffn, w_gate_up, w_down, norm_final):
    in_maps = prepare_inputs(x, norm_mix, w_in, w_pool, pool_scale, lambda_q1, lambda_k1, lambda_q2, lambda_k2,
                             subln_gain, w_out, norm_ffn, w_gate_up, w_down, norm_final)
    if "nc" not in _NC_CACHE:
        _NC_CACHE["nc"] = build_program()
    nc = _NC_CACHE["nc"]
    res = run_bass_kernel_spmd(nc, in_maps, core_ids=list(range(8)))
    out = np.empty((4, 2048, D), np.float32)
    for core in range(8):
        b, half = core // 2, core % 2
        o = np.asarray(res.results[core]["outT"], np.float32)
        out[b, half * S_OWN:(half + 1) * S_OWN, :] = o.transpose(2, 1, 0).reshape(S_OWN, D)
    return out
```

```python
import contextlib
import math

import numpy as np
import concourse.bass as bass
import concourse.mybir as mybir
from concourse.bass_utils import run_bass_kernel_spmd

F32 = mybir.dt.float32
BF16 = mybir.dt.bfloat16
AF = mybir.ActivationFunctionType
ALU = mybir.AluOpType
AX = mybir.AxisListType

D = 2048
NDC = 16
S_OWN = 1024
FFN = 5632
NJ = 44
NG = 4
JG = 11
EPS = 1e-6
LAM_INIT = 0.8 - 0.6 * math.exp(0.0)
NEG_BIG = -30000.0

C_G1, C_G2, C_G3 = 0, 16, 32
C_PS = 48
C_PB = 56
C_IC = 57
C_LQ1, C_LK1, C_LQ2, C_LK2 = 121, 185, 249, 313
C_GREP = 377
C_ONES = 505
C_ID = 633
C_RM = 761
C_N = 896

NS = 4


class Op:
    __slots__ = ("id", "eng", "idx", "fn", "deps", "dma", "signal", "cum", "ordinal", "ndma")


class Prog:
    ENGS = ("pe", "act", "dve", "pool", "sp")

    def __init__(self):
        self.ops = []
        self.eng_ops = {e: [] for e in self.ENGS}
        self.last_w = {}
        self.readers = {}
        self.pending_fence = {}
        self.last_dma = {}
        self.aggregate = set()

    def add(self, eng, fn, reads=(), writes=(), dma=None, ndma=1):
        op = Op()
        op.ndma = ndma
        op.id = len(self.ops)
        op.eng = eng
        op.fn = fn
        op.dma = dma
        op.signal = False
        writes = list(writes) + [r for r in reads if isinstance(r, tuple) and r[0] == "ps" and r not in writes]
        deps = set()
        for r in reads:
            lw = self.last_w.get(r)
            if lw is not None:
                deps.add(lw)
        for w in writes:
            lw = self.last_w.get(w)
            if lw is not None:
                deps.add(lw)
            rs = self.readers.get(w)
            if rs:
                deps.update(rs)
        for r in reads:
            self.readers.setdefault(r, []).append(op.id)
        for w in writes:
            self.last_w[w] = op.id
            self.readers[w] = []
        pf = self.pending_fence.pop(eng, None)
        if pf:
            deps.update(pf)
        deps.discard(op.id)
        if eng == "pe" and dma is None:
            deps = {d for d in deps if not (self.ops[d].eng == "pe" and self.ops[d].dma is None)}
        op.deps = deps
        op.idx = len(self.eng_ops[eng])
        self.eng_ops[eng].append(op)
        self.ops.append(op)
        if dma is not None:
            self.last_dma[dma] = op.id
        return op.id

    def fence(self):
        deps = set()
        for e in self.ENGS:
            lst = self.eng_ops[e]
            for o in reversed(lst):
                if o.dma is None and o.fn is not None:
                    deps.add(o.id)
                    break
        deps.update(self.last_dma.values())
        for e in self.ENGS:
            self.pending_fence.setdefault(e, set()).update(deps)

    def finalize(self, nc, stack):
        for op in self.ops:
            for d in op.deps:
                t = self.ops[d]
                if t.dma is None:
                    t.signal = True
        self.eng_sem = {}
        for e in self.ENGS:
            self.eng_sem[e] = stack.enter_context(nc.semaphore("s_" + e))
            c = 0
            for o in self.eng_ops[e]:
                if o.dma is None and o.signal:
                    c += 1
                o.cum = c
        self.dma_sem = {}
        counts = {}
        for op in self.ops:
            if op.dma is not None:
                k = op.dma
                if k not in self.dma_sem:
                    nm = "d_" + "_".join(str(x) for x in (k if isinstance(k, tuple) else (k,)))
                    self.dma_sem[k] = stack.enter_context(nc.semaphore(nm))
                counts[k] = counts.get(k, 0) + op.ndma
                op.ordinal = counts[k]
        self.dma_total = counts

    def wait_of(self, d):
        t = self.ops[d]
        if t.dma is None:
            return (self.eng_sem[t.eng], t.cum)
        k = t.dma
        n = self.dma_total[k] if k in self.aggregate else t.ordinal
        return (self.dma_sem[k], 16 * n)

    def emit(self, eng, e):
        known = {}
        for op in self.eng_ops[eng]:
            ws = {}
            for d in op.deps:
                sem, val = self.wait_of(d)
                key = id(sem)
                if ws.get(key, (None, 0))[1] < val:
                    ws[key] = (sem, val)
            for key, (sem, val) in ws.items():
                if known.get(key, 0) < val:
                    e.wait_ge(sem, val)
                    known[key] = val
            if op.fn is None:
                continue
            ins = op.fn(e)
            if op.dma is not None:
                for one in (ins if isinstance(ins, list) else [ins]):
                    one.then_inc(self.dma_sem[op.dma], 16)
            elif op.signal:
                ins.then_inc(self.eng_sem[eng], 1)


class _Stop(Exception):
    pass


def build_program(stop=None):
    nc = bass.Bass("TRN2", target_bir_lowering=False)
    dt = nc.dram_tensor
    xT_own = dt("xT_own", [128, NDC, S_OWN], F32, kind="ExternalInput")
    xT_prev = dt("xT_prev", [128, NDC, S_OWN], F32, kind="ExternalInput")
    rot_d = dt("rot", [128, 2, 2, 1024], F32, kind="ExternalInput")
    cst_d = dt("cst", [128, C_N], F32, kind="ExternalInput")
    w_uqk = dt("w_uqk", [24, 128, 16, 128], F32, kind="ExternalInput")
    w_v = dt("w_v", [4, 128, 16, 256], F32, kind="ExternalInput")
    w_pool = dt("w_pool", [128, 4, 2, 256], F32, kind="ExternalInput")
    w_o = dt("w_o", [16, 128, 16, 128], F32, kind="ExternalInput")
    early = stop in ("setup", "pk_load", "pk_mm", "pk_rot", "n1prev", "projprev", "projown", "att", "wout", "n2")
    w_gu = None if early else dt("w_gu", [NJ, 128, 2, 16, 128], F32, kind="ExternalInput")
    w_dn = None if early else dt("w_dn", [NG, 8, 128, JG, 256], F32, kind="ExternalInput")
    outT = dt("outT", [128, NDC, S_OWN], F32, kind="ExternalOutput")

    BASE = 16640
    cnt = [0]

    def sb(name, shape, dtype, off):
        cnt[0] += 1
        return nc.alloc_sbuf_tensor_at(f"{name}_{cnt[0]}", list(shape), dtype, offset=BASE + off)

    cst = sb("cst", [128, C_N], F32, 0)
    ones_b = sb("ones", [128, 128], BF16, 3584)
    ident_b = sb("ident", [128, 128], BF16, 3840)
    Rm_b = sb("rm", [128, 128], BF16, 4096)
    small = sb("small", [128, 256], F32, 4352)
    wp = sb("wp", [128, 4, 2, 256], BF16, 5376)
    W0 = 9472
    wr_a = [sb("wra", [128, 16, 128], BF16, W0 + s * 8192) for s in range(NS)]
    wr_v = [sb("wrv", [128, 16, 256], BF16, W0 + s * 8192) for s in range(NS)]
    wr_gu = [sb("wrgu", [128, 2, 16, 128], BF16, W0 + s * 8192) for s in range(NS)]
    wr_dn = [sb("wrdn", [128, JG, 256], BF16, W0 + s * 8192) for s in range(NS)]
    RH = 42240
    hT = sb("hT", [128, 16, 1024], BF16, RH)
    mixA = sb("mixA", [128, 8, 1024], BF16, RH)
    ATS = RH + 16384
    PT = [sb("PT", [128, 512], BF16, ATS + r * 1024) for r in range(8)]
    PTd = [[sb("PTd", [128, 128], BF16, ATS + 8192 + (r * 2 + c) * 256) for c in range(2)] for r in range(2)]
    o_a = [sb("oa", [128, 128], F32, ATS + 9216 + r * 512) for r in range(2)]
    o_b = [sb("ob", [128, 128], F32, ATS + 10240 + r * 512) for r in range(2)]
    o_c = [sb("oc", [128, 128], BF16, ATS + 11264 + r * 256) for r in range(2)]
    esm = [sb("esm", [128, 16], F32, ATS + 11776 + r * 64) for r in range(2)]
    h2T = sb("h2T", [128, 16, 1024], BF16, RH)
    RK = 75008
    KT = sb("KT", [128, 8, 2048], BF16, RK)
    QT = sb("QT", [128, 8, 1024], BF16, RK + 32768)
    Vaug = sb("Vaug", [128, 16, 8, 130], BF16, RK + 49152)
    X1 = sb("X1", [128, 16, 1024], F32, RK)
    CS = RK + 65536
    sqbC = [sb("sqbC", [128, 1024], BF16, CS + r * 2048) for r in range(2)]
    rstdC = sb("rstdC", [128, 1024], F32, CS + 4096)
    sg = [sb("sg", [128, 512], F32, CS + 8192 + r * 2048) for r in range(2)]
    MP = 157440
    mixP = sb("mixP", [128, 8, 1024], BF16, MP)
    hTg = [sb("hTg", [128, JG, 1024], BF16, MP + r * 22528) for r in range(2)]
    RT = 173824
    rot = sb("rot", [128, 2, 1024], F32, RT)
    hThalo = sb("hThalo", [128, 16, 16], BF16, 182016)
    SCR = 182528
    xs = [sb("xs", [128, 1024], F32, SCR + r * 4096) for r in range(2)]
    sqb = [sb("sqb", [128, 1024], BF16, SCR + 8192 + r * 2048) for r in range(2)]
    rstd = sb("rstd", [128, 1024], F32, SCR + 12288)
    ubuf = sb("ubuf", [128, 1040], F32, SCR)
    pa = [sb("pa", [128, 1040], F32, SCR + 4160 * (r + 1)) for r in range(2)]
    pooled = [sb("pooled", [128, 1024], BF16, SCR + 12480 + r * 2048) for r in range(2)]
    tmp16 = sb("tmp16", [128, 16], F32, SCR + 16576)
    RS = 199168
    qb = [sb("qb", [128, 512], BF16, RS + r * 1024) for r in range(2)]
    t1 = [sb("t1", [128, 512], F32, RS + 2048 + r * 2048) for r in range(2)]
    t2 = [sb("t2", [128, 512], F32, RS + 6144 + r * 2048) for r in range(2)]
    assert BASE + RS + 10240 <= 229344

    ps = [nc.alloc_psum_tensor(f"ps{b}", [128, 512], F32) for b in range(8)]
    psb = [p.bitcast(BF16) for p in ps]

    P = Prog()
    P.aggregate.update(["const", "out"])
    bank = [0]

    def nextbank(lo=0, hi=8):
        b = lo + (bank[0] % (hi - lo))
        bank[0] += 1
        return b

    wslot = [0]

    def load_w(kind, src):
        s = wslot[0] % NS
        wslot[0] += 1
        if kind == "a":
            P.add("pool", lambda e, s=s, src=src: e.dma_start(out=wr_a[s][:, :, :], in_=src, max_dma_last_dim=4096),
                  writes=[("w", s)], dma=("w", s))
        elif kind == "v":
            P.add("pool", lambda e, s=s, src=src: e.dma_start(out=wr_v[s][:, :, :], in_=src, max_dma_last_dim=4096),
                  writes=[("w", s)], dma=("w", s))
        elif kind == "gu":
            P.add("pool", lambda e, s=s, src=src: [
                e.dma_start(out=wr_gu[s][:, 0, :, :], in_=src[:, 0, :, :], max_dma_last_dim=4096),
                e.dma_start(out=wr_gu[s][:, 1, :, :], in_=src[:, 1, :, :], max_dma_last_dim=4096)],
                writes=[("w", s)], dma=("w", s), ndma=2)
        elif kind == "dn":
            P.add("pool", lambda e, s=s, src=src: e.dma_start(out=wr_dn[s][:, :, :], in_=src, max_dma_last_dim=4096),
                  writes=[("w", s)], dma=("w", s))
        return s

    def cs(c0, n=1):
        return cst[:, c0:c0 + n]

    def ck(name):
        if stop == name:
            raise _Stop()

    def dump(name):
        srcs = {
            "setup": [(small[:, :], outT[:, 0, 0:256])],
            "pk_load": [(wr_a[0][:, dc, :], outT[:, dc, 0:128]) for dc in range(16)],
            "pk_mm": [(hT[:, 0, :], outT[:, 0, :])],
            "pk_rot": [(KT[:, 0, 0:1024], outT[:, 0, :])],
            "n1prev": [(rstd[:, :], outT[:, 0, :])] + [(hT[:, dc, :], outT[:, 1 + dc % 15, :]) for dc in (0, 5)],
            "projprev": [(KT[:, h, 0:1024], outT[:, h, :]) for h in range(8)] +
                        [(Vaug[:, tb, :, 0:128], outT[:, 8 + tb, :]) for tb in range(8)],
            "projown": [(KT[:, 0, 1024:2048], outT[:, 0, :]), (KT[:, 7, 0:1024], outT[:, 3, :])] +
                       [(QT[:, h, :], outT[:, 1 + h, :]) for h in (0, 1)] +
                       [(mixP[:, c, :], outT[:, 4 + c, :]) for c in range(8)] +
                       [(Vaug[:, tb, :, 0:128], outT[:, 12 + i_, :]) for i_, tb in enumerate((8, 9, 3, 15))],
            "att": [(mixA[:, h, :], outT[:, h, :]) for h in range(8)] + [(mixP[:, c, :], outT[:, 8 + c, :]) for c in range(8)],
            "wout": [(X1[:, c, :], outT[:, c, :]) for c in range(16)],
            "n2": [(X1[:, c, :], outT[:, c, :]) for c in range(8)] + [(h2T[:, c, :], outT[:, 8 + c, :]) for c in range(8)],
            "ffn": [(X1[:, c, :], outT[:, c, :]) for c in range(16)],
        }[name]
        P.fence()
        for i, (src, dst) in enumerate(srcs):
            P.add("pool", lambda e, src=src, dst=dst: e.dma_start(out=dst, in_=src, max_dma_last_dim=2048), writes=[("out", i)], dma="out")
        P.add("sp", None, reads=[("out", i) for i in range(len(srcs))])

    def body():
        P.add("sp", lambda e: e.dma_start(out=cst[:, :], in_=cst_d[:, :]), writes=["cst"], dma="const")
        P.add("pool", lambda e: e.dma_start(out=wp[:, :, :, :], in_=w_pool[:, :, :, :], max_dma_last_dim=4096),
              writes=["wp"], dma="wp")
        P.add("dve", lambda e: e.tensor_copy(out=ones_b[:, :], in_=cs(C_ONES, 128)), reads=["cst"], writes=["ones"])
        P.add("dve", lambda e: e.tensor_copy(out=ident_b[:, :], in_=cs(C_ID, 128)), reads=["cst"], writes=["ident"])
        P.add("dve", lambda e: e.tensor_copy(out=Rm_b[:, :], in_=cs(C_RM, 128)), reads=["cst"], writes=["rm"])
        P.add("dve", lambda e: e.tensor_tensor(out=small[:, 0:64], in0=cs(C_LQ1, 64), in1=cs(C_LK1, 64), op=ALU.mult),
              reads=["cst"], writes=["sm0"])
        P.add("dve", lambda e: e.reduce_sum(out=small[:, 128:129], in_=small[:, 0:64], axis=AX.X), reads=["sm0"], writes=["sm1"])
        P.add("dve", lambda e: e.tensor_tensor(out=small[:, 64:128], in0=cs(C_LQ2, 64), in1=cs(C_LK2, 64), op=ALU.mult),
              reads=["cst"], writes=["sm2"])
        P.add("dve", lambda e: e.reduce_sum(out=small[:, 129:130], in_=small[:, 64:128], axis=AX.X), reads=["sm2"], writes=["sm3"])
        P.add("act", lambda e: e.activation(out=small[:, 130:132], in_=small[:, 128:130], func=AF.Exp),
              reads=["sm1", "sm3"], writes=["sm4"])
        P.add("dve", lambda e: e.tensor_tensor(out=small[:, 132:133], in0=small[:, 131:132], in1=small[:, 130:131], op=ALU.subtract),
              reads=["sm4"], writes=["sm5"])
        P.add("dve", lambda e: e.tensor_scalar_add(out=small[:, 133:134], in0=small[:, 132:133], scalar1=-LAM_INIT),
              reads=["sm5"], writes=["neglam"])
        neglam = small[:, 133:134]
        P.add("dve", lambda e: e.tensor_scalar_mul(out=cst[:, C_GREP:C_GREP + 128], in0=cst[:, C_GREP:C_GREP + 128],
                                                   scalar1=(1.0 - LAM_INIT)), reads=["cst"], writes=["cst"])

        ck('setup')
        def norm_stats(get_src, sq_bufs, rstd_buf, tag):
            b0, b1 = nextbank(), nextbank()
            for dc in range(NDC):
                r = dc % 2
                for th, b in ((0, b0), (1, b1)):
                    ap, res = get_src(dc, th)
                    P.add("act", lambda e, ap=ap, r=r, th=th: e.activation(
                        out=sq_bufs[r][:, th * 512:(th + 1) * 512], in_=ap, func=AF.Square),
                        reads=res, writes=[(tag + "sq", r, th)])
                    P.add("pe", lambda e, r=r, th=th, b=b, dc=dc: e.matmul(
                        ps[b][:, :], lhsT=ones_b[:, :], rhs=sq_bufs[r][:, th * 512:(th + 1) * 512],
                        start=(dc == 0), stop=(dc == NDC - 1)),
                        reads=[(tag + "sq", r, th), "ones"], writes=[("ps", b)])
            for th, b in ((0, b0), (1, b1)):
                P.add("dve", lambda e, th=th, b=b: e.tensor_scalar(
                    out=rstd_buf[:, th * 512:(th + 1) * 512], in0=ps[b][:, :], scalar1=1.0 / D, scalar2=EPS,
                    op0=ALU.mult, op1=ALU.add), reads=[("ps", b)], writes=[(tag + "rstd", th)])
            P.add("act", lambda e: e.activation(out=rstd_buf[:, :], in_=rstd_buf[:, :], func=AF.Sqrt),
                  reads=[(tag + "rstd", 0), (tag + "rstd", 1)], writes=[tag + "rstd2"])
            P.add("dve", lambda e: e.reciprocal(out=rstd_buf[:, :], in_=rstd_buf[:, :]),
                  reads=[tag + "rstd2"], writes=[tag + "rstdF"])

        rot_pending = []
        wp_pending = []

        def flush_wp():
            while wp_pending:
                wp_pending.pop(0)()

        def flush_rot():
            while rot_pending:
                rot_pending.pop(0)()

        rotc = [0]

        def rotary_tile(b, dest_ap, dest_res, th):
            r = rotc[0] % 2
            rotc[0] += 1
            P.add("act", lambda e, b=b, r=r: e.activation(out=qb[r][:, :], in_=ps[b][:, :], func=AF.Copy),
                  reads=[("ps", b)], writes=[("qb", r)])

            def rest(b=b, r=r, dest_ap=dest_ap, dest_res=dest_res, th=th):
                b2 = nextbank()
                P.add("pe", lambda e: e.matmul(ps[b2][:, :], lhsT=Rm_b[:, :], rhs=qb[r][:, :], start=True, stop=True),
                      reads=[("qb", r), "rm"], writes=[("ps", b2)])
                P.add("dve", lambda e: e.tensor_tensor(out=t1[r][:, :], in0=ps[b][:, :], in1=rot[:, 0, th * 512:(th + 1) * 512], op=ALU.mult),
                      reads=[("ps", b), "rot"], writes=[("t1", r)])
                P.add("dve", lambda e: e.tensor_tensor(out=t2[r][:, :], in0=ps[b2][:, :], in1=rot[:, 1, th * 512:(th + 1) * 512], op=ALU.mult),
                      reads=[("ps", b2), "rot"], writes=[("t2", r)])
                P.add("dve", lambda e: e.tensor_tensor(out=dest_ap, in0=t1[r][:, :], in1=t2[r][:, :], op=ALU.add),
                      reads=[("t1", r), ("t2", r)], writes=[dest_res])
            rot_pending.append(rest)

        def proj_fm_matmuls(s, th, b):
            for dc in range(NDC):
                P.add("pe", lambda e, dc=dc: e.matmul(ps[b][:, :], lhsT=wr_a[s][:, dc, :], rhs=hT[:, dc, th * 512:(th + 1) * 512],
                                                       start=(dc == 0), stop=(dc == NDC - 1)),
                      reads=[("w", s), "hT"], writes=[("ps", b)])

        for ipass, (xsrc, tokoff, tboff) in enumerate(((xT_prev, 0, 0), (xT_own, 1024, 8))):
            own = ipass == 1
            P.add("sp", lambda e, ipass=ipass: e.dma_start(out=rot[:, :, :], in_=rot_d[:, ipass, :, :]), writes=["rot"], dma="rot")

            def get_src(dc, th, xsrc=xsrc):
                r = dc % 2
                P.add("sp", lambda e, dc=dc, r=r, th=th: e.dma_start(
                    out=xs[r][:, th * 512:(th + 1) * 512], in_=xsrc[:, dc, th * 512:(th + 1) * 512]),
                    writes=[("xs", r, th)], dma=("xs", r, th))
                P.add("dve", lambda e, dc=dc, r=r, th=th: e.tensor_scalar(
                    out=hT[:, dc, th * 512:(th + 1) * 512], in0=xs[r][:, th * 512:(th + 1) * 512],
                    scalar1=cs(C_G1 + dc), scalar2=None, op0=ALU.mult),
                    reads=[("xs", r, th), "cst"], writes=[("hTraw", dc, th)])
                return xs[r][:, th * 512:(th + 1) * 512], [("xs", r, th)]
            P.add("dve", lambda e: e.tensor_copy(out=small[:, 140:141], in_=small[:, 140:141]), reads=[], writes=["hT"])
            norm_stats(get_src, sqb, rstd, "n1")
            for dc in range(NDC):
                P.add("dve", lambda e, dc=dc: e.tensor_tensor(
                    out=hT[:, dc, :], in0=hT[:, dc, :], in1=rstd[:, :], op=ALU.mult),
                    reads=[("hTraw", dc, 0), ("hTraw", dc, 1), "n1rstdF"], writes=["hT"])
            if not own:
                P.add("dve", lambda e: e.tensor_copy(out=hThalo[:, :, :], in_=hT[:, :, 1008:1024]), reads=["hT"], writes=["hThalo"])
                ck('n1prev')
            else:
                P.fence()

            if own:
                for n in range(8):
                    g = n // 2
                    a_ = n % 2
                    s = load_w("a", w_uqk[n, :, :, :])
                    bh, b0, b1 = nextbank(), nextbank(), nextbank()
                    for dc in range(NDC):
                        P.add("pe", lambda e, dc=dc, s=s, bh=bh: e.matmul(ps[bh][:, 0:16], lhsT=wr_a[s][:, dc, :], rhs=hThalo[:, dc, :],
                                                                          start=(dc == 0), stop=(dc == NDC - 1)),
                              reads=[("w", s), "hThalo"], writes=[("ps", bh)])
                    proj_fm_matmuls(s, 0, b0)
                    proj_fm_matmuls(s, 1, b1)
                    flush_wp()
                    P.add("act", lambda e, bh=bh: e.activation(out=ubuf[:, 0:16], in_=ps[bh][:, 0:16], func=AF.Copy),
                          reads=[("ps", bh)], writes=["ubuf"])
                    P.add("act", lambda e, b0=b0: e.activation(out=ubuf[:, 16:528], in_=ps[b0][:, :], func=AF.Copy),
                          reads=[("ps", b0), "ubuf"], writes=["ubuf"])
                    P.add("act", lambda e, b1=b1: e.activation(out=ubuf[:, 528:1040], in_=ps[b1][:, :], func=AF.Copy),
                          reads=[("ps", b1), "ubuf"], writes=["ubuf"])
                    m = g + 1
                    src = ubuf
                    src_res = "ubuf"
                    for st in range(1, m + 1):
                        sh = 2 ** (st - 1)
                        S0 = 2 ** st - 1
                        dst = pa[(st - 1) % 2]
                        dres = ("pa", (st - 1) % 2)
                        P.add("dve", lambda e, src=src, dst=dst, sh=sh, S0=S0: e.tensor_tensor(
                            out=dst[:, S0:1040], in0=src[:, S0:1040], in1=src[:, S0 - sh:1040 - sh], op=ALU.add),
                            reads=[src_res], writes=[dres])
                        src, src_res = dst, dres
                    w_ = 2 ** m
                    P.add("dve", lambda e, src=src, a_=a_, w_=w_: e.scalar_tensor_tensor(
                        out=pooled[a_][:, :], in0=src[:, 16:1040], scalar=1.0 / w_, in1=ubuf[:, 16:1040],
                        op0=ALU.mult, op1=ALU.subtract), reads=[src_res, "ubuf"], writes=[("pooled", a_)])
                    P.add("dve", lambda e, src=src, g=g: e.tensor_tensor(
                        out=tmp16[:, :], in0=src[:, 16:32], in1=cs(C_IC + 16 * g, 16), op=ALU.mult),
                        reads=[src_res, "cst"], writes=["tmp16"])
                    P.add("dve", lambda e, a_=a_: e.tensor_tensor(
                        out=pooled[a_][:, 0:16], in0=tmp16[:, :], in1=ubuf[:, 16:32], op=ALU.subtract),
                        reads=["tmp16", "ubuf", ("pooled", a_)], writes=[("pooled", a_)])
                    if a_ == 1:
                        def wpool_job(g=g):
                            for o in range(2):
                                for th in range(2):
                                    b = nextbank()
                                    for a2 in range(2):
                                        P.add("pe", lambda e, g=g, a2=a2, o=o, th=th, b=b: e.matmul(
                                            ps[b][:, :], lhsT=wp[:, g, a2, o * 128:(o + 1) * 128],
                                            rhs=pooled[a2][:, th * 512:(th + 1) * 512], start=(a2 == 0), stop=(a2 == 1)),
                                            reads=["wp", ("pooled", a2)], writes=[("ps", b)])
                                    P.add("act", lambda e, g=g, o=o, th=th, b=b: e.activation(
                                        out=mixP[:, 2 * g + o, th * 512:(th + 1) * 512], in_=ps[b][:, :], func=AF.Identity,
                                        scale=cs(C_PS + 2 * g + o)), reads=[("ps", b), "cst"], writes=[("mixP", 2 * g + o, th)])
                        wp_pending.append(wpool_job)
                flush_wp()

            kinds = (["q"] if own else []) + ["k"]
            for kind in kinds:
                for hc in range(8):
                    n = (8 if kind == "q" else 16) + hc
                    s = load_w("a", w_uqk[n, :, :, :])
                    ck('pk_load')
                    for th in range(2):
                        b = nextbank()
                        proj_fm_matmuls(s, th, b)
                        ck('pk_mm')
                        flush_rot()
                        if th == 1:
                            ck('pk_rot')
                        if kind == "q":
                            dest = QT[:, hc, th * 512:(th + 1) * 512]
                            dres = ("QT", hc, th)
                        else:
                            dest = KT[:, hc, tokoff + th * 512: tokoff + (th + 1) * 512]
                            dres = ("KT", hc, ipass, th)
                        rotary_tile(b, dest, dres, th)
            flush_rot()

            if not own:
                P.add("dve", lambda e: e.memset(Vaug[:, :, :, 128:130], 1.0), writes=["Vones"])
            for vc in range(4):
                s = load_w("v", w_v[vc, :, :, :])
                for tbl in range(8):
                    b = nextbank()
                    for dc in range(NDC):
                        P.add("pe", lambda e, dc=dc, s=s, tbl=tbl, b=b: e.matmul(
                            ps[b][:, 0:256], lhsT=hT[:, dc, tbl * 128:(tbl + 1) * 128], rhs=wr_v[s][:, dc, :],
                            start=(dc == 0), stop=(dc == NDC - 1)), reads=[("w", s), "hT"], writes=[("ps", b)])
                    for hh in range(2):
                        P.add("act", lambda e, b=b, hh=hh, vc=vc, tbl=tbl, tboff=tboff: e.activation(
                            out=Vaug[:, tboff + tbl, 2 * vc + hh, 0:128], in_=ps[b][:, hh * 128:(hh + 1) * 128], func=AF.Copy),
                            reads=[("ps", b)], writes=[("V", tboff + tbl, 2 * vc + hh)])
            if not own:
                ck('projprev')

        ck('projown')
        P.fence()

        for r in range(2):
            for c in range(2):
                P.add("dve", lambda e, r=r, c=c: e.memset(PTd[r][c][:, :], 0.0), writes=[("PTd", r)])
        ptc = [0]
        NPT = len(PT)

        def emit_qk(g):
            h, i, kbs, bS = g["h"], g["i"], g["kbs"], g["bS"]
            for idx, kb in enumerate(kbs):
                for c in range(2):
                    P.add("pe", lambda e, c=c, idx=idx, kb=kb, h=h, i=i, bS=bS: e.matmul(
                        ps[bS[c]][:, idx * 128:(idx + 1) * 128],
                        lhsT=KT[c * 64:(c + 1) * 64, h, kb * 128:(kb + 1) * 128],
                        rhs=QT[c * 64:(c + 1) * 64, h, i * 128:(i + 1) * 128], start=True, stop=True),
                        reads=[("QT", h, i // 4), ("KT", h, kb // 8, (kb % 8) // 4)], writes=[("ps", bS[c])])

        def emit_exp(g):
            bS, kbs, u2 = g["bS"], g["kbs"], g["u2"]
            W = len(kbs) * 128
            if g["diag"]:
                for c in range(2):
                    P.add("act", lambda e, c=c, u2=u2, bS=bS: e.activation(
                        out=PTd[u2][c][0:64, 0:128], in_=ps[bS[c]][0:64, 0:128], func=AF.Exp, scale=0.125),
                        reads=[("ps", bS[c])], writes=[("PTd", u2)])
                    P.add("act", lambda e, c=c, u2=u2, bS=bS: e.activation(
                        out=PTd[u2][c][64:128, 64:128], in_=ps[bS[c]][64:128, 64:128], func=AF.Exp, scale=0.125),
                        reads=[("ps", bS[c]), ("PTd", u2)], writes=[("PTd", u2)])
                return
            prs = []
            for c in range(2):
                pr = ptc[0] % NPT
                ptc[0] += 1
                prs.append(pr)
                if g["prev"]:
                    P.add("act", lambda e, c=c, pr=pr, bS=bS, W=W: e.activation(
                        out=PT[pr][:, 0:W], in_=ps[bS[c]][:, 0:W], func=AF.Exp, bias=cs(C_PB), scale=0.125),
                        reads=[("ps", bS[c]), "cst"], writes=[("PT", pr)])
                else:
                    P.add("act", lambda e, c=c, pr=pr, bS=bS, W=W: e.activation(
                        out=PT[pr][:, 0:W], in_=ps[bS[c]][:, 0:W], func=AF.Exp, scale=0.125),
                        reads=[("ps", bS[c])], writes=[("PT", pr)])
            g["prs"] = prs

        def emit_av(g):
            h, kbs, bO, u2 = g["h"], g["kbs"], g["bO"], g["u2"]
            if g["diag"]:
                kb = kbs[0]
                for c in range(2):
                    P.add("pe", lambda e, c=c, kb=kb, h=h, u2=u2, bO=bO: e.matmul(
                        ps[bO[c]][:, c * 256:c * 256 + 129], lhsT=PTd[u2][c][:, :], rhs=Vaug[:, kb, h, 0:129],
                        start=False, stop=True, skip_group_check=True),
                        reads=[("PTd", u2), ("V", kb, h), "Vones"], writes=[("ps", bO[c])])
                return
            prs = g["prs"]
            for idx, kb in enumerate(kbs):
                for c in range(2):
                    st = g["first"] and idx == 0 and c == 0
                    P.add("pe", lambda e, c=c, idx=idx, kb=kb, h=h, pr=prs[c], st=st, bO=bO: e.matmul(
                        ps[bO[c]][:, c * 256:c * 256 + 129], lhsT=PT[pr][:, idx * 128:(idx + 1) * 128], rhs=Vaug[:, kb, h, 0:129],
                        start=st, stop=False, skip_group_check=True),
                        reads=[("PT", prs[c]), ("V", kb, h), "Vones"], writes=[("ps", bO[c])])

        def epilogue_stages(h, i, u2, bO):
            E = esm[u2]
            er = ("esm", u2)

            def stA():
                P.add("dve", lambda e: e.reciprocal(out=E[:, 0:1], in_=ps[bO[0]][:, 128:129]),
                      reads=[("ps", bO[0])], writes=[er])
                P.add("dve", lambda e: e.reciprocal(out=E[:, 1:2], in_=ps[bO[1]][:, 384:385]),
                      reads=[("ps", bO[1]), er], writes=[er])
                P.add("dve", lambda e: e.tensor_tensor(out=E[:, 2:3], in0=E[:, 1:2], in1=neglam, op=ALU.mult),
                      reads=[er, "neglam"], writes=[er])
                P.add("dve", lambda e: e.tensor_scalar(
                    out=o_a[u2][:, :], in0=ps[bO[0]][:, 0:128], scalar1=E[:, 0:1], scalar2=None, op0=ALU.mult),
                    reads=[("ps", bO[0]), er], writes=[("oa", u2)])
                P.add("dve", lambda e: e.scalar_tensor_tensor(
                    out=o_b[u2][:, :], in0=ps[bO[1]][:, 256:384], scalar=E[:, 2:3], in1=o_a[u2][:, :], op0=ALU.mult, op1=ALU.add),
                    reads=[("ps", bO[1]), er, ("oa", u2)], writes=[("ob", u2)])
                P.add("dve", lambda e: e.scalar_tensor_tensor(
                    out=o_a[u2][:, :], in0=o_b[u2][:, :], scalar=1.0, in1=o_b[u2][:, :], op0=ALU.mult, op1=ALU.mult,
                    accum_out=E[:, 3:4]), reads=[("ob", u2), ("oa", u2)], writes=[("oa", u2), er])

            def stB():
                P.add("act", lambda e: e.activation(out=E[:, 4:5], in_=E[:, 3:4], func=AF.Ln, scale=1.0 / 128, bias=EPS),
                      reads=[er], writes=[er])
                P.add("act", lambda e: e.activation(out=E[:, 5:6], in_=E[:, 4:5], func=AF.Exp, scale=-0.5),
                      reads=[er], writes=[er])
                P.add("dve", lambda e: e.scalar_tensor_tensor(
                    out=o_c[u2][:, :], in0=o_b[u2][:, :], scalar=E[:, 5:6], in1=cs(C_GREP, 128), op0=ALU.mult, op1=ALU.mult),
                    reads=[("ob", u2), er, "cst"], writes=[("oc", u2)])

            def stC():
                bt = bO[0]
                P.add("pe", lambda e: e.transpose(out=psb[bt][:, 0:128], in_=o_c[u2][:, :], identity=ident_b[:, :]),
                      reads=[("oc", u2), "ident"], writes=[("ps", bt)])
                P.add("act", lambda e: e.activation(
                    out=mixA[:, h, i * 128:(i + 1) * 128], in_=psb[bt][:, 0:128], func=AF.Copy),
                    reads=[("ps", bt)], writes=[("mixA", h, i // 4)])
            return [stA, stB, stC]

        glist = []
        unit = 0
        for h in range(8):
            for i in range(8):
                u2 = unit % 2
                bO = (u2, u2)
                gs = [([0, 1, 2, 3], True), ([4, 5, 6, 7], True)]
                ownk = list(range(8, 8 + i))
                while ownk:
                    gs.append((ownk[:4], False))
                    ownk = ownk[4:]
                ug = []
                for gi, (kbs, isprev) in enumerate(gs):
                    ug.append(dict(h=h, i=i, u2=u2, bO=bO, kbs=kbs, prev=isprev, diag=False, first=(gi == 0), last=False, gi=gi))
                ug.append(dict(h=h, i=i, u2=u2, bO=bO, kbs=[8 + i], prev=False, diag=True, first=False, last=True, gi=len(gs)))
                glist.extend(ug)
                unit += 1

        pend = {}

        def start_group(g):
            g["bS"] = (nextbank(2, 8), nextbank(2, 8))
            emit_qk(g)

        start_group(glist[0])
        start_group(glist[1])
        for n, g in enumerate(glist):
            if n + 2 < len(glist):
                start_group(glist[n + 2])
            emit_exp(g)
            emit_av(g)
            for stg in pend.pop(g["gi"], []):
                stg()
            if g["last"]:
                for k_ in sorted(pend):
                    for stg in pend[k_]:
                        stg()
                pend.clear()
                sA, sB, sC = epilogue_stages(g["h"], g["i"], g["u2"], g["bO"])
                sA()
                pend = {0: [sB], 1: [sC]}
        for k_ in sorted(pend):
            for stg in pend[k_]:
                stg()

        ck('att')
        P.fence()

        for c in range(16):
            s = load_w("a", w_o[c, :, :, :])
            r = c % 2
            for th in range(2):
                P.add("sp", lambda e, c=c, r=r, th=th: e.dma_start(
                    out=xs[r][:, th * 512:(th + 1) * 512], in_=xT_own[:, c, th * 512:(th + 1) * 512]),
                    writes=[("xs", r, th)], dma=("xs", r, th))
            for th in range(2):
                b = nextbank()
                for fc in range(16):
                    rhs = (mixP if fc < 8 else mixA)
                    P.add("pe", lambda e, fc=fc, s=s, th=th, b=b, rhs=rhs: e.matmul(
                        ps[b][:, :], lhsT=wr_a[s][:, fc, :], rhs=rhs[:, fc % 8, th * 512:(th + 1) * 512],
                        start=(fc == 0), stop=(fc == 15)),
                        reads=[("w", s), (("mixP", fc, th) if fc < 8 else ("mixA", fc - 8, th))], writes=[("ps", b)])
                P.add("dve", lambda e, c=c, th=th, b=b, r=r: e.tensor_tensor(
                    out=X1[:, c, th * 512:(th + 1) * 512], in0=ps[b][:, :], in1=xs[r][:, th * 512:(th + 1) * 512], op=ALU.add),
                    reads=[("ps", b), ("xs", r, th)], writes=[("X1", c)])

        ck('wout')
        P.fence()

        def get_x1(dc, th):
            return X1[:, dc, th * 512:(th + 1) * 512], [("X1", dc)]
        norm_stats(get_x1, sqbC, rstdC, "n2")
        for dc in range(NDC):
            P.add("dve", lambda e, dc=dc: e.scalar_tensor_tensor(
                out=h2T[:, dc, :], in0=X1[:, dc, :], scalar=cs(C_G2 + dc), in1=rstdC[:, :], op0=ALU.mult, op1=ALU.mult),
                reads=[("X1", dc), "n2rstdF", "cst"], writes=["h2T"])

        ck('n2')
        sgc = [0]
        for G in range(NG):
            hb = hTg[G % 2]
            for jj in range(JG):
                j = G * JG + jj
                s = load_w("gu", w_gu[j, :, :, :, :])
                bks = {}
                for gu in range(2):
                    for th in range(2):
                        b = nextbank()
                        bks[(gu, th)] = b
                        for dc in range(NDC):
                            P.add("pe", lambda e, dc=dc, s=s, gu=gu, th=th, b=b: e.matmul(
                                ps[b][:, :], lhsT=wr_gu[s][:, gu, dc, :], rhs=h2T[:, dc, th * 512:(th + 1) * 512],
                                start=(dc == 0), stop=(dc == NDC - 1)),
                                reads=[("w", s), "h2T"], writes=[("ps", b)])
                for th in range(2):
                    r = sgc[0] % 2
                    sgc[0] += 1
                    bg, bu = bks[(0, th)], bks[(1, th)]
                    P.add("act", lambda e, r=r, bg=bg: e.activation(out=sg[r][:, :], in_=ps[bg][:, :], func=AF.Silu),
                          reads=[("ps", bg)], writes=[("sg", r)])
                    P.add("dve", lambda e, r=r, bu=bu, jj=jj, th=th, hb=hb: e.tensor_tensor(
                        out=hb[:, jj, th * 512:(th + 1) * 512], in0=sg[r][:, :], in1=ps[bu][:, :], op=ALU.mult),
                        reads=[("sg", r), ("ps", bu)], writes=[("hTg", G % 2, th)])
            for cp in range(8):
                s = load_w("dn", w_dn[G, cp, :, :, :])
                for cc in range(2):
                    c = cp * 2 + cc
                    for th in range(2):
                        b = nextbank()
                        for jj in range(JG):
                            P.add("pe", lambda e, jj=jj, s=s, cc=cc, th=th, b=b, hb=hb: e.matmul(
                                ps[b][:, :], lhsT=wr_dn[s][:, jj, cc * 128:(cc + 1) * 128], rhs=hb[:, jj, th * 512:(th + 1) * 512],
                                start=(jj == 0), stop=(jj == JG - 1)),
                                reads=[("w", s), ("hTg", G % 2, th)], writes=[("ps", b)])
                        P.add("dve", lambda e, c=c, th=th, b=b: e.tensor_tensor(
                            out=X1[:, c, th * 512:(th + 1) * 512], in0=X1[:, c, th * 512:(th + 1) * 512], in1=ps[b][:, :], op=ALU.add),
                            reads=[("ps", b), ("X1", c)], writes=[("X1", c)])

        ck('ffn')
        norm_stats(get_x1, sqbC, rstdC, "n3")
        for dc in range(NDC):
            P.add("dve", lambda e, dc=dc: e.scalar_tensor_tensor(
                out=X1[:, dc, :], in0=X1[:, dc, :], scalar=cs(C_G3 + dc), in1=rstdC[:, :], op0=ALU.mult, op1=ALU.mult),
                reads=[("X1", dc), "n3rstdF", "cst"], writes=[("X1", dc)])
            P.add("sp", lambda e, dc=dc: e.dma_start(out=outT[:, dc, :], in_=X1[:, dc, :]),
                  reads=[("X1", dc)], writes=[("out", dc)], dma="out")
        P.add("sp", None, reads=[("out", dc) for dc in range(NDC)])

    try:
        body()
    except _Stop:
        dump(stop)

    with contextlib.ExitStack() as stack:
        P.finalize(nc, stack)
        with nc.Block() as block:
            @block.tensor
            def _(e):
                P.emit("pe", e)

            @block.vector
            def _(e):
                P.emit("dve", e)

            @block.scalar
            def _(e):
                P.emit("act", e)

            @block.gpsimd
            def _(e):
                P.emit("pool", e)

            @block.sync
            def _(e):
                P.emit("sp", e)
    return nc


def _fm(a):
    t = a.shape[0]
    return np.ascontiguousarray(a.T.reshape(NDC, 128, t).transpose(1, 0, 2))


def prepare_inputs(x, norm_mix, w_in, w_pool, pool_scale, lambda_q1, lambda_k1, lambda_q2, lambda_k2,
                   subln_gain, w_out, norm_ffn, w_gate_up, w_down, norm_final):
    f = np.float32
    x = np.asarray(x, f)
    w_in = np.asarray(w_in, f)[0]
    w_pool = np.asarray(w_pool, f)[0]
    w_out = np.asarray(w_out, f)[0]
    w_gate_up = np.asarray(w_gate_up, f)[0]
    w_down = np.asarray(w_down, f)[0]

    def colchunks(w, ncol):
        C = w.shape[1]
        return np.ascontiguousarray(w.reshape(NDC, 128, C // ncol, ncol).transpose(2, 1, 0, 3))

    w_uqk = colchunks(w_in[:, :3072], 128)
    w_v = colchunks(w_in[:, 3072:], 256)
    w_pool_r = np.ascontiguousarray(w_pool.reshape(4, 2, 128, 256).transpose(2, 0, 1, 3))
    w_o = colchunks(w_out, 128)
    w_gu = np.ascontiguousarray(w_gate_up.reshape(NDC, 128, 2, NJ, 128).transpose(3, 1, 2, 0, 4))
    w_dn = np.ascontiguousarray(w_down.reshape(NG, JG, 128, 8, 256).transpose(0, 3, 2, 1, 4))

    inv_freq = (np.float32(10000.0) ** (-(np.arange(0, 64, 2, dtype=f) / f(64)))).astype(f)
    p_idx = np.arange(128)
    fr = inv_freq[p_idx % 32]
    sign = np.where((p_idx % 64) < 32, -1.0, 1.0).astype(f)
    partner = np.where((p_idx % 64) < 32, p_idx + 32, p_idx - 32)
    Rm = np.zeros((128, 128), f)
    Rm[partner, p_idx] = 1.0

    base = np.zeros((128, C_N), f)
    base[:, C_G1:C_G1 + 16] = np.asarray(norm_mix, f)[0].reshape(16, 128).T
    base[:, C_G2:C_G2 + 16] = np.asarray(norm_ffn, f)[0].reshape(16, 128).T
    base[:, C_G3:C_G3 + 16] = np.asarray(norm_final, f).reshape(16, 128).T
    base[:, C_PS:C_PS + 8] = np.asarray(pool_scale, f)[0].reshape(8, 128).T
    base[:, C_LQ1:C_LQ1 + 64] = np.asarray(lambda_q1, f)[0][None, :]
    base[:, C_LK1:C_LK1 + 64] = np.asarray(lambda_k1, f)[0][None, :]
    base[:, C_LQ2:C_LQ2 + 64] = np.asarray(lambda_q2, f)[0][None, :]
    base[:, C_LK2:C_LK2 + 64] = np.asarray(lambda_k2, f)[0][None, :]
    base[:, C_GREP:C_GREP + 128] = np.asarray(subln_gain, f)[0][None, :]
    base[:, C_ONES:C_ONES + 128] = 1.0
    base[:, C_ID:C_ID + 128] = np.eye(128, dtype=f)
    base[:, C_RM:C_RM + 128] = Rm

    shared = dict(w_uqk=w_uqk, w_v=w_v, w_pool=w_pool_r, w_o=w_o, w_gu=w_gu, w_dn=w_dn)
    in_maps = []
    for core in range(8):
        b, half = core // 2, core % 2
        own0 = half * S_OWN
        xo = _fm(x[b, own0:own0 + S_OWN])
        if half == 1:
            xp = _fm(x[b, 0:S_OWN])
        else:
            xp = np.zeros_like(xo)
        rot = np.zeros((128, 2, 2, 1024), f)
        for ip in range(2):
            pos = (own0 - S_OWN + ip * S_OWN + np.arange(S_OWN)).astype(f)
            ang = (pos[None, :] * fr[:, None]).astype(f)
            rot[:, ip, 0, :] = np.cos(ang)
            rot[:, ip, 1, :] = np.sin(ang) * sign[:, None]
        c = base.copy()
        c[:, C_PB] = 0.0 if half == 1 else NEG_BIG
        for g, w in enumerate((2, 4, 8, 16)):
            posn = own0 + np.arange(16)
            c[:, C_IC + 16 * g:C_IC + 16 * (g + 1)] = (1.0 / np.minimum(posn + 1, w).astype(f))[None, :]
        m = dict(shared)
        m.update(xT_own=xo, xT_prev=xp, rot=rot, cst=c)
        in_maps.append(m)
    return in_maps


_NC_CACHE = {}


def kernel(x, norm_mix, w_in, w_pool, pool_scale, lambda_q1, lambda_k1, lambda_q2, lambda_k2,
           subln_gain, w_out, norm_ffn, w_gate_up, w_down, norm_final):
    in_maps = prepare_inputs(x, norm_mix, w_in, w_pool, pool_scale, lambda_q1, lambda_k1, lambda_q2, lambda_k2,
                             subln_gain, w_out, norm_ffn, w_gate_up, w_down, norm_final)
    if "nc" not in _NC_CACHE:
        _NC_CACHE["nc"] = build_program()
    nc = _NC_CACHE["nc"]
    res = run_bass_kernel_spmd(nc, in_maps, core_ids=list(range(8)))
    out = np.empty((4, 2048, D), np.float32)
    for core in range(8):
        b, half = core // 2, core % 2
        o = np.asarray(res.results[core]["outT"], np.float32)
        out[b, half * S_OWN:(half + 1) * S_OWN, :] = o.transpose(2, 1, 0).reshape(S_OWN, D)
    return out
```

```python
import contextlib
import math

import numpy as np
import concourse.bass as bass
import concourse.mybir as mybir
from concourse.bass_utils import run_bass_kernel_spmd

F32 = mybir.dt.float32
BF16 = mybir.dt.bfloat16
AF = mybir.ActivationFunctionType
ALU = mybir.AluOpType
AX = mybir.AxisListType

D = 2048
NDC = 16
S_OWN = 1024
FFN = 5632
NJ = 44
NG = 4
JG = 11
EPS = 1e-6
LAM_INIT = 0.8 - 0.6 * math.exp(0.0)
NEG_BIG = -30000.0

C_G1, C_G2, C_G3 = 0, 16, 32
C_PS = 48
C_PB = 56
C_IC = 57
C_LQ1, C_LK1, C_LQ2, C_LK2 = 121, 185, 249, 313
C_GREP = 377
C_ONES = 505
C_ID = 633
C_RM = 761
C_N = 896

NS = 4


class Op:
    __slots__ = ("id", "eng", "idx", "fn", "deps", "dma", "signal", "cum", "ordinal", "ndma")


class Prog:
    ENGS = ("pe", "act", "dve", "pool", "sp")

    def __init__(self):
        self.ops = []
        self.eng_ops = {e: [] for e in self.ENGS}
        self.last_w = {}
        self.readers = {}
        self.pending_fence = {}
        self.last_dma = {}
        self.aggregate = set()

    def add(self, eng, fn, reads=(), writes=(), dma=None, ndma=1):
        op = Op()
        op.ndma = ndma
        op.id = len(self.ops)
        op.eng = eng
        op.fn = fn
        op.dma = dma
        op.signal = False
        writes = list(writes) + [r for r in reads if isinstance(r, tuple) and r[0] == "ps" and r not in writes]
        deps = set()
        for r in reads:
            lw = self.last_w.get(r)
            if lw is not None:
                deps.add(lw)
        for w in writes:
            lw = self.last_w.get(w)
            if lw is not None:
                deps.add(lw)
            rs = self.readers.get(w)
            if rs:
                deps.update(rs)
        for r in reads:
            self.readers.setdefault(r, []).append(op.id)
        for w in writes:
            self.last_w[w] = op.id
            self.readers[w] = []
        pf = self.pending_fence.pop(eng, None)
        if pf:
            deps.update(pf)
        deps.discard(op.id)
        if eng == "pe" and dma is None:
            deps = {d for d in deps if not (self.ops[d].eng == "pe" and self.ops[d].dma is None)}
        op.deps = deps
        op.idx = len(self.eng_ops[eng])
        self.eng_ops[eng].append(op)
        self.ops.append(op)
        if dma is not None:
            self.last_dma[dma] = op.id
        return op.id

    def fence(self):
        deps = set()
        for e in self.ENGS:
            lst = self.eng_ops[e]
            for o in reversed(lst):
                if o.dma is None and o.fn is not None:
                    deps.add(o.id)
                    break
        deps.update(self.last_dma.values())
        for e in self.ENGS:
            self.pending_fence.setdefault(e, set()).update(deps)

    def finalize(self, nc, stack):
        for op in self.ops:
            for d in op.deps:
                t = self.ops[d]
                if t.dma is None:
                    t.signal = True
        self.eng_sem = {}
        for e in self.ENGS:
            self.eng_sem[e] = stack.enter_context(nc.semaphore("s_" + e))
            c = 0
            for o in self.eng_ops[e]:
                if o.dma is None and o.signal:
                    c += 1
                o.cum = c
        self.dma_sem = {}
        counts = {}
        for op in self.ops:
            if op.dma is not None:
                k = op.dma
                if k not in self.dma_sem:
                    nm = "d_" + "_".join(str(x) for x in (k if isinstance(k, tuple) else (k,)))
                    self.dma_sem[k] = stack.enter_context(nc.semaphore(nm))
                counts[k] = counts.get(k, 0) + op.ndma
                op.ordinal = counts[k]
        self.dma_total = counts

    def wait_of(self, d):
        t = self.ops[d]
        if t.dma is None:
            return (self.eng_sem[t.eng], t.cum)
        k = t.dma
        n = self.dma_total[k] if k in self.aggregate else t.ordinal
        return (self.dma_sem[k], 16 * n)

    def emit(self, eng, e):
        known = {}
        for op in self.eng_ops[eng]:
            ws = {}
            for d in op.deps:
                sem, val = self.wait_of(d)
                key = id(sem)
                if ws.get(key, (None, 0))[1] < val:
                    ws[key] = (sem, val)
            for key, (sem, val) in ws.items():
                if known.get(key, 0) < val:
                    e.wait_ge(sem, val)
                    known[key] = val
            if op.fn is None:
                continue
            ins = op.fn(e)
            if op.dma is not None:
                for one in (ins if isinstance(ins, list) else [ins]):
                    one.then_inc(self.dma_sem[op.dma], 16)
            elif op.signal:
                ins.then_inc(self.eng_sem[eng], 1)


class _Stop(Exception):
    pass


def build_program(stop=None):
    nc = bass.Bass("TRN2", target_bir_lowering=False)
    dt = nc.dram_tensor
    xT_own = dt("xT_own", [128, NDC, S_OWN], F32, kind="ExternalInput")
    xT_prev = dt("xT_prev", [128, NDC, S_OWN], F32, kind="ExternalInput")
    rot_d = dt("rot", [128, 2, 2, 1024], F32, kind="ExternalInput")
    cst_d = dt("cst", [128, C_N], F32, kind="ExternalInput")
    w_uqk = dt("w_uqk", [24, 128, 16, 128], F32, kind="ExternalInput")
    w_v = dt("w_v", [4, 128, 16, 256], F32, kind="ExternalInput")
    w_pool = dt("w_pool", [128, 4, 2, 256], F32, kind="ExternalInput")
    w_o = dt("w_o", [16, 128, 16, 128], F32, kind="ExternalInput")
    early = stop in ("setup", "pk_load", "pk_mm", "pk_rot", "n1prev", "projprev", "projown", "att", "wout", "n2")
    w_gu = None if early else dt("w_gu", [NJ, 128, 2, 16, 128], F32, kind="ExternalInput")
    w_dn = None if early else dt("w_dn", [NG, 8, 128, JG, 256], F32, kind="ExternalInput")
    outT = dt("outT", [128, NDC, S_OWN], F32, kind="ExternalOutput")

    BASE = 16640
    cnt = [0]

    def sb(name, shape, dtype, off):
        cnt[0] += 1
        return nc.alloc_sbuf_tensor_at(f"{name}_{cnt[0]}", list(shape), dtype, offset=BASE + off)

    cst = sb("cst", [128, C_N], F32, 0)
    ones_b = sb("ones", [128, 128], BF16, 3584)
    ident_b = sb("ident", [128, 128], BF16, 3840)
    Rm_b = sb("rm", [128, 128], BF16, 4096)
    small = sb("small", [128, 256], F32, 4352)
    wp = sb("wp", [128, 4, 2, 256], BF16, 5376)
    W0 = 9472
    wr_a = [sb("wra", [128, 16, 128], BF16, W0 + s * 8192) for s in range(NS)]
    wr_v = [sb("wrv", [128, 16, 256], BF16, W0 + s * 8192) for s in range(NS)]
    wr_gu = [sb("wrgu", [128, 2, 16, 128], BF16, W0 + s * 8192) for s in range(NS)]
    wr_dn = [sb("wrdn", [128, JG, 256], BF16, W0 + s * 8192) for s in range(NS)]
    RH = 42240
    hT = sb("hT", [128, 16, 1024], BF16, RH)
    mixA = sb("mixA", [128, 8, 1024], BF16, RH)
    ATS = RH + 16384
    PT = [sb("PT", [128, 512], BF16, ATS + r * 1024) for r in range(8)]
    PTd = [[sb("PTd", [128, 128], BF16, ATS + 8192 + (r * 2 + c) * 256) for c in range(2)] for r in range(2)]
    PT2 = [sb("PT2", [128, 2, 512], BF16, ATS + k * 2048) for k in range(4)]
    PTd2 = [sb("PTd2", [128, 2, 128], BF16, ATS + 8192 + r * 512) for r in range(2)]
    o_a = [sb("oa", [128, 128], F32, ATS + 9216 + r * 512) for r in range(2)]
    o_b = [sb("ob", [128, 128], F32, ATS + 10240 + r * 512) for r in range(2)]
    o_c = [sb("oc", [128, 128], BF16, ATS + 11264 + r * 256) for r in range(2)]
    esm = [sb("esm", [128, 16], F32, ATS + 11776 + r * 64) for r in range(2)]
    h2T = sb("h2T", [128, 16, 1024], BF16, RH)
    RK = 75008
    KT = sb("KT", [128, 8, 2048], BF16, RK)
    QT = sb("QT", [128, 8, 1024], BF16, RK + 32768)
    Vaug = sb("Vaug", [128, 16, 8, 130], BF16, RK + 49152)
    X1 = sb("X1", [128, 16, 1024], F32, RK)
    CS = RK + 65536
    sqbC = [sb("sqbC", [128, 1024], BF16, CS + r * 2048) for r in range(2)]
    rstdC = sb("rstdC", [128, 1024], F32, CS + 4096)
    sg = [sb("sg", [128, 512], F32, CS + 8192 + r * 2048) for r in range(2)]
    MP = 157440
    mixP = sb("mixP", [128, 8, 1024], BF16, MP)
    hTg = [sb("hTg", [128, JG, 1024], BF16, MP + r * 22528) for r in range(2)]
    RT = 173824
    rot = sb("rot", [128, 2, 1024], F32, RT)
    hThalo = sb("hThalo", [128, 16, 16], BF16, 182016)
    SCR = 182528
    xs = [sb("xs", [128, 1024], F32, SCR + r * 4096) for r in range(2)]
    sqb = [sb("sqb", [128, 1024], BF16, SCR + 8192 + r * 2048) for r in range(2)]
    rstd = sb("rstd", [128, 1024], F32, SCR + 12288)
    ubuf = sb("ubuf", [128, 1040], F32, SCR)
    pa = [sb("pa", [128, 1040], F32, SCR + 4160 * (r + 1)) for r in range(2)]
    pooled = [sb("pooled", [128, 1024], BF16, SCR + 12480 + r * 2048) for r in range(2)]
    tmp16 = sb("tmp16", [128, 16], F32, SCR + 16576)
    RS = 199168
    qb = [sb("qb", [128, 512], BF16, RS + r * 1024) for r in range(2)]
    t1 = [sb("t1", [128, 512], F32, RS + 2048 + r * 2048) for r in range(2)]
    t2 = [sb("t2", [128, 512], F32, RS + 6144 + r * 2048) for r in range(2)]
    assert BASE + RS + 10240 <= 229344

    class _BankView:
        def __init__(self, t, c):
            self.t, self.c = t, c

        def __getitem__(self, idx):
            return self.t[idx[0], self.c, idx[1]]

    ps01 = [nc.alloc_psum_tensor(f"ps{b}", [128, 512], F32) for b in range(2)]
    pspair = [nc.alloc_psum_tensor(f"pp{k}", [128, 2, 512], F32) for k in range(3)]
    ps = list(ps01) + [_BankView(pspair[k], c) for k in range(3) for c in range(2)]
    psb = [p.bitcast(BF16) for p in ps01]

    P = Prog()
    P.aggregate.update(["const", "out"])
    bank = [0]

    def nextbank(lo=0, hi=8):
        b = lo + (bank[0] % (hi - lo))
        bank[0] += 1
        return b

    wslot = [0]

    def load_w(kind, src):
        s = wslot[0] % NS
        wslot[0] += 1
        if kind == "a":
            P.add("pool", lambda e, s=s, src=src: e.dma_start(out=wr_a[s][:, :, :], in_=src, max_dma_last_dim=4096),
                  writes=[("w", s)], dma=("w", s))
        elif kind == "v":
            P.add("pool", lambda e, s=s, src=src: e.dma_start(out=wr_v[s][:, :, :], in_=src, max_dma_last_dim=4096),
                  writes=[("w", s)], dma=("w", s))
        elif kind == "gu":
            P.add("pool", lambda e, s=s, src=src: [
                e.dma_start(out=wr_gu[s][:, 0, :, :], in_=src[:, 0, :, :], max_dma_last_dim=4096),
                e.dma_start(out=wr_gu[s][:, 1, :, :], in_=src[:, 1, :, :], max_dma_last_dim=4096)],
                writes=[("w", s)], dma=("w", s), ndma=2)
        elif kind == "dn":
            P.add("pool", lambda e, s=s, src=src: e.dma_start(out=wr_dn[s][:, :, :], in_=src, max_dma_last_dim=4096),
                  writes=[("w", s)], dma=("w", s))
        return s

    def cs(c0, n=1):
        return cst[:, c0:c0 + n]

    def ck(name):
        if stop == name:
            raise _Stop()

    def dump(name):
        srcs = {
            "setup": [(small[:, :], outT[:, 0, 0:256])],
            "pk_load": [(wr_a[0][:, dc, :], outT[:, dc, 0:128]) for dc in range(16)],
            "pk_mm": [(hT[:, 0, :], outT[:, 0, :])],
            "pk_rot": [(KT[:, 0, 0:1024], outT[:, 0, :])],
            "n1prev": [(rstd[:, :], outT[:, 0, :])] + [(hT[:, dc, :], outT[:, 1 + dc % 15, :]) for dc in (0, 5)],
            "projprev": [(KT[:, h, 0:1024], outT[:, h, :]) for h in range(8)] +
                        [(Vaug[:, tb, :, 0:128], outT[:, 8 + tb, :]) for tb in range(8)],
            "projown": [(KT[:, 0, 1024:2048], outT[:, 0, :]), (KT[:, 7, 0:1024], outT[:, 3, :])] +
                       [(QT[:, h, :], outT[:, 1 + h, :]) for h in (0, 1)] +
                       [(mixP[:, c, :], outT[:, 4 + c, :]) for c in range(8)] +
                       [(Vaug[:, tb, :, 0:128], outT[:, 12 + i_, :]) for i_, tb in enumerate((8, 9, 3, 15))],
            "att": [(mixA[:, h, :], outT[:, h, :]) for h in range(8)] + [(mixP[:, c, :], outT[:, 8 + c, :]) for c in range(8)],
            "wout": [(X1[:, c, :], outT[:, c, :]) for c in range(16)],
            "n2": [(X1[:, c, :], outT[:, c, :]) for c in range(8)] + [(h2T[:, c, :], outT[:, 8 + c, :]) for c in range(8)],
            "ffn": [(X1[:, c, :], outT[:, c, :]) for c in range(16)],
        }[name]
        P.fence()
        for i, (src, dst) in enumerate(srcs):
            P.add("pool", lambda e, src=src, dst=dst: e.dma_start(out=dst, in_=src, max_dma_last_dim=2048), writes=[("out", i)], dma="out")
        P.add("sp", None, reads=[("out", i) for i in range(len(srcs))])

    def body():
        P.add("sp", lambda e: e.dma_start(out=cst[:, :], in_=cst_d[:, :]), writes=["cst"], dma="const")
        P.add("pool", lambda e: e.dma_start(out=wp[:, :, :, :], in_=w_pool[:, :, :, :], max_dma_last_dim=4096),
              writes=["wp"], dma="wp")
        P.add("dve", lambda e: e.tensor_copy(out=ones_b[:, :], in_=cs(C_ONES, 128)), reads=["cst"], writes=["ones"])
        P.add("dve", lambda e: e.tensor_copy(out=ident_b[:, :], in_=cs(C_ID, 128)), reads=["cst"], writes=["ident"])
        P.add("dve", lambda e: e.tensor_copy(out=Rm_b[:, :], in_=cs(C_RM, 128)), reads=["cst"], writes=["rm"])
        P.add("dve", lambda e: e.tensor_tensor(out=small[:, 0:64], in0=cs(C_LQ1, 64), in1=cs(C_LK1, 64), op=ALU.mult),
              reads=["cst"], writes=["sm0"])
        P.add("dve", lambda e: e.reduce_sum(out=small[:, 128:129], in_=small[:, 0:64], axis=AX.X), reads=["sm0"], writes=["sm1"])
        P.add("dve", lambda e: e.tensor_tensor(out=small[:, 64:128], in0=cs(C_LQ2, 64), in1=cs(C_LK2, 64), op=ALU.mult),
              reads=["cst"], writes=["sm2"])
        P.add("dve", lambda e: e.reduce_sum(out=small[:, 129:130], in_=small[:, 64:128], axis=AX.X), reads=["sm2"], writes=["sm3"])
        P.add("act", lambda e: e.activation(out=small[:, 130:132], in_=small[:, 128:130], func=AF.Exp),
              reads=["sm1", "sm3"], writes=["sm4"])
        P.add("dve", lambda e: e.tensor_tensor(out=small[:, 132:133], in0=small[:, 131:132], in1=small[:, 130:131], op=ALU.subtract),
              reads=["sm4"], writes=["sm5"])
        P.add("dve", lambda e: e.tensor_scalar_add(out=small[:, 133:134], in0=small[:, 132:133], scalar1=-LAM_INIT),
              reads=["sm5"], writes=["neglam"])
        neglam = small[:, 133:134]
        P.add("dve", lambda e: e.tensor_scalar_mul(out=cst[:, C_GREP:C_GREP + 128], in0=cst[:, C_GREP:C_GREP + 128],
                                                   scalar1=(1.0 - LAM_INIT)), reads=["cst"], writes=["cst"])

        ck('setup')
        def norm_stats(get_src, sq_bufs, rstd_buf, tag):
            b0, b1 = nextbank(), nextbank()
            for dc in range(NDC):
                r = dc % 2
                for th, b in ((0, b0), (1, b1)):
                    ap, res = get_src(dc, th)
                    P.add("act", lambda e, ap=ap, r=r, th=th: e.activation(
                        out=sq_bufs[r][:, th * 512:(th + 1) * 512], in_=ap, func=AF.Square),
                        reads=res, writes=[(tag + "sq", r, th)])
                    P.add("pe", lambda e, r=r, th=th, b=b, dc=dc: e.matmul(
                        ps[b][:, :], lhsT=ones_b[:, :], rhs=sq_bufs[r][:, th * 512:(th + 1) * 512],
                        start=(dc == 0), stop=(dc == NDC - 1)),
                        reads=[(tag + "sq", r, th), "ones"], writes=[("ps", b)])
            for th, b in ((0, b0), (1, b1)):
                P.add("dve", lambda e, th=th, b=b: e.tensor_scalar(
                    out=rstd_buf[:, th * 512:(th + 1) * 512], in0=ps[b][:, :], scalar1=1.0 / D, scalar2=EPS,
                    op0=ALU.mult, op1=ALU.add), reads=[("ps", b)], writes=[(tag + "rstd", th)])
            P.add("act", lambda e: e.activation(out=rstd_buf[:, :], in_=rstd_buf[:, :], func=AF.Sqrt),
                  reads=[(tag + "rstd", 0), (tag + "rstd", 1)], writes=[tag + "rstd2"])
            P.add("dve", lambda e: e.reciprocal(out=rstd_buf[:, :], in_=rstd_buf[:, :]),
                  reads=[tag + "rstd2"], writes=[tag + "rstdF"])

        rot_pending = []
        wp_pending = []

        def flush_wp():
            while wp_pending:
                wp_pending.pop(0)()

        def flush_rot():
            while rot_pending:
                rot_pending.pop(0)()

        rotc = [0]

        def rotary_tile(b, dest_ap, dest_res, th):
            r = rotc[0] % 2
            rotc[0] += 1
            P.add("act", lambda e, b=b, r=r: e.activation(out=qb[r][:, :], in_=ps[b][:, :], func=AF.Copy),
                  reads=[("ps", b)], writes=[("qb", r)])

            def rest(b=b, r=r, dest_ap=dest_ap, dest_res=dest_res, th=th):
                b2 = nextbank()
                P.add("pe", lambda e: e.matmul(ps[b2][:, :], lhsT=Rm_b[:, :], rhs=qb[r][:, :], start=True, stop=True),
                      reads=[("qb", r), "rm"], writes=[("ps", b2)])
                P.add("dve", lambda e: e.tensor_tensor(out=t1[r][:, :], in0=ps[b][:, :], in1=rot[:, 0, th * 512:(th + 1) * 512], op=ALU.mult),
                      reads=[("ps", b), "rot"], writes=[("t1", r)])
                P.add("dve", lambda e: e.tensor_tensor(out=t2[r][:, :], in0=ps[b2][:, :], in1=rot[:, 1, th * 512:(th + 1) * 512], op=ALU.mult),
                      reads=[("ps", b2), "rot"], writes=[("t2", r)])
                P.add("dve", lambda e: e.tensor_tensor(out=dest_ap, in0=t1[r][:, :], in1=t2[r][:, :], op=ALU.add),
                      reads=[("t1", r), ("t2", r)], writes=[dest_res])
            rot_pending.append(rest)

        def proj_fm_matmuls(s, th, b):
            for dc in range(NDC):
                P.add("pe", lambda e, dc=dc: e.matmul(ps[b][:, :], lhsT=wr_a[s][:, dc, :], rhs=hT[:, dc, th * 512:(th + 1) * 512],
                                                       start=(dc == 0), stop=(dc == NDC - 1)),
                      reads=[("w", s), "hT"], writes=[("ps", b)])

        for ipass, (xsrc, tokoff, tboff) in enumerate(((xT_prev, 0, 0), (xT_own, 1024, 8))):
            own = ipass == 1
            P.add("sp", lambda e, ipass=ipass: e.dma_start(out=rot[:, :, :], in_=rot_d[:, ipass, :, :]), writes=["rot"], dma="rot")

            def get_src(dc, th, xsrc=xsrc):
                r = dc % 2
                P.add("sp", lambda e, dc=dc, r=r, th=th: e.dma_start(
                    out=xs[r][:, th * 512:(th + 1) * 512], in_=xsrc[:, dc, th * 512:(th + 1) * 512]),
                    writes=[("xs", r, th)], dma=("xs", r, th))
                P.add("dve", lambda e, dc=dc, r=r, th=th: e.tensor_scalar(
                    out=hT[:, dc, th * 512:(th + 1) * 512], in0=xs[r][:, th * 512:(th + 1) * 512],
                    scalar1=cs(C_G1 + dc), scalar2=None, op0=ALU.mult),
                    reads=[("xs", r, th), "cst"], writes=[("hTraw", dc, th)])
                return xs[r][:, th * 512:(th + 1) * 512], [("xs", r, th)]
            P.add("dve", lambda e: e.tensor_copy(out=small[:, 140:141], in_=small[:, 140:141]), reads=[], writes=["hT"])
            norm_stats(get_src, sqb, rstd, "n1")
            for dc in range(NDC):
                P.add("dve", lambda e, dc=dc: e.tensor_tensor(
                    out=hT[:, dc, :], in0=hT[:, dc, :], in1=rstd[:, :], op=ALU.mult),
                    reads=[("hTraw", dc, 0), ("hTraw", dc, 1), "n1rstdF"], writes=["hT"])
            if not own:
                P.add("dve", lambda e: e.tensor_copy(out=hThalo[:, :, :], in_=hT[:, :, 1008:1024]), reads=["hT"], writes=["hThalo"])
                ck('n1prev')
            else:
                P.fence()

            if own:
                for n in range(8):
                    g = n // 2
                    a_ = n % 2
                    s = load_w("a", w_uqk[n, :, :, :])
                    bh, b0, b1 = nextbank(), nextbank(), nextbank()
                    for dc in range(NDC):
                        P.add("pe", lambda e, dc=dc, s=s, bh=bh: e.matmul(ps[bh][:, 0:16], lhsT=wr_a[s][:, dc, :], rhs=hThalo[:, dc, :],
                                                                          start=(dc == 0), stop=(dc == NDC - 1)),
                              reads=[("w", s), "hThalo"], writes=[("ps", bh)])
                    proj_fm_matmuls(s, 0, b0)
                    proj_fm_matmuls(s, 1, b1)
                    flush_wp()
                    P.add("act", lambda e, bh=bh: e.activation(out=ubuf[:, 0:16], in_=ps[bh][:, 0:16], func=AF.Copy),
                          reads=[("ps", bh)], writes=["ubuf"])
                    P.add("act", lambda e, b0=b0: e.activation(out=ubuf[:, 16:528], in_=ps[b0][:, :], func=AF.Copy),
                          reads=[("ps", b0), "ubuf"], writes=["ubuf"])
                    P.add("act", lambda e, b1=b1: e.activation(out=ubuf[:, 528:1040], in_=ps[b1][:, :], func=AF.Copy),
                          reads=[("ps", b1), "ubuf"], writes=["ubuf"])
                    m = g + 1
                    src = ubuf
                    src_res = "ubuf"
                    for st in range(1, m + 1):
                        sh = 2 ** (st - 1)
                        S0 = 2 ** st - 1
                        dst = pa[(st - 1) % 2]
                        dres = ("pa", (st - 1) % 2)
                        P.add("dve", lambda e, src=src, dst=dst, sh=sh, S0=S0: e.tensor_tensor(
                            out=dst[:, S0:1040], in0=src[:, S0:1040], in1=src[:, S0 - sh:1040 - sh], op=ALU.add),
                            reads=[src_res], writes=[dres])
                        src, src_res = dst, dres
                    w_ = 2 ** m
                    P.add("dve", lambda e, src=src, a_=a_, w_=w_: e.scalar_tensor_tensor(
                        out=pooled[a_][:, :], in0=src[:, 16:1040], scalar=1.0 / w_, in1=ubuf[:, 16:1040],
                        op0=ALU.mult, op1=ALU.subtract), reads=[src_res, "ubuf"], writes=[("pooled", a_)])
                    P.add("dve", lambda e, src=src, g=g: e.tensor_tensor(
                        out=tmp16[:, :], in0=src[:, 16:32], in1=cs(C_IC + 16 * g, 16), op=ALU.mult),
                        reads=[src_res, "cst"], writes=["tmp16"])
                    P.add("dve", lambda e, a_=a_: e.tensor_tensor(
                        out=pooled[a_][:, 0:16], in0=tmp16[:, :], in1=ubuf[:, 16:32], op=ALU.subtract),
                        reads=["tmp16", "ubuf", ("pooled", a_)], writes=[("pooled", a_)])
                    if a_ == 1:
                        def wpool_job(g=g):
                            for o in range(2):
                                for th in range(2):
                                    b = nextbank()
                                    for a2 in range(2):
                                        P.add("pe", lambda e, g=g, a2=a2, o=o, th=th, b=b: e.matmul(
                                            ps[b][:, :], lhsT=wp[:, g, a2, o * 128:(o + 1) * 128],
                                            rhs=pooled[a2][:, th * 512:(th + 1) * 512], start=(a2 == 0), stop=(a2 == 1)),
                                            reads=["wp", ("pooled", a2)], writes=[("ps", b)])
                                    P.add("act", lambda e, g=g, o=o, th=th, b=b: e.activation(
                                        out=mixP[:, 2 * g + o, th * 512:(th + 1) * 512], in_=ps[b][:, :], func=AF.Identity,
                                        scale=cs(C_PS + 2 * g + o)), reads=[("ps", b), "cst"], writes=[("mixP", 2 * g + o, th)])
                        wp_pending.append(wpool_job)
                flush_wp()

            kinds = (["q"] if own else []) + ["k"]
            for kind in kinds:
                for hc in range(8):
                    n = (8 if kind == "q" else 16) + hc
                    s = load_w("a", w_uqk[n, :, :, :])
                    ck('pk_load')
                    for th in range(2):
                        b = nextbank()
                        proj_fm_matmuls(s, th, b)
                        ck('pk_mm')
                        flush_rot()
                        if th == 1:
                            ck('pk_rot')
                        if kind == "q":
                            dest = QT[:, hc, th * 512:(th + 1) * 512]
                            dres = ("QT", hc, th)
                        else:
                            dest = KT[:, hc, tokoff + th * 512: tokoff + (th + 1) * 512]
                            dres = ("KT", hc, ipass, th)
                        rotary_tile(b, dest, dres, th)
            flush_rot()

            if not own:
                P.add("dve", lambda e: e.memset(Vaug[:, :, :, 128:130], 1.0), writes=["Vones"])
            for vc in range(4):
                s = load_w("v", w_v[vc, :, :, :])
                for tbl in range(8):
                    b = nextbank()
                    for dc in range(NDC):
                        P.add("pe", lambda e, dc=dc, s=s, tbl=tbl, b=b: e.matmul(
                            ps[b][:, 0:256], lhsT=hT[:, dc, tbl * 128:(tbl + 1) * 128], rhs=wr_v[s][:, dc, :],
                            start=(dc == 0), stop=(dc == NDC - 1)), reads=[("w", s), "hT"], writes=[("ps", b)])
                    for hh in range(2):
                        P.add("act", lambda e, b=b, hh=hh, vc=vc, tbl=tbl, tboff=tboff: e.activation(
                            out=Vaug[:, tboff + tbl, 2 * vc + hh, 0:128], in_=ps[b][:, hh * 128:(hh + 1) * 128], func=AF.Copy),
                            reads=[("ps", b)], writes=[("V", tboff + tbl, 2 * vc + hh)])
            if not own:
                ck('projprev')

        ck('projown')
        P.fence()

        for r in range(2):
            for c in range(2):
                P.add("dve", lambda e, r=r, c=c: e.memset(PTd[r][c][:, :], 0.0), writes=[("PTd", r)])
        ptc = [0]
        NPT = len(PT)
        bank[0] = 0

        def emit_qk(g):
            h, i, kbs, bS = g["h"], g["i"], g["kbs"], g["bS"]
            for idx, kb in enumerate(kbs):
                for c in range(2):
                    P.add("pe", lambda e, c=c, idx=idx, kb=kb, h=h, i=i, bS=bS: e.matmul(
                        ps[bS[c]][:, idx * 128:(idx + 1) * 128],
                        lhsT=KT[c * 64:(c + 1) * 64, h, kb * 128:(kb + 1) * 128],
                        rhs=QT[c * 64:(c + 1) * 64, h, i * 128:(i + 1) * 128], start=True, stop=True),
                        reads=[("QT", h, i // 4), ("KT", h, kb // 8, (kb % 8) // 4)], writes=[("ps", bS[c])])

        def emit_exp(g):
            bS, kbs, u2 = g["bS"], g["kbs"], g["u2"]
            assert bS[0] % 2 == 0 and bS[1] == bS[0] + 1 and bS[0] >= 2
            pair = pspair[(bS[0] - 2) // 2]
            W = len(kbs) * 128
            if g["diag"]:
                P.add("act", lambda e: e.activation(
                    out=PTd2[u2][0:64, :, 0:128], in_=pair[0:64, :, 0:128], func=AF.Exp, scale=0.125),
                    reads=[("ps", bS[0]), ("ps", bS[1])], writes=[("PTd", u2)])
                P.add("act", lambda e: e.activation(
                    out=PTd2[u2][64:128, :, 64:128], in_=pair[64:128, :, 64:128], func=AF.Exp, scale=0.125),
                    reads=[("ps", bS[0]), ("ps", bS[1]), ("PTd", u2)], writes=[("PTd", u2)])
                return
            k2 = (ptc[0] // 2) % (NPT // 2)
            prs = [2 * k2, 2 * k2 + 1]
            ptc[0] += 2
            if g["prev"]:
                P.add("act", lambda e: e.activation(
                    out=PT2[k2][:, :, 0:W], in_=pair[:, :, 0:W], func=AF.Exp, bias=cs(C_PB), scale=0.125),
                    reads=[("ps", bS[0]), ("ps", bS[1]), "cst"], writes=[("PT", prs[0]), ("PT", prs[1])])
            else:
                P.add("act", lambda e: e.activation(
                    out=PT2[k2][:, :, 0:W], in_=pair[:, :, 0:W], func=AF.Exp, scale=0.125),
                    reads=[("ps", bS[0]), ("ps", bS[1])], writes=[("PT", prs[0]), ("PT", prs[1])])
            g["prs"] = prs

        def emit_av(g):
            h, kbs, bO, u2 = g["h"], g["kbs"], g["bO"], g["u2"]
            if g["diag"]:
                kb = kbs[0]
                for c in range(2):
                    P.add("pe", lambda e, c=c, kb=kb, h=h, u2=u2, bO=bO: e.matmul(
                        ps[bO[c]][:, c * 256:c * 256 + 129], lhsT=PTd[u2][c][:, :], rhs=Vaug[:, kb, h, 0:129],
                        start=False, stop=True, skip_group_check=True),
                        reads=[("PTd", u2), ("V", kb, h), "Vones"], writes=[("ps", bO[c])])
                return
            prs = g["prs"]
            for idx, kb in enumerate(kbs):
                for c in range(2):
                    st = g["first"] and idx == 0 and c == 0
                    P.add("pe", lambda e, c=c, idx=idx, kb=kb, h=h, pr=prs[c], st=st, bO=bO: e.matmul(
                        ps[bO[c]][:, c * 256:c * 256 + 129], lhsT=PT[pr][:, idx * 128:(idx + 1) * 128], rhs=Vaug[:, kb, h, 0:129],
                        start=st, stop=False, skip_group_check=True),
                        reads=[("PT", prs[c]), ("V", kb, h), "Vones"], writes=[("ps", bO[c])])

        def epilogue_stages(h, i, u2, bO):
            E = esm[u2]
            er = ("esm", u2)

            def stA():
                P.add("dve", lambda e: e.reciprocal(out=E[:, 0:1], in_=ps[bO[0]][:, 128:129]),
                      reads=[("ps", bO[0])], writes=[er])
                P.add("dve", lambda e: e.reciprocal(out=E[:, 1:2], in_=ps[bO[1]][:, 384:385]),
                      reads=[("ps", bO[1]), er], writes=[er])
                P.add("dve", lambda e: e.tensor_tensor(out=E[:, 2:3], in0=E[:, 1:2], in1=neglam, op=ALU.mult),
                      reads=[er, "neglam"], writes=[er])
                P.add("dve", lambda e: e.tensor_scalar(
                    out=o_a[u2][:, :], in0=ps[bO[0]][:, 0:128], scalar1=E[:, 0:1], scalar2=None, op0=ALU.mult),
                    reads=[("ps", bO[0]), er], writes=[("oa", u2)])
                P.add("dve", lambda e: e.scalar_tensor_tensor(
                    out=o_b[u2][:, :], in0=ps[bO[1]][:, 256:384], scalar=E[:, 2:3], in1=o_a[u2][:, :], op0=ALU.mult, op1=ALU.add),
                    reads=[("ps", bO[1]), er, ("oa", u2)], writes=[("ob", u2)])
                P.add("dve", lambda e: e.scalar_tensor_tensor(
                    out=o_a[u2][:, :], in0=o_b[u2][:, :], scalar=1.0, in1=o_b[u2][:, :], op0=ALU.mult, op1=ALU.mult,
                    accum_out=E[:, 3:4]), reads=[("ob", u2), ("oa", u2)], writes=[("oa", u2), er])

            def stB():
                P.add("act", lambda e: e.activation(out=E[:, 4:5], in_=E[:, 3:4], func=AF.Ln, scale=1.0 / 128, bias=EPS),
                      reads=[er], writes=[er])
                P.add("act", lambda e: e.activation(out=E[:, 5:6], in_=E[:, 4:5], func=AF.Exp, scale=-0.5),
                      reads=[er], writes=[er])
                P.add("dve", lambda e: e.scalar_tensor_tensor(
                    out=o_c[u2][:, :], in0=o_b[u2][:, :], scalar=E[:, 5:6], in1=cs(C_GREP, 128), op0=ALU.mult, op1=ALU.mult),
                    reads=[("ob", u2), er, "cst"], writes=[("oc", u2)])

            def stC():
                bt = bO[0]
                P.add("pe", lambda e: e.transpose(out=psb[bt][:, 0:128], in_=o_c[u2][:, :], identity=ident_b[:, :]),
                      reads=[("oc", u2), "ident"], writes=[("ps", bt)])
                P.add("act", lambda e: e.activation(
                    out=mixA[:, h, i * 128:(i + 1) * 128], in_=psb[bt][:, 0:128], func=AF.Copy),
                    reads=[("ps", bt)], writes=[("mixA", h, i // 4)])
            return [stA, stB, stC]

        glist = []
        unit = 0
        for h in range(8):
            for i in range(8):
                u2 = unit % 2
                bO = (u2, u2)
                gs = [([0, 1, 2, 3], True), ([4, 5, 6, 7], True)]
                ownk = list(range(8, 8 + i))
                while ownk:
                    gs.append((ownk[:4], False))
                    ownk = ownk[4:]
                ug = []
                for gi, (kbs, isprev) in enumerate(gs):
                    ug.append(dict(h=h, i=i, u2=u2, bO=bO, kbs=kbs, prev=isprev, diag=False, first=(gi == 0), last=False, gi=gi))
                ug.append(dict(h=h, i=i, u2=u2, bO=bO, kbs=[8 + i], prev=False, diag=True, first=False, last=True, gi=len(gs)))
                glist.extend(ug)
                unit += 1

        pend = {}

        def start_group(g):
            g["bS"] = (nextbank(2, 8), nextbank(2, 8))
            emit_qk(g)

        start_group(glist[0])
        start_group(glist[1])
        for n, g in enumerate(glist):
            if n + 2 < len(glist):
                start_group(glist[n + 2])
            emit_exp(g)
            emit_av(g)
            for stg in pend.pop(g["gi"], []):
                stg()
            if g["last"]:
                for k_ in sorted(pend):
                    for stg in pend[k_]:
                        stg()
                pend.clear()
                sA, sB, sC = epilogue_stages(g["h"], g["i"], g["u2"], g["bO"])
                sA()
                pend = {0: [sB], 1: [sC]}
        for k_ in sorted(pend):
            for stg in pend[k_]:
                stg()

        ck('att')
        P.fence()

        for c in range(16):
            s = load_w("a", w_o[c, :, :, :])
            r = c % 2
            for th in range(2):
                P.add("sp", lambda e, c=c, r=r, th=th: e.dma_start(
                    out=xs[r][:, th * 512:(th + 1) * 512], in_=xT_own[:, c, th * 512:(th + 1) * 512]),
                    writes=[("xs", r, th)], dma=("xs", r, th))
            for th in range(2):
                b = nextbank()
                for fc in range(16):
                    rhs = (mixP if fc < 8 else mixA)
                    P.add("pe", lambda e, fc=fc, s=s, th=th, b=b, rhs=rhs: e.matmul(
                        ps[b][:, :], lhsT=wr_a[s][:, fc, :], rhs=rhs[:, fc % 8, th * 512:(th + 1) * 512],
                        start=(fc == 0), stop=(fc == 15)),
                        reads=[("w", s), (("mixP", fc, th) if fc < 8 else ("mixA", fc - 8, th))], writes=[("ps", b)])
                P.add("dve", lambda e, c=c, th=th, b=b, r=r: e.tensor_tensor(
                    out=X1[:, c, th * 512:(th + 1) * 512], in0=ps[b][:, :], in1=xs[r][:, th * 512:(th + 1) * 512], op=ALU.add),
                    reads=[("ps", b), ("xs", r, th)], writes=[("X1", c)])

        ck('wout')
        P.fence()

        def get_x1(dc, th):
            return X1[:, dc, th * 512:(th + 1) * 512], [("X1", dc)]
        norm_stats(get_x1, sqbC, rstdC, "n2")
        for dc in range(NDC):
            P.add("dve", lambda e, dc=dc: e.scalar_tensor_tensor(
                out=h2T[:, dc, :], in0=X1[:, dc, :], scalar=cs(C_G2 + dc), in1=rstdC[:, :], op0=ALU.mult, op1=ALU.mult),
                reads=[("X1", dc), "n2rstdF", "cst"], writes=["h2T"])

        ck('n2')
        sgc = [0]
        for G in range(NG):
            hb = hTg[G % 2]
            for jj in range(JG):
                j = G * JG + jj
                s = load_w("gu", w_gu[j, :, :, :, :])
                bks = {}
                for gu in range(2):
                    for th in range(2):
                        b = nextbank()
                        bks[(gu, th)] = b
                        for dc in range(NDC):
                            P.add("pe", lambda e, dc=dc, s=s, gu=gu, th=th, b=b: e.matmul(
                                ps[b][:, :], lhsT=wr_gu[s][:, gu, dc, :], rhs=h2T[:, dc, th * 512:(th + 1) * 512],
                                start=(dc == 0), stop=(dc == NDC - 1)),
                                reads=[("w", s), "h2T"], writes=[("ps", b)])
                for th in range(2):
                    r = sgc[0] % 2
                    sgc[0] += 1
                    bg, bu = bks[(0, th)], bks[(1, th)]
                    P.add("act", lambda e, r=r, bg=bg: e.activation(out=sg[r][:, :], in_=ps[bg][:, :], func=AF.Silu),
                          reads=[("ps", bg)], writes=[("sg", r)])
                    P.add("dve", lambda e, r=r, bu=bu, jj=jj, th=th, hb=hb: e.tensor_tensor(
                        out=hb[:, jj, th * 512:(th + 1) * 512], in0=sg[r][:, :], in1=ps[bu][:, :], op=ALU.mult),
                        reads=[("sg", r), ("ps", bu)], writes=[("hTg", G % 2, th)])
            for cp in range(8):
                s = load_w("dn", w_dn[G, cp, :, :, :])
                for cc in range(2):
                    c = cp * 2 + cc
                    for th in range(2):
                        b = nextbank()
                        for jj in range(JG):
                            P.add("pe", lambda e, jj=jj, s=s, cc=cc, th=th, b=b, hb=hb: e.matmul(
                                ps[b][:, :], lhsT=wr_dn[s][:, jj, cc * 128:(cc + 1) * 128], rhs=hb[:, jj, th * 512:(th + 1) * 512],
                                start=(jj == 0), stop=(jj == JG - 1)),
                                reads=[("w", s), ("hTg", G % 2, th)], writes=[("ps", b)])
                        P.add("dve", lambda e, c=c, th=th, b=b: e.tensor_tensor(
                            out=X1[:, c, th * 512:(th + 1) * 512], in0=X1[:, c, th * 512:(th + 1) * 512], in1=ps[b][:, :], op=ALU.add),
                            reads=[("ps", b), ("X1", c)], writes=[("X1", c)])

        ck('ffn')
        norm_stats(get_x1, sqbC, rstdC, "n3")
        for dc in range(NDC):
            P.add("dve", lambda e, dc=dc: e.scalar_tensor_tensor(
                out=X1[:, dc, :], in0=X1[:, dc, :], scalar=cs(C_G3 + dc), in1=rstdC[:, :], op0=ALU.mult, op1=ALU.mult),
                reads=[("X1", dc), "n3rstdF", "cst"], writes=[("X1", dc)])
            P.add("sp", lambda e, dc=dc: e.dma_start(out=outT[:, dc, :], in_=X1[:, dc, :]),
                  reads=[("X1", dc)], writes=[("out", dc)], dma="out")
        P.add("sp", None, reads=[("out", dc) for dc in range(NDC)])

    try:
        body()
    except _Stop:
        dump(stop)

    with contextlib.ExitStack() as stack:
        P.finalize(nc, stack)
        with nc.Block() as block:
            @block.tensor
            def _(e):
                P.emit("pe", e)

            @block.vector
            def _(e):
                P.emit("dve", e)

            @block.scalar
            def _(e):
                P.emit("act", e)

            @block.gpsimd
            def _(e):
                P.emit("pool", e)

            @block.sync
            def _(e):
                P.emit("sp", e)
    return nc


def _fm(a):
    t = a.shape[0]
    return np.ascontiguousarray(a.T.reshape(NDC, 128, t).transpose(1, 0, 2))


def prepare_inputs(x, norm_mix, w_in, w_pool, pool_scale, lambda_q1, lambda_k1, lambda_q2, lambda_k2,
                   subln_gain, w_out, norm_ffn, w_gate_up, w_down, norm_final):
    f = np.float32
    x = np.asarray(x, f)
    w_in = np.asarray(w_in, f)[0]
    w_pool = np.asarray(w_pool, f)[0]
    w_out = np.asarray(w_out, f)[0]
    w_gate_up = np.asarray(w_gate_up, f)[0]
    w_down = np.asarray(w_down, f)[0]

    def colchunks(w, ncol):
        C = w.shape[1]
        return np.ascontiguousarray(w.reshape(NDC, 128, C // ncol, ncol).transpose(2, 1, 0, 3))

    w_uqk = colchunks(w_in[:, :3072], 128)
    w_v = colchunks(w_in[:, 3072:], 256)
    w_pool_r = np.ascontiguousarray(w_pool.reshape(4, 2, 128, 256).transpose(2, 0, 1, 3))
    w_o = colchunks(w_out, 128)
    w_gu = np.ascontiguousarray(w_gate_up.reshape(NDC, 128, 2, NJ, 128).transpose(3, 1, 2, 0, 4))
    w_dn = np.ascontiguousarray(w_down.reshape(NG, JG, 128, 8, 256).transpose(0, 3, 2, 1, 4))

    inv_freq = (np.float32(10000.0) ** (-(np.arange(0, 64, 2, dtype=f) / f(64)))).astype(f)
    p_idx = np.arange(128)
    fr = inv_freq[p_idx % 32]
    sign = np.where((p_idx % 64) < 32, -1.0, 1.0).astype(f)
    partner = np.where((p_idx % 64) < 32, p_idx + 32, p_idx - 32)
    Rm = np.zeros((128, 128), f)
    Rm[partner, p_idx] = 1.0

    base = np.zeros((128, C_N), f)
    base[:, C_G1:C_G1 + 16] = np.asarray(norm_mix, f)[0].reshape(16, 128).T
    base[:, C_G2:C_G2 + 16] = np.asarray(norm_ffn, f)[0].reshape(16, 128).T
    base[:, C_G3:C_G3 + 16] = np.asarray(norm_final, f).reshape(16, 128).T
    base[:, C_PS:C_PS + 8] = np.asarray(pool_scale, f)[0].reshape(8, 128).T
    base[:, C_LQ1:C_LQ1 + 64] = np.asarray(lambda_q1, f)[0][None, :]
    base[:, C_LK1:C_LK1 + 64] = np.asarray(lambda_k1, f)[0][None, :]
    base[:, C_LQ2:C_LQ2 + 64] = np.asarray(lambda_q2, f)[0][None, :]
    base[:, C_LK2:C_LK2 + 64] = np.asarray(lambda_k2, f)[0][None, :]
    base[:, C_GREP:C_GREP + 128] = np.asarray(subln_gain, f)[0][None, :]
    base[:, C_ONES:C_ONES + 128] = 1.0
    base[:, C_ID:C_ID + 128] = np.eye(128, dtype=f)
    base[:, C_RM:C_RM + 128] = Rm

    shared = dict(w_uqk=w_uqk, w_v=w_v, w_pool=w_pool_r, w_o=w_o, w_gu=w_gu, w_dn=w_dn)
    in_maps = []
    for core in range(8):
        b, half = core // 2, core % 2
        own0 = half * S_OWN
        xo = _fm(x[b, own0:own0 + S_OWN])
        if half == 1:
            xp = _fm(x[b, 0:S_OWN])
        else:
            xp = np.zeros_like(xo)
        rot = np.zeros((128, 2, 2, 1024), f)
        for ip in range(2):
            pos = (own0 - S_OWN + ip * S_OWN + np.arange(S_OWN)).astype(f)
            ang = (pos[None, :] * fr[:, None]).astype(f)
            rot[:, ip, 0, :] = np.cos(ang)
            rot[:, ip, 1, :] = np.sin(ang) * sign[:, None]
        c = base.copy()
        c[:, C_PB] = 0.0 if half == 1 else NEG_BIG
        for g, w in enumerate((2, 4, 8, 16)):
            posn = own0 + np.arange(16)
            c[:, C_IC + 16 * g:C_IC + 16 * (g + 1)] = (1.0 / np.minimum(posn + 1, w).astype(f))[None, :]
        m = dict(shared)
        m.update(xT_own=xo, xT_prev=xp, rot=rot, cst=c)
        in_maps.append(m)
    return in_maps


_NC_CACHE = {}


def kernel(x, norm_mix, w_in, w_pool, pool_scale, lambda_q1, lambda_k1, lambda_q2, lambda_k2,
           subln_gain, w_out, norm_ffn, w_gate_up, w_down, norm_final):
    in_maps = prepare_inputs(x, norm_mix, w_in, w_pool, pool_scale, lambda_q1, lambda_k1, lambda_q2, lambda_k2,
                             subln_gain, w_out, norm_ffn, w_gate_up, w_down, norm_final)
    if "nc" not in _NC_CACHE:
        _NC_CACHE["nc"] = build_program()
    nc = _NC_CACHE["nc"]
    res = run_bass_kernel_spmd(nc, in_maps, core_ids=list(range(8)))
    out = np.empty((4, 2048, D), np.float32)
    for core in range(8):
        b, half = core // 2, core % 2
        o = np.asarray(res.results[core]["outT"], np.float32)
        out[b, half * S_OWN:(half + 1) * S_OWN, :] = o.transpose(2, 1, 0).reshape(S_OWN, D)
    return out
```

```python
import contextlib
import math

import numpy as np
import concourse.bass as bass
import concourse.mybir as mybir
from concourse.bass_utils import run_bass_kernel_spmd

F32 = mybir.dt.float32
BF16 = mybir.dt.bfloat16
AF = mybir.ActivationFunctionType
ALU = mybir.AluOpType
AX = mybir.AxisListType

D = 2048
NDC = 16
S_OWN = 1024
FFN = 5632
NJ = 44
NG = 4
JG = 11
EPS = 1e-6
LAM_INIT = 0.8 - 0.6 * math.exp(0.0)
NEG_BIG = -30000.0

C_G1, C_G2, C_G3 = 0, 16, 32
C_PS = 48
C_PB = 56
C_IC = 57
C_LQ1, C_LK1, C_LQ2, C_LK2 = 121, 185, 249, 313
C_GREP = 377
C_ONES = 505
C_ID = 633
C_RM = 761
C_N = 896

NS = 4


class Op:
    __slots__ = ("id", "eng", "idx", "fn", "deps", "dma", "signal", "cum", "ordinal", "ndma")


class Prog:
    ENGS = ("pe", "act", "dve", "pool", "sp")

    def __init__(self):
        self.ops = []
        self.eng_ops = {e: [] for e in self.ENGS}
        self.last_w = {}
        self.readers = {}
        self.pending_fence = {}
        self.last_dma = {}
        self.aggregate = set()

    def add(self, eng, fn, reads=(), writes=(), dma=None, ndma=1):
        op = Op()
        op.ndma = ndma
        op.id = len(self.ops)
        op.eng = eng
        op.fn = fn
        op.dma = dma
        op.signal = False
        writes = list(writes) + [r for r in reads if isinstance(r, tuple) and r[0] == "ps" and r not in writes]
        deps = set()
        for r in reads:
            lw = self.last_w.get(r)
            if lw is not None:
                deps.add(lw)
        for w in writes:
            lw = self.last_w.get(w)
            if lw is not None:
                deps.add(lw)
            rs = self.readers.get(w)
            if rs:
                deps.update(rs)
        for r in reads:
            self.readers.setdefault(r, []).append(op.id)
        for w in writes:
            self.last_w[w] = op.id
            self.readers[w] = []
        pf = self.pending_fence.pop(eng, None)
        if pf:
            deps.update(pf)
        deps.discard(op.id)
        if eng == "pe" and dma is None:
            deps = {d for d in deps if not (self.ops[d].eng == "pe" and self.ops[d].dma is None)}
        op.deps = deps
        op.idx = len(self.eng_ops[eng])
        self.eng_ops[eng].append(op)
        self.ops.append(op)
        if dma is not None:
            self.last_dma[dma] = op.id
        return op.id

    def fence(self):
        deps = set()
        for e in self.ENGS:
            lst = self.eng_ops[e]
            for o in reversed(lst):
                if o.dma is None and o.fn is not None:
                    deps.add(o.id)
                    break
        deps.update(self.last_dma.values())
        for e in self.ENGS:
            self.pending_fence.setdefault(e, set()).update(deps)

    def finalize(self, nc, stack):
        for op in self.ops:
            for d in op.deps:
                t = self.ops[d]
                if t.dma is None:
                    t.signal = True
        self.eng_sem = {}
        for e in self.ENGS:
            self.eng_sem[e] = stack.enter_context(nc.semaphore("s_" + e))
            c = 0
            for o in self.eng_ops[e]:
                if o.dma is None and o.signal:
                    c += 1
                o.cum = c
        self.dma_sem = {}
        counts = {}
        for op in self.ops:
            if op.dma is not None:
                k = op.dma
                if k not in self.dma_sem:
                    nm = "d_" + "_".join(str(x) for x in (k if isinstance(k, tuple) else (k,)))
                    self.dma_sem[k] = stack.enter_context(nc.semaphore(nm))
                counts[k] = counts.get(k, 0) + op.ndma
                op.ordinal = counts[k]
        self.dma_total = counts

    def wait_of(self, d):
        t = self.ops[d]
        if t.dma is None:
            return (self.eng_sem[t.eng], t.cum)
        k = t.dma
        n = self.dma_total[k] if k in self.aggregate else t.ordinal
        return (self.dma_sem[k], 16 * n)

    def emit(self, eng, e):
        known = {}
        for op in self.eng_ops[eng]:
            ws = {}
            for d in op.deps:
                sem, val = self.wait_of(d)
                key = id(sem)
                if ws.get(key, (None, 0))[1] < val:
                    ws[key] = (sem, val)
            for key, (sem, val) in ws.items():
                if known.get(key, 0) < val:
                    e.wait_ge(sem, val)
                    known[key] = val
            if op.fn is None:
                continue
            ins = op.fn(e)
            if op.dma is not None:
                for one in (ins if isinstance(ins, list) else [ins]):
                    one.then_inc(self.dma_sem[op.dma], 16)
            elif op.signal:
                ins.then_inc(self.eng_sem[eng], 1)


class _Stop(Exception):
    pass


def build_program(stop=None):
    nc = bass.Bass("TRN2", target_bir_lowering=False)
    dt = nc.dram_tensor
    xT_own = dt("xT_own", [128, NDC, S_OWN], F32, kind="ExternalInput")
    xT_prev = dt("xT_prev", [128, NDC, S_OWN], F32, kind="ExternalInput")
    rot_d = dt("rot", [128, 2, 2, 1024], F32, kind="ExternalInput")
    cst_d = dt("cst", [128, C_N], F32, kind="ExternalInput")
    w_uqk = dt("w_uqk", [24, 128, 16, 128], F32, kind="ExternalInput")
    w_v = dt("w_v", [4, 128, 16, 256], F32, kind="ExternalInput")
    w_pool = dt("w_pool", [128, 4, 2, 256], F32, kind="ExternalInput")
    w_o = dt("w_o", [16, 128, 16, 128], F32, kind="ExternalInput")
    early = stop in ("setup", "pk_load", "pk_mm", "pk_rot", "n1prev", "projprev", "projown", "att", "wout", "n2")
    w_gu = None if early else dt("w_gu", [NJ, 128, 2, 16, 128], F32, kind="ExternalInput")
    w_dn = None if early else dt("w_dn", [NG, 8, 128, JG, 256], F32, kind="ExternalInput")
    outT = dt("outT", [128, NDC, S_OWN], F32, kind="ExternalOutput")

    BASE = 16640
    cnt = [0]

    def sb(name, shape, dtype, off):
        cnt[0] += 1
        return nc.alloc_sbuf_tensor_at(f"{name}_{cnt[0]}", list(shape), dtype, offset=BASE + off)

    cst = sb("cst", [128, C_N], F32, 0)
    ones_b = sb("ones", [128, 128], BF16, 3584)
    ident_b = sb("ident", [128, 128], BF16, 3840)
    Rm_b = sb("rm", [128, 128], BF16, 4096)
    small = sb("small", [128, 256], F32, 4352)
    wp = sb("wp", [128, 4, 2, 256], BF16, 5376)
    W0 = 9472
    wr_a = [sb("wra", [128, 16, 128], BF16, W0 + s * 8192) for s in range(NS)]
    wr_v = [sb("wrv", [128, 16, 256], BF16, W0 + s * 8192) for s in range(NS)]
    wr_gu = [sb("wrgu", [128, 2, 16, 128], BF16, W0 + s * 8192) for s in range(NS)]
    wr_dn = [sb("wrdn", [128, JG, 256], BF16, W0 + s * 8192) for s in range(NS)]
    RH = 42240
    hT = sb("hT", [128, 16, 1024], BF16, RH)
    mixA = sb("mixA", [128, 8, 1024], BF16, RH)
    ATS = RH + 16384
    PT = [sb("PT", [128, 512], BF16, ATS + r * 1024) for r in range(8)]
    PTd = [[sb("PTd", [128, 128], BF16, ATS + 8192 + (r * 2 + c) * 256) for c in range(2)] for r in range(2)]
    PT2 = [sb("PT2", [128, 2, 512], BF16, ATS + k * 2048) for k in range(4)]
    PTd2 = [sb("PTd2", [128, 2, 128], BF16, ATS + 8192 + r * 512) for r in range(2)]
    o_a = [sb("oa", [128, 128], F32, ATS + 9216 + r * 512) for r in range(2)]
    o_b = [sb("ob", [128, 128], F32, ATS + 10240 + r * 512) for r in range(2)]
    o_c = [sb("oc", [128, 128], BF16, ATS + 11264 + r * 256) for r in range(2)]
    esm = [sb("esm", [128, 16], F32, ATS + 11776 + r * 64) for r in range(2)]
    h2T = sb("h2T", [128, 16, 1024], BF16, RH)
    RK = 75008
    KT = sb("KT", [128, 8, 2048], BF16, RK)
    QT = sb("QT", [128, 8, 1024], BF16, RK + 32768)
    Vaug = sb("Vaug", [128, 16, 8, 130], BF16, RK + 49152)
    X1 = sb("X1", [128, 16, 1024], F32, RK)
    CS = RK + 65536
    sqbC = [sb("sqbC", [128, 1024], BF16, CS + r * 2048) for r in range(2)]
    rstdC = sb("rstdC", [128, 1024], F32, CS + 4096)
    sg = [sb("sg", [128, 512], F32, CS + 8192 + r * 2048) for r in range(2)]
    MP = 157440
    mixP = sb("mixP", [128, 8, 1024], BF16, MP)
    hTg = [sb("hTg", [128, JG, 1024], BF16, MP + r * 22528) for r in range(2)]
    RT = 173824
    rot = sb("rot", [128, 2, 1024], F32, RT)
    hThalo = sb("hThalo", [128, 16, 16], BF16, 182016)
    SCR = 182528
    xs = [sb("xs", [128, 1024], F32, SCR + r * 4096) for r in range(2)]
    sqb = [sb("sqb", [128, 1024], BF16, SCR + 8192 + r * 2048) for r in range(2)]
    rstd = sb("rstd", [128, 1024], F32, SCR + 12288)
    ubuf = sb("ubuf", [128, 1040], F32, SCR)
    pa = [sb("pa", [128, 1040], F32, SCR + 4160 * (r + 1)) for r in range(2)]
    pooled = [sb("pooled", [128, 1024], BF16, SCR + 12480 + r * 2048) for r in range(2)]
    tmp16 = sb("tmp16", [128, 16], F32, SCR + 16576)
    RS = 199168
    qb = [sb("qb", [128, 512], BF16, RS + r * 1024) for r in range(2)]
    t1 = [sb("t1", [128, 512], F32, RS + 2048 + r * 2048) for r in range(2)]
    t2 = [sb("t2", [128, 512], F32, RS + 6144 + r * 2048) for r in range(2)]
    assert BASE + RS + 10240 <= 229344

    class _BankView:
        def __init__(self, t, c):
            self.t, self.c = t, c

        def __getitem__(self, idx):
            return self.t[idx[0], self.c, idx[1]]

    ps01 = [nc.alloc_psum_tensor(f"ps{b}", [128, 512], F32) for b in range(2)]
    pspair = [nc.alloc_psum_tensor(f"pp{k}", [128, 2, 512], F32) for k in range(3)]
    ps = list(ps01) + [_BankView(pspair[k], c) for k in range(3) for c in range(2)]
    psb = [p.bitcast(BF16) for p in ps01]

    P = Prog()
    P.aggregate.update(["const", "out"])
    bank = [0]

    def nextbank(lo=0, hi=8):
        b = lo + (bank[0] % (hi - lo))
        bank[0] += 1
        return b

    wslot = [0]

    def load_w(kind, src):
        s = wslot[0] % NS
        wslot[0] += 1
        if kind == "a":
            P.add("pool", lambda e, s=s, src=src: e.dma_start(out=wr_a[s][:, :, :], in_=src, max_dma_last_dim=4096),
                  writes=[("w", s)], dma=("w", s))
        elif kind == "v":
            P.add("pool", lambda e, s=s, src=src: e.dma_start(out=wr_v[s][:, :, :], in_=src, max_dma_last_dim=4096),
                  writes=[("w", s)], dma=("w", s))
        elif kind == "gu":
            P.add("pool", lambda e, s=s, src=src: [
                e.dma_start(out=wr_gu[s][:, 0, :, :], in_=src[:, 0, :, :], max_dma_last_dim=4096),
                e.dma_start(out=wr_gu[s][:, 1, :, :], in_=src[:, 1, :, :], max_dma_last_dim=4096)],
                writes=[("w", s)], dma=("w", s), ndma=2)
        elif kind == "dn":
            P.add("pool", lambda e, s=s, src=src: e.dma_start(out=wr_dn[s][:, :, :], in_=src, max_dma_last_dim=4096),
                  writes=[("w", s)], dma=("w", s))
        return s

    def cs(c0, n=1):
        return cst[:, c0:c0 + n]

    def ck(name):
        if stop == name:
            raise _Stop()

    def dump(name):
        srcs = {
            "setup": [(small[:, :], outT[:, 0, 0:256])],
            "pk_load": [(wr_a[0][:, dc, :], outT[:, dc, 0:128]) for dc in range(16)],
            "pk_mm": [(hT[:, 0, :], outT[:, 0, :])],
            "pk_rot": [(KT[:, 0, 0:1024], outT[:, 0, :])],
            "n1prev": [(rstd[:, :], outT[:, 0, :])] + [(hT[:, dc, :], outT[:, 1 + dc % 15, :]) for dc in (0, 5)],
            "projprev": [(KT[:, h, 0:1024], outT[:, h, :]) for h in range(8)] +
                        [(Vaug[:, tb, :, 0:128], outT[:, 8 + tb, :]) for tb in range(8)],
            "projown": [(KT[:, 0, 1024:2048], outT[:, 0, :]), (KT[:, 7, 0:1024], outT[:, 3, :])] +
                       [(QT[:, h, :], outT[:, 1 + h, :]) for h in (0, 1)] +
                       [(mixP[:, c, :], outT[:, 4 + c, :]) for c in range(8)] +
                       [(Vaug[:, tb, :, 0:128], outT[:, 12 + i_, :]) for i_, tb in enumerate((8, 9, 3, 15))],
            "att": [(mixA[:, h, :], outT[:, h, :]) for h in range(8)] + [(mixP[:, c, :], outT[:, 8 + c, :]) for c in range(8)],
            "wout": [(X1[:, c, :], outT[:, c, :]) for c in range(16)],
            "n2": [(X1[:, c, :], outT[:, c, :]) for c in range(8)] + [(h2T[:, c, :], outT[:, 8 + c, :]) for c in range(8)],
            "ffn": [(X1[:, c, :], outT[:, c, :]) for c in range(16)],
        }[name]
        P.fence()
        for i, (src, dst) in enumerate(srcs):
            P.add("pool", lambda e, src=src, dst=dst: e.dma_start(out=dst, in_=src, max_dma_last_dim=2048), writes=[("out", i)], dma="out")
        P.add("sp", None, reads=[("out", i) for i in range(len(srcs))])

    def body():
        P.add("sp", lambda e: e.dma_start(out=cst[:, :], in_=cst_d[:, :]), writes=["cst"], dma="const")
        P.add("pool", lambda e: e.dma_start(out=wp[:, :, :, :], in_=w_pool[:, :, :, :], max_dma_last_dim=4096),
              writes=["wp"], dma="wp")
        P.add("dve", lambda e: e.tensor_copy(out=ones_b[:, :], in_=cs(C_ONES, 128)), reads=["cst"], writes=["ones"])
        P.add("dve", lambda e: e.tensor_copy(out=ident_b[:, :], in_=cs(C_ID, 128)), reads=["cst"], writes=["ident"])
        P.add("dve", lambda e: e.tensor_copy(out=Rm_b[:, :], in_=cs(C_RM, 128)), reads=["cst"], writes=["rm"])
        P.add("dve", lambda e: e.tensor_tensor(out=small[:, 0:64], in0=cs(C_LQ1, 64), in1=cs(C_LK1, 64), op=ALU.mult),
              reads=["cst"], writes=["sm0"])
        P.add("dve", lambda e: e.reduce_sum(out=small[:, 128:129], in_=small[:, 0:64], axis=AX.X), reads=["sm0"], writes=["sm1"])
        P.add("dve", lambda e: e.tensor_tensor(out=small[:, 64:128], in0=cs(C_LQ2, 64), in1=cs(C_LK2, 64), op=ALU.mult),
              reads=["cst"], writes=["sm2"])
        P.add("dve", lambda e: e.reduce_sum(out=small[:, 129:130], in_=small[:, 64:128], axis=AX.X), reads=["sm2"], writes=["sm3"])
        P.add("act", lambda e: e.activation(out=small[:, 130:132], in_=small[:, 128:130], func=AF.Exp),
              reads=["sm1", "sm3"], writes=["sm4"])
        P.add("dve", lambda e: e.tensor_tensor(out=small[:, 132:133], in0=small[:, 131:132], in1=small[:, 130:131], op=ALU.subtract),
              reads=["sm4"], writes=["sm5"])
        P.add("dve", lambda e: e.tensor_scalar_add(out=small[:, 133:134], in0=small[:, 132:133], scalar1=-LAM_INIT),
              reads=["sm5"], writes=["neglam"])
        neglam = small[:, 133:134]
        P.add("dve", lambda e: e.tensor_scalar_mul(out=cst[:, C_GREP:C_GREP + 128], in0=cst[:, C_GREP:C_GREP + 128],
                                                   scalar1=(1.0 - LAM_INIT)), reads=["cst"], writes=["cst"])

        ck('setup')
        def norm_stats(get_src, sq_bufs, rstd_buf, tag):
            b0, b1 = nextbank(), nextbank()
            for dc in range(NDC):
                r = dc % 2
                for th, b in ((0, b0), (1, b1)):
                    ap, res = get_src(dc, th)
                    P.add("act", lambda e, ap=ap, r=r, th=th: e.activation(
                        out=sq_bufs[r][:, th * 512:(th + 1) * 512], in_=ap, func=AF.Square),
                        reads=res, writes=[(tag + "sq", r, th)])
                    P.add("pe", lambda e, r=r, th=th, b=b, dc=dc: e.matmul(
                        ps[b][:, :], lhsT=ones_b[:, :], rhs=sq_bufs[r][:, th * 512:(th + 1) * 512],
                        start=(dc == 0), stop=(dc == NDC - 1)),
                        reads=[(tag + "sq", r, th), "ones"], writes=[("ps", b)])
            for th, b in ((0, b0), (1, b1)):
                P.add("dve", lambda e, th=th, b=b: e.tensor_scalar(
                    out=rstd_buf[:, th * 512:(th + 1) * 512], in0=ps[b][:, :], scalar1=1.0 / D, scalar2=EPS,
                    op0=ALU.mult, op1=ALU.add), reads=[("ps", b)], writes=[(tag + "rstd", th)])
            P.add("act", lambda e: e.activation(out=rstd_buf[:, :], in_=rstd_buf[:, :], func=AF.Sqrt),
                  reads=[(tag + "rstd", 0), (tag + "rstd", 1)], writes=[tag + "rstd2"])
            P.add("dve", lambda e: e.reciprocal(out=rstd_buf[:, :], in_=rstd_buf[:, :]),
                  reads=[tag + "rstd2"], writes=[tag + "rstdF"])

        rot_pending = []
        wp_pending = []

        def flush_wp():
            while wp_pending:
                wp_pending.pop(0)()

        def flush_rot():
            while rot_pending:
                rot_pending.pop(0)()

        rotc = [0]

        def rotary_tile(b, dest_ap, dest_res, th):
            r = rotc[0] % 2
            rotc[0] += 1
            P.add("act", lambda e, b=b, r=r: e.activation(out=qb[r][:, :], in_=ps[b][:, :], func=AF.Copy),
                  reads=[("ps", b)], writes=[("qb", r)])

            def rest(b=b, r=r, dest_ap=dest_ap, dest_res=dest_res, th=th):
                b2 = nextbank()
                P.add("pe", lambda e: e.matmul(ps[b2][:, :], lhsT=Rm_b[:, :], rhs=qb[r][:, :], start=True, stop=True),
                      reads=[("qb", r), "rm"], writes=[("ps", b2)])
                P.add("dve", lambda e: e.tensor_tensor(out=t1[r][:, :], in0=ps[b][:, :], in1=rot[:, 0, th * 512:(th + 1) * 512], op=ALU.mult),
                      reads=[("ps", b), "rot"], writes=[("t1", r)])
                P.add("dve", lambda e: e.tensor_tensor(out=t2[r][:, :], in0=ps[b2][:, :], in1=rot[:, 1, th * 512:(th + 1) * 512], op=ALU.mult),
                      reads=[("ps", b2), "rot"], writes=[("t2", r)])
                P.add("dve", lambda e: e.tensor_tensor(out=dest_ap, in0=t1[r][:, :], in1=t2[r][:, :], op=ALU.add),
                      reads=[("t1", r), ("t2", r)], writes=[dest_res])
            rot_pending.append(rest)

        def proj_fm_matmuls(s, th, b):
            for dc in range(NDC):
                P.add("pe", lambda e, dc=dc: e.matmul(ps[b][:, :], lhsT=wr_a[s][:, dc, :], rhs=hT[:, dc, th * 512:(th + 1) * 512],
                                                       start=(dc == 0), stop=(dc == NDC - 1)),
                      reads=[("w", s), "hT"], writes=[("ps", b)])

        for ipass, (xsrc, tokoff, tboff) in enumerate(((xT_prev, 0, 0), (xT_own, 1024, 8))):
            own = ipass == 1
            P.add("sp", lambda e, ipass=ipass: e.dma_start(out=rot[:, :, :], in_=rot_d[:, ipass, :, :]), writes=["rot"], dma="rot")

            def get_src(dc, th, xsrc=xsrc):
                r = dc % 2
                P.add("sp", lambda e, dc=dc, r=r, th=th: e.dma_start(
                    out=xs[r][:, th * 512:(th + 1) * 512], in_=xsrc[:, dc, th * 512:(th + 1) * 512]),
                    writes=[("xs", r, th)], dma=("xs", r, th))
                P.add("dve", lambda e, dc=dc, r=r, th=th: e.tensor_scalar(
                    out=hT[:, dc, th * 512:(th + 1) * 512], in0=xs[r][:, th * 512:(th + 1) * 512],
                    scalar1=cs(C_G1 + dc), scalar2=None, op0=ALU.mult),
                    reads=[("xs", r, th), "cst"], writes=[("hTraw", dc, th)])
                return xs[r][:, th * 512:(th + 1) * 512], [("xs", r, th)]
            P.add("dve", lambda e: e.tensor_copy(out=small[:, 140:141], in_=small[:, 140:141]), reads=[], writes=["hT"])
            norm_stats(get_src, sqb, rstd, "n1")
            for dc in range(NDC):
                P.add("dve", lambda e, dc=dc: e.tensor_tensor(
                    out=hT[:, dc, :], in0=hT[:, dc, :], in1=rstd[:, :], op=ALU.mult),
                    reads=[("hTraw", dc, 0), ("hTraw", dc, 1), "n1rstdF"], writes=["hT"])
            if not own:
                P.add("dve", lambda e: e.tensor_copy(out=hThalo[:, :, :], in_=hT[:, :, 1008:1024]), reads=["hT"], writes=["hThalo"])
                ck('n1prev')
            else:
                P.fence()

            if own:
                for n in range(8):
                    g = n // 2
                    a_ = n % 2
                    s = load_w("a", w_uqk[n, :, :, :])
                    bh, b0, b1 = nextbank(), nextbank(), nextbank()
                    for dc in range(NDC):
                        P.add("pe", lambda e, dc=dc, s=s, bh=bh: e.matmul(ps[bh][:, 0:16], lhsT=wr_a[s][:, dc, :], rhs=hThalo[:, dc, :],
                                                                          start=(dc == 0), stop=(dc == NDC - 1)),
                              reads=[("w", s), "hThalo"], writes=[("ps", bh)])
                    proj_fm_matmuls(s, 0, b0)
                    proj_fm_matmuls(s, 1, b1)
                    flush_wp()
                    P.add("act", lambda e, bh=bh: e.activation(out=ubuf[:, 0:16], in_=ps[bh][:, 0:16], func=AF.Copy),
                          reads=[("ps", bh)], writes=["ubuf"])
                    P.add("act", lambda e, b0=b0: e.activation(out=ubuf[:, 16:528], in_=ps[b0][:, :], func=AF.Copy),
                          reads=[("ps", b0), "ubuf"], writes=["ubuf"])
                    P.add("act", lambda e, b1=b1: e.activation(out=ubuf[:, 528:1040], in_=ps[b1][:, :], func=AF.Copy),
                          reads=[("ps", b1), "ubuf"], writes=["ubuf"])
                    m = g + 1
                    src = ubuf
                    src_res = "ubuf"
                    for st in range(1, m + 1):
                        sh = 2 ** (st - 1)
                        S0 = 2 ** st - 1
                        dst = pa[(st - 1) % 2]
                        dres = ("pa", (st - 1) % 2)
                        P.add("dve", lambda e, src=src, dst=dst, sh=sh, S0=S0: e.tensor_tensor(
                            out=dst[:, S0:1040], in0=src[:, S0:1040], in1=src[:, S0 - sh:1040 - sh], op=ALU.add),
                            reads=[src_res], writes=[dres])
                        src, src_res = dst, dres
                    w_ = 2 ** m
                    P.add("dve", lambda e, src=src, a_=a_, w_=w_: e.scalar_tensor_tensor(
                        out=pooled[a_][:, :], in0=src[:, 16:1040], scalar=1.0 / w_, in1=ubuf[:, 16:1040],
                        op0=ALU.mult, op1=ALU.subtract), reads=[src_res, "ubuf"], writes=[("pooled", a_)])
                    P.add("dve", lambda e, src=src, g=g: e.tensor_tensor(
                        out=tmp16[:, :], in0=src[:, 16:32], in1=cs(C_IC + 16 * g, 16), op=ALU.mult),
                        reads=[src_res, "cst"], writes=["tmp16"])
                    P.add("dve", lambda e, a_=a_: e.tensor_tensor(
                        out=pooled[a_][:, 0:16], in0=tmp16[:, :], in1=ubuf[:, 16:32], op=ALU.subtract),
                        reads=["tmp16", "ubuf", ("pooled", a_)], writes=[("pooled", a_)])
                    if a_ == 1:
                        def wpool_job(g=g):
                            for o in range(2):
                                for th in range(2):
                                    b = nextbank()
                                    for a2 in range(2):
                                        P.add("pe", lambda e, g=g, a2=a2, o=o, th=th, b=b: e.matmul(
                                            ps[b][:, :], lhsT=wp[:, g, a2, o * 128:(o + 1) * 128],
                                            rhs=pooled[a2][:, th * 512:(th + 1) * 512], start=(a2 == 0), stop=(a2 == 1)),
                                            reads=["wp", ("pooled", a2)], writes=[("ps", b)])
                                    P.add("act", lambda e, g=g, o=o, th=th, b=b: e.activation(
                                        out=mixP[:, 2 * g + o, th * 512:(th + 1) * 512], in_=ps[b][:, :], func=AF.Identity,
                                        scale=cs(C_PS + 2 * g + o)), reads=[("ps", b), "cst"], writes=[("mixP", 2 * g + o, th)])
                        wp_pending.append(wpool_job)
                flush_wp()

            kinds = (["q"] if own else []) + ["k"]
            for kind in kinds:
                for hc in range(8):
                    n = (8 if kind == "q" else 16) + hc
                    s = load_w("a", w_uqk[n, :, :, :])
                    ck('pk_load')
                    for th in range(2):
                        b = nextbank()
                        proj_fm_matmuls(s, th, b)
                        ck('pk_mm')
                        flush_rot()
                        if th == 1:
                            ck('pk_rot')
                        if kind == "q":
                            dest = QT[:, hc, th * 512:(th + 1) * 512]
                            dres = ("QT", hc, th)
                        else:
                            dest = KT[:, hc, tokoff + th * 512: tokoff + (th + 1) * 512]
                            dres = ("KT", hc, ipass, th)
                        rotary_tile(b, dest, dres, th)
            flush_rot()

            if not own:
                P.add("dve", lambda e: e.memset(Vaug[:, :, :, 128:130], 1.0), writes=["Vones"])
            for vc in range(4):
                s = load_w("v", w_v[vc, :, :, :])
                for tbl in range(8):
                    b = nextbank()
                    for dc in range(NDC):
                        P.add("pe", lambda e, dc=dc, s=s, tbl=tbl, b=b: e.matmul(
                            ps[b][:, 0:256], lhsT=hT[:, dc, tbl * 128:(tbl + 1) * 128], rhs=wr_v[s][:, dc, :],
                            start=(dc == 0), stop=(dc == NDC - 1)), reads=[("w", s), "hT"], writes=[("ps", b)])
                    for hh in range(2):
                        P.add("act", lambda e, b=b, hh=hh, vc=vc, tbl=tbl, tboff=tboff: e.activation(
                            out=Vaug[:, tboff + tbl, 2 * vc + hh, 0:128], in_=ps[b][:, hh * 128:(hh + 1) * 128], func=AF.Copy),
                            reads=[("ps", b)], writes=[("V", tboff + tbl, 2 * vc + hh)])
            if not own:
                ck('projprev')

        ck('projown')
        P.fence()

        for r in range(2):
            for c in range(2):
                P.add("dve", lambda e, r=r, c=c: e.memset(PTd[r][c][:, :], 0.0), writes=[("PTd", r)])
        ptc = [0]
        NPT = len(PT)
        bank[0] = 0

        def emit_qk(g):
            h, i, kbs, bS = g["h"], g["i"], g["kbs"], g["bS"]
            for idx, kb in enumerate(kbs):
                for c in range(2):
                    P.add("pe", lambda e, c=c, idx=idx, kb=kb, h=h, i=i, bS=bS: e.matmul(
                        ps[bS[c]][:, idx * 128:(idx + 1) * 128],
                        lhsT=KT[c * 64:(c + 1) * 64, h, kb * 128:(kb + 1) * 128],
                        rhs=QT[c * 64:(c + 1) * 64, h, i * 128:(i + 1) * 128], start=True, stop=True),
                        reads=[("QT", h, i // 4), ("KT", h, kb // 8, (kb % 8) // 4)], writes=[("ps", bS[c])])

        def emit_exp(g):
            bS, kbs, u2 = g["bS"], g["kbs"], g["u2"]
            assert bS[0] % 2 == 0 and bS[1] == bS[0] + 1 and bS[0] >= 2
            pair = pspair[(bS[0] - 2) // 2]
            W = len(kbs) * 128
            if g["diag"]:
                P.add("act", lambda e: e.activation(
                    out=PTd2[u2][0:64, :, 0:128], in_=pair[0:64, :, 0:128], func=AF.Exp, scale=0.125),
                    reads=[("ps", bS[0]), ("ps", bS[1])], writes=[("PTd", u2)])
                P.add("act", lambda e: e.activation(
                    out=PTd2[u2][64:128, :, 64:128], in_=pair[64:128, :, 64:128], func=AF.Exp, scale=0.125),
                    reads=[("ps", bS[0]), ("ps", bS[1]), ("PTd", u2)], writes=[("PTd", u2)])
                return
            k2 = (ptc[0] // 2) % (NPT // 2)
            prs = [2 * k2, 2 * k2 + 1]
            ptc[0] += 2
            if g["prev"]:
                P.add("act", lambda e: e.activation(
                    out=PT2[k2][:, :, 0:W], in_=pair[:, :, 0:W], func=AF.Exp, bias=cs(C_PB), scale=0.125),
                    reads=[("ps", bS[0]), ("ps", bS[1]), "cst"], writes=[("PT", prs[0]), ("PT", prs[1])])
            else:
                P.add("act", lambda e: e.activation(
                    out=PT2[k2][:, :, 0:W], in_=pair[:, :, 0:W], func=AF.Exp, scale=0.125),
                    reads=[("ps", bS[0]), ("ps", bS[1])], writes=[("PT", prs[0]), ("PT", prs[1])])
            g["prs"] = prs

        def emit_av(g):
            h, kbs, bO, u2 = g["h"], g["kbs"], g["bO"], g["u2"]
            if g["diag"]:
                kb = kbs[0]
                for c in range(2):
                    P.add("pe", lambda e, c=c, kb=kb, h=h, u2=u2, bO=bO: e.matmul(
                        ps[bO[c]][:, c * 256:c * 256 + 129], lhsT=PTd[u2][c][:, :], rhs=Vaug[:, kb, h, 0:129],
                        start=False, stop=True, skip_group_check=True),
                        reads=[("PTd", u2), ("V", kb, h), "Vones"], writes=[("ps", bO[c])])
                return
            prs = g["prs"]
            for idx, kb in enumerate(kbs):
                for c in range(2):
                    st = g["first"] and idx == 0 and c == 0
                    P.add("pe", lambda e, c=c, idx=idx, kb=kb, h=h, pr=prs[c], st=st, bO=bO: e.matmul(
                        ps[bO[c]][:, c * 256:c * 256 + 129], lhsT=PT[pr][:, idx * 128:(idx + 1) * 128], rhs=Vaug[:, kb, h, 0:129],
                        start=st, stop=False, skip_group_check=True),
                        reads=[("PT", prs[c]), ("V", kb, h), "Vones"], writes=[("ps", bO[c])])

        def epilogue_stages(h, i, u2, bO):
            E = esm[u2]
            er = ("esm", u2)

            def stA():
                P.add("dve", lambda e: e.reciprocal(out=E[:, 0:1], in_=ps[bO[0]][:, 128:129]),
                      reads=[("ps", bO[0])], writes=[er])
                P.add("dve", lambda e: e.reciprocal(out=E[:, 1:2], in_=ps[bO[1]][:, 384:385]),
                      reads=[("ps", bO[1]), er], writes=[er])
                P.add("dve", lambda e: e.tensor_tensor(out=E[:, 2:3], in0=E[:, 1:2], in1=neglam, op=ALU.mult),
                      reads=[er, "neglam"], writes=[er])
                P.add("dve", lambda e: e.tensor_scalar(
                    out=o_a[u2][:, :], in0=ps[bO[0]][:, 0:128], scalar1=E[:, 0:1], scalar2=None, op0=ALU.mult),
                    reads=[("ps", bO[0]), er], writes=[("oa", u2)])
                P.add("dve", lambda e: e.scalar_tensor_tensor(
                    out=o_b[u2][:, :], in0=ps[bO[1]][:, 256:384], scalar=E[:, 2:3], in1=o_a[u2][:, :], op0=ALU.mult, op1=ALU.add),
                    reads=[("ps", bO[1]), er, ("oa", u2)], writes=[("ob", u2)])
                P.add("dve", lambda e: e.scalar_tensor_tensor(
                    out=o_a[u2][:, :], in0=o_b[u2][:, :], scalar=1.0, in1=o_b[u2][:, :], op0=ALU.mult, op1=ALU.mult,
                    accum_out=E[:, 3:4]), reads=[("ob", u2), ("oa", u2)], writes=[("oa", u2), er])

            def stB():
                P.add("act", lambda e: e.activation(out=E[:, 4:5], in_=E[:, 3:4], func=AF.Ln, scale=1.0 / 128, bias=EPS),
                      reads=[er], writes=[er])
                P.add("act", lambda e: e.activation(out=E[:, 5:6], in_=E[:, 4:5], func=AF.Exp, scale=-0.5),
                      reads=[er], writes=[er])
                P.add("dve", lambda e: e.scalar_tensor_tensor(
                    out=o_c[u2][:, :], in0=o_b[u2][:, :], scalar=E[:, 5:6], in1=cs(C_GREP, 128), op0=ALU.mult, op1=ALU.mult),
                    reads=[("ob", u2), er, "cst"], writes=[("oc", u2)])

            def stC():
                bt = bO[0]
                P.add("pe", lambda e: e.transpose(out=psb[bt][:, 0:128], in_=o_c[u2][:, :], identity=ident_b[:, :]),
                      reads=[("oc", u2), "ident"], writes=[("ps", bt)])
                P.add("act", lambda e: e.activation(
                    out=mixA[:, h, i * 128:(i + 1) * 128], in_=psb[bt][:, 0:128], func=AF.Copy),
                    reads=[("ps", bt)], writes=[("mixA", h, i // 4)])
            return [stA, stB, stC]

        glist = []
        unit = 0
        for h in range(8):
            for i in range(8):
                u2 = unit % 2
                bO = (u2, u2)
                gs = [([0, 1, 2, 3], True), ([4, 5, 6, 7], True)]
                ownk = list(range(8, 8 + i))
                while ownk:
                    gs.append((ownk[:4], False))
                    ownk = ownk[4:]
                ug = []
                for gi, (kbs, isprev) in enumerate(gs):
                    ug.append(dict(h=h, i=i, u2=u2, bO=bO, kbs=kbs, prev=isprev, diag=False, first=(gi == 0), last=False, gi=gi))
                ug.append(dict(h=h, i=i, u2=u2, bO=bO, kbs=[8 + i], prev=False, diag=True, first=False, last=True, gi=len(gs)))
                glist.extend(ug)
                unit += 1

        pend = {}

        def start_group(g):
            g["bS"] = (nextbank(2, 8), nextbank(2, 8))
            emit_qk(g)

        NGR = len(glist)
        for n in range(min(3, NGR)):
            start_group(glist[n])
        nxt = min(3, NGR)
        for k2 in range(0, NGR, 2):
            pair_g = glist[k2:k2 + 2]
            for g in pair_g:
                emit_exp(g)
            for g in pair_g:
                emit_av(g)
                for stg in pend.pop(g["gi"], []):
                    stg()
                if g["last"]:
                    for k_ in sorted(pend):
                        for stg in pend[k_]:
                            stg()
                    pend.clear()
                    sA, sB, sC = epilogue_stages(g["h"], g["i"], g["u2"], g["bO"])
                    sA()
                    pend = {0: [sB], 1: [sC]}
            for _ in range(2):
                if nxt < NGR:
                    start_group(glist[nxt])
                    nxt += 1
        for k_ in sorted(pend):
            for stg in pend[k_]:
                stg()

        ck('att')
        P.fence()

        for c in range(16):
            s = load_w("a", w_o[c, :, :, :])
            r = c % 2
            for th in range(2):
                P.add("sp", lambda e, c=c, r=r, th=th: e.dma_start(
                    out=xs[r][:, th * 512:(th + 1) * 512], in_=xT_own[:, c, th * 512:(th + 1) * 512]),
                    writes=[("xs", r, th)], dma=("xs", r, th))
            for th in range(2):
                b = nextbank()
                for fc in range(16):
                    rhs = (mixP if fc < 8 else mixA)
                    P.add("pe", lambda e, fc=fc, s=s, th=th, b=b, rhs=rhs: e.matmul(
                        ps[b][:, :], lhsT=wr_a[s][:, fc, :], rhs=rhs[:, fc % 8, th * 512:(th + 1) * 512],
                        start=(fc == 0), stop=(fc == 15)),
                        reads=[("w", s), (("mixP", fc, th) if fc < 8 else ("mixA", fc - 8, th))], writes=[("ps", b)])
                P.add("dve", lambda e, c=c, th=th, b=b, r=r: e.tensor_tensor(
                    out=X1[:, c, th * 512:(th + 1) * 512], in0=ps[b][:, :], in1=xs[r][:, th * 512:(th + 1) * 512], op=ALU.add),
                    reads=[("ps", b), ("xs", r, th)], writes=[("X1", c)])

        ck('wout')
        P.fence()

        def get_x1(dc, th):
            return X1[:, dc, th * 512:(th + 1) * 512], [("X1", dc)]
        norm_stats(get_x1, sqbC, rstdC, "n2")
        for dc in range(NDC):
            P.add("dve", lambda e, dc=dc: e.scalar_tensor_tensor(
                out=h2T[:, dc, :], in0=X1[:, dc, :], scalar=cs(C_G2 + dc), in1=rstdC[:, :], op0=ALU.mult, op1=ALU.mult),
                reads=[("X1", dc), "n2rstdF", "cst"], writes=["h2T"])

        ck('n2')
        sgc = [0]
        for G in range(NG):
            hb = hTg[G % 2]
            for jj in range(JG):
                j = G * JG + jj
                s = load_w("gu", w_gu[j, :, :, :, :])
                bks = {}
                for gu in range(2):
                    for th in range(2):
                        b = nextbank()
                        bks[(gu, th)] = b
                        for dc in range(NDC):
                            P.add("pe", lambda e, dc=dc, s=s, gu=gu, th=th, b=b: e.matmul(
                                ps[b][:, :], lhsT=wr_gu[s][:, gu, dc, :], rhs=h2T[:, dc, th * 512:(th + 1) * 512],
                                start=(dc == 0), stop=(dc == NDC - 1)),
                                reads=[("w", s), "h2T"], writes=[("ps", b)])
                for th in range(2):
                    r = sgc[0] % 2
                    sgc[0] += 1
                    bg, bu = bks[(0, th)], bks[(1, th)]
                    P.add("act", lambda e, r=r, bg=bg: e.activation(out=sg[r][:, :], in_=ps[bg][:, :], func=AF.Silu),
                          reads=[("ps", bg)], writes=[("sg", r)])
                    P.add("dve", lambda e, r=r, bu=bu, jj=jj, th=th, hb=hb: e.tensor_tensor(
                        out=hb[:, jj, th * 512:(th + 1) * 512], in0=sg[r][:, :], in1=ps[bu][:, :], op=ALU.mult),
                        reads=[("sg", r), ("ps", bu)], writes=[("hTg", G % 2, th)])
            for cp in range(8):
                s = load_w("dn", w_dn[G, cp, :, :, :])
                for cc in range(2):
                    c = cp * 2 + cc
                    for th in range(2):
                        b = nextbank()
                        for jj in range(JG):
                            P.add("pe", lambda e, jj=jj, s=s, cc=cc, th=th, b=b, hb=hb: e.matmul(
                                ps[b][:, :], lhsT=wr_dn[s][:, jj, cc * 128:(cc + 1) * 128], rhs=hb[:, jj, th * 512:(th + 1) * 512],
                                start=(jj == 0), stop=(jj == JG - 1)),
                                reads=[("w", s), ("hTg", G % 2, th)], writes=[("ps", b)])
                        P.add("dve", lambda e, c=c, th=th, b=b: e.tensor_tensor(
                            out=X1[:, c, th * 512:(th + 1) * 512], in0=X1[:, c, th * 512:(th + 1) * 512], in1=ps[b][:, :], op=ALU.add),
                            reads=[("ps", b), ("X1", c)], writes=[("X1", c)])

        ck('ffn')
        norm_stats(get_x1, sqbC, rstdC, "n3")
        for dc in range(NDC):
            P.add("dve", lambda e, dc=dc: e.scalar_tensor_tensor(
                out=X1[:, dc, :], in0=X1[:, dc, :], scalar=cs(C_G3 + dc), in1=rstdC[:, :], op0=ALU.mult, op1=ALU.mult),
                reads=[("X1", dc), "n3rstdF", "cst"], writes=[("X1", dc)])
            P.add("sp", lambda e, dc=dc: e.dma_start(out=outT[:, dc, :], in_=X1[:, dc, :]),
                  reads=[("X1", dc)], writes=[("out", dc)], dma="out")
        P.add("sp", None, reads=[("out", dc) for dc in range(NDC)])

    try:
        body()
    except _Stop:
        dump(stop)

    with contextlib.ExitStack() as stack:
        P.finalize(nc, stack)
        with nc.Block() as block:
            @block.tensor
            def _(e):
                P.emit("pe", e)

            @block.vector
            def _(e):
                P.emit("dve", e)

            @block.scalar
            def _(e):
                P.emit("act", e)

            @block.gpsimd
            def _(e):
                P.emit("pool", e)

            @block.sync
            def _(e):
                P.emit("sp", e)
    return nc


def _fm(a):
    t = a.shape[0]
    return np.ascontiguousarray(a.T.reshape(NDC, 128, t).transpose(1, 0, 2))


def prepare_inputs(x, norm_mix, w_in, w_pool, pool_scale, lambda_q1, lambda_k1, lambda_q2, lambda_k2,
                   subln_gain, w_out, norm_ffn, w_gate_up, w_down, norm_final):
    f = np.float32
    x = np.asarray(x, f)
    w_in = np.asarray(w_in, f)[0]
    w_pool = np.asarray(w_pool, f)[0]
    w_out = np.asarray(w_out, f)[0]
    w_gate_up = np.asarray(w_gate_up, f)[0]
    w_down = np.asarray(w_down, f)[0]

    def colchunks(w, ncol):
        C = w.shape[1]
        return np.ascontiguousarray(w.reshape(NDC, 128, C // ncol, ncol).transpose(2, 1, 0, 3))

    w_uqk = colchunks(w_in[:, :3072], 128)
    w_v = colchunks(w_in[:, 3072:], 256)
    w_pool_r = np.ascontiguousarray(w_pool.reshape(4, 2, 128, 256).transpose(2, 0, 1, 3))
    w_o = colchunks(w_out, 128)
    w_gu = np.ascontiguousarray(w_gate_up.reshape(NDC, 128, 2, NJ, 128).transpose(3, 1, 2, 0, 4))
    w_dn = np.ascontiguousarray(w_down.reshape(NG, JG, 128, 8, 256).transpose(0, 3, 2, 1, 4))

    inv_freq = (np.float32(10000.0) ** (-(np.arange(0, 64, 2, dtype=f) / f(64)))).astype(f)
    p_idx = np.arange(128)
    fr = inv_freq[p_idx % 32]
    sign = np.where((p_idx % 64) < 32, -1.0, 1.0).astype(f)
    partner = np.where((p_idx % 64) < 32, p_idx + 32, p_idx - 32)
    Rm = np.zeros((128, 128), f)
    Rm[partner, p_idx] = 1.0

    base = np.zeros((128, C_N), f)
    base[:, C_G1:C_G1 + 16] = np.asarray(norm_mix, f)[0].reshape(16, 128).T
    base[:, C_G2:C_G2 + 16] = np.asarray(norm_ffn, f)[0].reshape(16, 128).T
    base[:, C_G3:C_G3 + 16] = np.asarray(norm_final, f).reshape(16, 128).T
    base[:, C_PS:C_PS + 8] = np.asarray(pool_scale, f)[0].reshape(8, 128).T
    base[:, C_LQ1:C_LQ1 + 64] = np.asarray(lambda_q1, f)[0][None, :]
    base[:, C_LK1:C_LK1 + 64] = np.asarray(lambda_k1, f)[0][None, :]
    base[:, C_LQ2:C_LQ2 + 64] = np.asarray(lambda_q2, f)[0][None, :]
    base[:, C_LK2:C_LK2 + 64] = np.asarray(lambda_k2, f)[0][None, :]
    base[:, C_GREP:C_GREP + 128] = np.asarray(subln_gain, f)[0][None, :]
    base[:, C_ONES:C_ONES + 128] = 1.0
    base[:, C_ID:C_ID + 128] = np.eye(128, dtype=f)
    base[:, C_RM:C_RM + 128] = Rm

    shared = dict(w_uqk=w_uqk, w_v=w_v, w_pool=w_pool_r, w_o=w_o, w_gu=w_gu, w_dn=w_dn)
    in_maps = []
    for core in range(8):
        b, half = core // 2, core % 2
        own0 = half * S_OWN
        xo = _fm(x[b, own0:own0 + S_OWN])
        if half == 1:
            xp = _fm(x[b, 0:S_OWN])
        else:
            xp = np.zeros_like(xo)
        rot = np.zeros((128, 2, 2, 1024), f)
        for ip in range(2):
            pos = (own0 - S_OWN + ip * S_OWN + np.arange(S_OWN)).astype(f)
            ang = (pos[None, :] * fr[:, None]).astype(f)
            rot[:, ip, 0, :] = np.cos(ang)
            rot[:, ip, 1, :] = np.sin(ang) * sign[:, None]
        c = base.copy()
        c[:, C_PB] = 0.0 if half == 1 else NEG_BIG
        for g, w in enumerate((2, 4, 8, 16)):
            posn = own0 + np.arange(16)
            c[:, C_IC + 16 * g:C_IC + 16 * (g + 1)] = (1.0 / np.minimum(posn + 1, w).astype(f))[None, :]
        m = dict(shared)
        m.update(xT_own=xo, xT_prev=xp, rot=rot, cst=c)
        in_maps.append(m)
    return in_maps


_NC_CACHE = {}


def kernel(x, norm_mix, w_in, w_pool, pool_scale, lambda_q1, lambda_k1, lambda_q2, lambda_k2,
           subln_gain, w_out, norm_ffn, w_gate_up, w_down, norm_final):
    in_maps = prepare_inputs(x, norm_mix, w_in, w_pool, pool_scale, lambda_q1, lambda_k1, lambda_q2, lambda_k2,
                             subln_gain, w_out, norm_ffn, w_gate_up, w_down, norm_final)
    if "nc" not in _NC_CACHE:
        _NC_CACHE["nc"] = build_program()
    nc = _NC_CACHE["nc"]
    res = run_bass_kernel_spmd(nc, in_maps, core_ids=list(range(8)))
    out = np.empty((4, 2048, D), np.float32)
    for core in range(8):
        b, half = core // 2, core % 2
        o = np.asarray(res.results[core]["outT"], np.float32)
        out[b, half * S_OWN:(half + 1) * S_OWN, :] = o.transpose(2, 1, 0).reshape(S_OWN, D)
    return out
```
